# Optimizing a Trainium2 kernel written in Bass

```python
import jax
import jax.numpy as jnp
from jax import lax
import numpy as np

D_MODEL = 1024
BATCH = 2
SEQ = 8192
DEPTH = 2

GRID_W = 64
CTX_LEN = 256
NORM_EPS = 1e-6
N_BRANCH = 3
BRANCH_W = D_MODEL // 2
NA_HEAD_DIM = 64
NA_HEADS = BRANCH_W // NA_HEAD_DIM
NA_WIN_ROWS = 8
NA_WIN_COLS = 16
GLA_HEADS = 4
GLA_DV = BRANCH_W // GLA_HEADS
GLA_DK = GLA_DV // 2
GLA_GATE_RANK = 16
GLA_GATE_TAU = 16.0
RET_HEADS = 4
RET_DV = BRANCH_W // RET_HEADS
RET_DK = RET_DV // 2
RET_DECAY_FWD = 5.0
RET_DECAY_BWD = 5.5
CHUNK = 64
ROPE_BASE = 10000.0
IN_SPLITS = (
    NA_HEADS * NA_HEAD_DIM, NA_HEADS * NA_HEAD_DIM, NA_HEADS * NA_HEAD_DIM,
    GLA_HEADS * GLA_DK, GLA_HEADS * GLA_DK, GLA_HEADS * GLA_DV, 2 * GLA_GATE_RANK,
    RET_HEADS * RET_DK, RET_HEADS * RET_DK, RET_HEADS * RET_DV,
    N_BRANCH * BRANCH_W, N_BRANCH * D_MODEL,
)
N_IN = sum(IN_SPLITS)

kernel_name = 'hybrid_na_gla_retention_prefix_dit'


def rms_norm(x, w):
    xf = x.astype(jnp.float32)
    y = xf * lax.rsqrt(jnp.mean(xf * xf, axis=-1, keepdims=True) + NORM_EPS)
    return (y * w).astype(x.dtype)


def split_heads(t, n_heads):
    b, l, _ = t.shape
    return t.reshape(b, l, n_heads, -1).transpose(0, 2, 1, 3)


def merge_heads(t):
    b, h, l, d = t.shape
    return t.transpose(0, 2, 1, 3).reshape(b, l, h * d)


def head_rms_merge(o, gain):
    b, h, t, dv = o.shape
    return merge_heads(rms_norm(o, gain.reshape(h, 1, dv)))


def split_cols(t):
    idx = np.cumsum(IN_SPLITS)[:-1].tolist()
    return jnp.split(t, idx, axis=-1)


def axial_rope(t):
    n, dh = t.shape[2], t.shape[3]
    half = dh // 2
    quarter = half // 2
    pos = jnp.arange(n)
    row = (pos // GRID_W).astype(jnp.float32)
    col = (pos % GRID_W).astype(jnp.float32)
    inv = ROPE_BASE ** (-jnp.arange(quarter, dtype=jnp.float32) / quarter)

    def rot(u, p):
        ang = p[:, None] * inv[None, :]
        cos, sin = jnp.cos(ang), jnp.sin(ang)
        u1, u2 = u[..., :quarter], u[..., quarter:]
        return jnp.concatenate([u1 * cos - u2 * sin, u1 * sin + u2 * cos], axis=-1)

    return jnp.concatenate([rot(t[..., :half], row), rot(t[..., half:], col)], axis=-1).astype(t.dtype)


def chunk_scan(q, k, v, log_a, s0):
    b, h, t, _ = q.shape
    dv = v.shape[-1]
    n = t // CHUNK

    def to_chunks(u):
        return u.astype(jnp.float32).reshape(b, h, n, CHUNK, u.shape[-1]).transpose(2, 0, 1, 3, 4)

    lower = jnp.tril(jnp.ones((CHUNK, CHUNK), dtype=bool))[:, :, None]

    def step(s, inp):
        qc, kc, vc, lc = inp
        cum = jnp.cumsum(lc, axis=2)
        diff = cum[:, :, :, None, :] - cum[:, :, None, :, :]
        decay = jnp.exp(jnp.where(lower, diff, -jnp.inf))
        att = jnp.sum(qc[:, :, :, None, :] * kc[:, :, None, :, :] * decay, axis=-1)
        o = jnp.einsum('bhts,bhsv->bhtv', att, vc) + jnp.einsum('bhtd,bhdv->bhtv', qc * jnp.exp(cum), s)
        last = cum[:, :, -1:, :]
        s_new = jnp.exp(last[:, :, 0, :, None]) * s + jnp.einsum('bhsd,bhsv->bhdv', kc * jnp.exp(last - cum), vc)
        return s_new, o

    s_fin, o = lax.scan(step, s0, (to_chunks(q), to_chunks(k), to_chunks(v), to_chunks(log_a)))
    o = o.transpose(1, 2, 0, 3, 4).reshape(b, h, t, dv).astype(v.dtype)
    return o, s_fin


def bidir_scan(ctx_in, lat_in, with_ctx):
    qc, kc, vc, lfc, lbc = ctx_in
    ql, kl, vl, lfl, lbl = lat_in
    b, h, _, dk = qc.shape
    s0 = jnp.zeros((b, h, dk, vc.shape[-1]), jnp.float32)
    rev = lambda u: jnp.flip(u, axis=2)
    oc_f, s_f = chunk_scan(qc, kc, vc, lfc, s0)
    oc_b, s_b = chunk_scan(rev(qc), rev(kc), rev(vc), rev(lbc), s0)
    ol_f, _ = chunk_scan(ql, kl, vl, lfl, s_f)
    ol_b, _ = chunk_scan(rev(ql), rev(kl), rev(vl), rev(lbl), s_b)
    o_lat = ol_f + rev(ol_b)
    o_ctx = oc_f + rev(oc_b) if with_ctx else None
    return o_lat, o_ctx


def neighbourhood_attention(q, k, v, kc, vc, rpb):
    b, h, n, dh = q.shape
    rows = n // GRID_W
    kr = min(NA_WIN_ROWS, rows)
    scale = dh ** -0.5
    qg = q.reshape(b, h, rows, GRID_W, dh)
    kg = k.reshape(b, h, rows, GRID_W, dh)
    vg = v.reshape(b, h, rows, GRID_W, dh)
    r_idx = jnp.arange(rows)
    row_start = jnp.clip(r_idx - kr // 2, 0, rows - kr)
    cidx = jnp.arange(GRID_W)
    col_start = jnp.clip(cidx - NA_WIN_COLS // 2, 0, GRID_W - NA_WIN_COLS)
    col_idx = col_start[:, None] + jnp.arange(NA_WIN_COLS)
    col_bias_idx = col_idx - cidx[:, None] + NA_WIN_COLS - 1
    rpb_cols = rpb[:, :, col_bias_idx]
    n_win = kr * NA_WIN_COLS

    def row_block(args):
        r, q_r = args
        rs = row_start[r]
        k_band = lax.dynamic_slice_in_dim(kg, rs, kr, axis=2)
        v_band = lax.dynamic_slice_in_dim(vg, rs, kr, axis=2)
        k_win = k_band[:, :, :, col_idx]
        v_win = v_band[:, :, :, col_idx]
        row_bias_idx = rs + jnp.arange(kr) - r + NA_WIN_ROWS - 1
        bias = jnp.take(rpb_cols, row_bias_idx, axis=1).transpose(0, 2, 1, 3)
        s_win = jnp.einsum('bhwd,bhrwcd->bhwrc', q_r, k_win).astype(jnp.float32) * scale + bias[None]
        s_ctx = jnp.einsum('bhwd,bhnd->bhwn', q_r, kc).astype(jnp.float32) * scale
        s = jnp.concatenate([s_win.reshape(b, h, GRID_W, n_win), s_ctx], axis=-1)
        p = jax.nn.softmax(s, axis=-1).astype(v.dtype)
        p_win = p[..., :n_win].reshape(b, h, GRID_W, kr, NA_WIN_COLS)
        p_ctx = p[..., n_win:]
        return jnp.einsum('bhwrc,bhrwcd->bhwd', p_win, v_win) + jnp.einsum('bhwn,bhnd->bhwd', p_ctx, vc)

    o = lax.map(row_block, (r_idx, qg.transpose(2, 0, 1, 3, 4)))
    return o.transpose(1, 2, 0, 3, 4).reshape(b, h, n, dh)


def context_attention(qc, kc, vc):
    s = jnp.einsum('bhqd,bhkd->bhqk', qc, kc).astype(jnp.float32) * qc.shape[-1] ** -0.5
    p = jax.nn.softmax(s, axis=-1).astype(vc.dtype)
    return jnp.einsum('bhqk,bhkd->bhqd', p, vc)


def na_branch(pl, pc, q_norm, k_norm, rpb, with_ctx):
    def prep(q, k, v):
        q = rms_norm(split_heads(q, NA_HEADS), q_norm)
        k = rms_norm(split_heads(k, NA_HEADS), k_norm)
        return q, k, split_heads(v, NA_HEADS)
    ql, kl, vl = prep(*pl)
    qc, kc, vc = prep(*pc)
    o_lat = merge_heads(neighbourhood_attention(ql, kl, vl, kc, vc, rpb))
    o_ctx = merge_heads(context_attention(qc, kc, vc)) if with_ctx else None
    return o_lat, o_ctx


def gla_branch(pl, pc, w_gate, b_gate, out_norm, with_ctx):
    def prep(q, k, v, g):
        q = split_heads(q, GLA_HEADS) * GLA_DK ** -0.5
        k = split_heads(k, GLA_HEADS)
        v = split_heads(v, GLA_HEADS)
        g_f, g_b = jnp.split(g, 2, axis=-1)
        la_f = split_heads(jax.nn.log_sigmoid((g_f @ w_gate[0] + b_gate[0]).astype(jnp.float32)) / GLA_GATE_TAU, GLA_HEADS)
        la_b = split_heads(jax.nn.log_sigmoid((g_b @ w_gate[1] + b_gate[1]).astype(jnp.float32)) / GLA_GATE_TAU, GLA_HEADS)
        return q, k, v, la_f, la_b
    o_lat, o_ctx = bidir_scan(prep(*pc), prep(*pl), with_ctx)
    o_lat = head_rms_merge(o_lat, out_norm)
    o_ctx = head_rms_merge(o_ctx, out_norm) if with_ctx else None
    return o_lat, o_ctx


def log_gamma(offset):
    return jnp.log1p(-jnp.exp2(-(offset + jnp.arange(RET_HEADS, dtype=jnp.float32))))


def ret_branch(pl, pc, out_norm, with_ctx):
    def prep(q, k, v, rotate):
        q = split_heads(q, RET_HEADS) * RET_DK ** -0.5
        k = split_heads(k, RET_HEADS)
        if rotate:
            q, k = axial_rope(q), axial_rope(k)
        v = split_heads(v, RET_HEADS)
        b, h, t, _ = q.shape
        la_f = jnp.broadcast_to(log_gamma(RET_DECAY_FWD)[None, :, None, None], (b, h, t, 1))
        la_b = jnp.broadcast_to(log_gamma(RET_DECAY_BWD)[None, :, None, None], (b, h, t, 1))
        return q, k, v, la_f, la_b
    o_lat, o_ctx = bidir_scan(prep(*pc, False), prep(*pl, True), with_ctx)
    o_lat = head_rms_merge(o_lat, out_norm)
    o_ctx = head_rms_merge(o_ctx, out_norm) if with_ctx else None
    return o_lat, o_ctx


def merge_branches(outs, z, g, w_branch, w_out):
    zs = jnp.split(z, N_BRANCH, axis=-1)
    gs = jnp.split(g, N_BRANCH, axis=-1)
    y = jax.nn.sigmoid(gs[0]) * ((outs[0] * jax.nn.silu(zs[0])) @ w_branch[0])
    for i in range(1, N_BRANCH):
        y = y + jax.nn.sigmoid(gs[i]) * ((outs[i] * jax.nn.silu(zs[i])) @ w_branch[i])
    return y @ w_out


def hybrid_layer(x, xc, c, c_ctx, w_mod, b_mod, norm_w, w_in, na_q_norm, na_k_norm, na_rpb,
                 gla_w_gate, gla_b_gate, gla_out_norm, ret_out_norm, w_branch, w_out, with_ctx):
    mod = jax.nn.silu(c) @ w_mod + b_mod
    mod_c = jax.nn.silu(c_ctx) @ w_mod + b_mod
    shift, scale, gate = jnp.split(mod[:, None, :], 3, axis=-1)
    shift_c, scale_c, gate_c = jnp.split(mod_c, 3, axis=-1)
    h = rms_norm(x, norm_w) * (1.0 + scale) + shift
    hc = rms_norm(xc, norm_w) * (1.0 + scale_c) + shift_c
    (na_q, na_k, na_v, gla_q, gla_k, gla_v, gla_g, ret_q, ret_k, ret_v, z, g) = split_cols(h @ w_in)
    (na_qc, na_kc, na_vc, gla_qc, gla_kc, gla_vc, gla_gc, ret_qc, ret_kc, ret_vc, zc, gc) = split_cols(hc @ w_in)
    na_l, na_c = na_branch((na_q, na_k, na_v), (na_qc, na_kc, na_vc), na_q_norm, na_k_norm, na_rpb, with_ctx)
    gla_l, gla_c = gla_branch((gla_q, gla_k, gla_v, gla_g), (gla_qc, gla_kc, gla_vc, gla_gc),
                              gla_w_gate, gla_b_gate, gla_out_norm, with_ctx)
    ret_l, ret_c = ret_branch((ret_q, ret_k, ret_v), (ret_qc, ret_kc, ret_vc), ret_out_norm, with_ctx)
    x = x + gate * merge_branches((na_l, gla_l, ret_l), z, g, w_branch, w_out)
    if with_ctx:
        xc = xc + gate_c * merge_branches((na_c, gla_c, ret_c), zc, gc, w_branch, w_out)
    return x, xc


def setup_inputs(seed: int = 0) -> dict:
    key = jax.random.key(seed)
    ks = jax.random.split(key, 18)
    L, D = DEPTH, D_MODEL
    nrm = lambda k, shape, s: jax.random.normal(k, shape, jnp.float32) * s
    return {
        'x': nrm(ks[0], (BATCH, SEQ, D), 1.0),
        'c': nrm(ks[1], (BATCH, D), 1.0),
        'ctx': nrm(ks[2], (BATCH, CTX_LEN, D), 1.0),
        'c_ctx': nrm(ks[3], (D,), 1.0),
        'w_mod': nrm(ks[4], (L, D, 3 * D), 0.5 * D ** -0.5),
        'b_mod': nrm(ks[5], (L, 3 * D), 0.02),
        'norm_w': 1.0 + nrm(ks[6], (L, D), 0.02),
        'w_in': nrm(ks[7], (L, D, N_IN), D ** -0.5),
        'na_q_norm': 1.0 + nrm(ks[8], (L, NA_HEAD_DIM), 0.02),
        'na_k_norm': 1.0 + nrm(ks[9], (L, NA_HEAD_DIM), 0.02),
        'na_rpb': nrm(ks[10], (L, NA_HEADS, 2 * NA_WIN_ROWS - 1, 2 * NA_WIN_COLS - 1), 0.1),
        'gla_w_gate': nrm(ks[11], (L, 2, GLA_GATE_RANK, GLA_HEADS * GLA_DK), GLA_GATE_RANK ** -0.5),
        'gla_b_gate': nrm(ks[12], (L, 2, GLA_HEADS * GLA_DK), 0.1),
        'gla_out_norm': 1.0 + nrm(ks[13], (L, GLA_HEADS * GLA_DV), 0.02),
        'ret_out_norm': 1.0 + nrm(ks[14], (L, RET_HEADS * RET_DV), 0.02),
        'w_branch': nrm(ks[15], (L, N_BRANCH, BRANCH_W, D), BRANCH_W ** -0.5),
        'w_out': nrm(ks[16], (L, D, D), D ** -0.5),
    }


def reference(x, c, ctx, c_ctx, w_mod, b_mod, norm_w, w_in, na_q_norm, na_k_norm, na_rpb,
              gla_w_gate, gla_b_gate, gla_out_norm, ret_out_norm, w_branch, w_out):
    xc = ctx
    for layer in range(DEPTH):
        x, xc = hybrid_layer(x, xc, c, c_ctx, w_mod[layer], b_mod[layer], norm_w[layer], w_in[layer],
                             na_q_norm[layer], na_k_norm[layer], na_rpb[layer], gla_w_gate[layer],
                             gla_b_gate[layer], gla_out_norm[layer], ret_out_norm[layer], w_branch[layer],
                             w_out[layer], layer < DEPTH - 1)
    return x
```

```python
import os
import contextlib
import numpy as np
import ml_dtypes
import concourse.bass as bass
import concourse.mybir as mybir
from concourse.bass_utils import run_bass_kernel_spmd

F32 = mybir.dt.float32
BF16 = mybir.dt.bfloat16
AF = mybir.ActivationFunctionType
ALU = mybir.AluOpType

COMPUTE = ("pe", "act", "dve", "pool")
ALL_ENG = COMPUTE + ("sp",)

D = 1024
SEQ = 8192
CTX = 256
TOK = 2048
DEPTH = 2
GROUPS = [[0, 1, 2, 3], [4, 5, 6, 7]]
EPS = 1e-6
NEG = -30000.0


class Buf:
    __slots__ = ("name", "t", "last_ws", "readers")

    def __init__(self, name, t=None):
        self.name = name
        self.t = t
        self.last_ws = []
        self.readers = []

    def __getitem__(self, k):
        return self.t[k]


class Op:
    __slots__ = ("eng", "fn", "deps", "is_dma", "sem", "semval", "signal", "inc", "is_cc")

    def __init__(self, eng, fn, is_dma):
        self.eng = eng
        self.fn = fn
        self.deps = []
        self.is_dma = is_dma
        self.sem = None
        self.semval = None
        self.signal = False
        self.inc = 1
        self.is_cc = False


class Sched:
    def __init__(self, nc, n_dma_sems=32):
        self.nc = nc
        self.ops = {e: [] for e in ALL_ENG}
        self.n_dma_sems = n_dma_sems
        self.dma_rr = 0
        self.dma_last = [None] * n_dma_sems
        self.dma_count = [0] * n_dma_sems
        self.bar_deps = {}
        self.n_ops = 0

    def op(self, eng, fn, reads=(), writes=(), dma=False, inc=16, cw=False):
        o = Op(eng, fn, dma)
        deps = []
        for b in reads:
            deps.extend(b.last_ws)
        for b in writes:
            if not (cw and not b.readers):
                deps.extend(b.last_ws)
            deps.extend(b.readers)
        if eng in self.bar_deps:
            deps.extend(self.bar_deps.pop(eng))
        if dma:
            k = self.dma_rr
            self.dma_rr = (self.dma_rr + 1) % self.n_dma_sems
            prev = self.dma_last[k]
            if prev is not None:
                deps.append(prev)
            self.dma_last[k] = o
            self.dma_count[k] += inc
            o.sem = ("dma", k)
            o.semval = self.dma_count[k]
            o.inc = inc
            o.signal = True
        seen = set()
        for d in deps:
            if d is o or id(d) in seen:
                continue
            seen.add(id(d))
            if (not d.is_dma) and d.eng == eng:
                if eng == "pe" or eng == "sp":
                    continue
                if not any((d in b.last_ws) for b in reads):
                    continue
            o.deps.append(d)
        for b in reads:
            b.readers.append(o)
        for b in writes:
            if cw and not b.readers:
                b.last_ws.append(o)
            else:
                b.last_ws = [o]
            b.readers = []
        self.ops[eng].append(o)
        self.n_ops += 1
        return o

    def I(self, eng, name, reads, writes, *a, **kw):
        return self.op(eng, lambda e: getattr(e, name)(*a, **kw), reads, writes)

    def dma(self, out_ap, in_ap, reads=(), writes=(), eng="sp", cw=False, **kw):
        return self.op(eng, lambda e: e.dma_start(out=out_ap, in_=in_ap, **kw), reads, writes, dma=True, cw=cw)

    def barrier(self):
        last = []
        for e in ALL_ENG:
            for o in reversed(self.ops[e]):
                if not o.is_dma:
                    last.append(o)
                    break
        last.extend(o for o in self.dma_last if o is not None and not o.is_cc)
        for e in ALL_ENG:
            self.bar_deps[e] = list(last) + self.bar_deps.get(e, [])

    def emit(self, final_waits=()):
        nc = self.nc
        for e in ALL_ENG:
            for o in self.ops[e]:
                for d in o.deps:
                    if not d.is_dma:
                        d.signal = True
        for e in ALL_ENG:
            c = 0
            for o in self.ops[e]:
                if o.is_dma:
                    continue
                if o.signal:
                    c += 1
                    o.sem = ("eng", e)
                    o.semval = c
        sems = {}
        with contextlib.ExitStack() as st:
            for e in ALL_ENG:
                sems[("eng", e)] = st.enter_context(nc.semaphore("s_" + e))
            for k in range(self.n_dma_sems):
                sems[("dma", k)] = st.enter_context(nc.semaphore("s_dma%d" % k))
            block = st.enter_context(nc.Block())
            handles = {"pe": block.tensor, "act": block.scalar, "dve": block.vector,
                       "pool": block.gpsimd, "sp": block.sync}

            def make(e):
                def body(eng):
                    known = {}
                    for o in self.ops[e]:
                        need = {}
                        for d in o.deps:
                            if known.get(d.sem, 0) >= d.semval:
                                continue
                            if need.get(d.sem, 0) < d.semval:
                                need[d.sem] = d.semval
                        for s, v in need.items():
                            eng.wait_ge(sems[s], v)
                            known[s] = v
                        ins = o.fn(eng)
                        if o.signal:
                            ins.then_inc(sems[o.sem], o.inc if o.is_dma else 1)
                    if e == "sp":
                        need = {}
                        for d in final_waits:
                            if need.get(d.sem, 0) < d.semval:
                                need[d.sem] = d.semval
                        for s, v in need.items():
                            if known.get(s, 0) < v:
                                eng.wait_ge(sems[s], v)
                return body

            for e in ALL_ENG:
                handles[e](make(e))


def build_program(stage=99, depth=DEPTH):
    nc = bass.Bass("TRN2", target_bir_lowering=False)
    S = Sched(nc)
    L = DEPTH

    declared = []
    need = {"x_sh": 0, "ctx_b": 0, "cT": 0, "w_mod": 1, "b_modT": 1, "norm_wT": 1, "w_na": 2, "w_gr": 3, "w_zg": 4,
            "w_br": 4, "w_o": 4, "na_g": 2, "na_strip": 2, "na_edge": 2, "gla_wg": 3, "gla_bg": 3, "on_g": 3,
            "econst": 3, "rope": 3, "sel": 0, "c_ident": 0, "c_identf": 0, "c_tri": 0, "c_bones": 0}

    def din(name, shape, dt=F32):
        if stage < need[name]:
            return None
        declared.append(name)
        return nc.dram_tensor(name, list(shape), dt, kind="ExternalInput").ap()

    x_sh = din("x_sh", [TOK, D])
    ctx_b = din("ctx_b", [CTX, D])
    cT = din("cT", [128, 8, 2])
    w_mod = din("w_mod", [L, D, 3 * D])
    b_modT = din("b_modT", [L, 128, 24])
    norm_wT = din("norm_wT", [L, 128, 8])
    w_na = din("w_na", [L, D, 384])
    w_gr = din("w_gr", [L, D, 800])
    w_zg = din("w_zg", [L, D, 4608])
    w_br = din("w_br", [L, 3, 512, D])
    w_o = din("w_o", [L, D, D])
    na_g = din("na_g", [L, 128, 2])
    na_strip = din("na_strip", [L, 2, 128, 22 * 64])
    na_edge = din("na_edge", [L, 2, 12, 128, 512])
    gla_wg = din("gla_wg", [L, 2, 16, 64])
    gla_bg = din("gla_bg", [L, 64, 2])
    on_g = din("on_g", [L, 128, 2])
    econst = din("econst", [4, 128, 512])
    rope = din("rope", [2, 128, SEQ])
    sel_in = din("sel", [128, 4])
    c_ident = din("c_ident", [128, 128], BF16)
    c_identf = din("c_identf", [128, 128])
    c_tri = din("c_tri", [2, 128, 128], BF16)
    c_bones = din("c_bones", [128, 128], BF16)
    y_out = nc.dram_tensor("y", [TOK, D], F32, kind="ExternalOutput").ap()
    dbg_out = {}

    hT_loc = [nc.dram_tensor("hT_loc%d" % i, [D, 512], BF16).ap() for i in range(4)]
    hT_all = [nc.dram_tensor("hT_all%d" % i, [4 * D, 512], BF16).ap() for i in range(4)]
    b_hT_loc = [Buf("hT_loc%d" % i) for i in range(4)]
    b_hT_all = [Buf("hT_all%d" % i) for i in range(4)]
    oN_loc = [nc.dram_tensor("oN_loc%d" % i, [128, 4096], BF16).ap() for i in range(2)]
    oN_all = [nc.dram_tensor("oN_all%d" % i, [4 * 128, 4096], BF16).ap() for i in range(2)]
    oG_loc = [nc.dram_tensor("oG_loc%d" % i, [256, 2048], BF16).ap() for i in range(4)]
    oG_all = [nc.dram_tensor("oG_all%d" % i, [4 * 256, 2048], BF16).ap() for i in range(4)]
    oC_loc = nc.dram_tensor("oC_loc", [384, CTX], BF16).ap()
    oC_all = nc.dram_tensor("oC_all", [4 * 384, CTX], BF16).ap()
    b_oN_loc = [Buf("oN_loc%d" % i) for i in range(2)]
    b_oN_all = [Buf("oN_all%d" % i) for i in range(2)]
    b_oG_loc = [Buf("oG_loc%d" % i) for i in range(4)]
    b_oG_all = [Buf("oG_all%d" % i) for i in range(4)]
    b_oC_loc, b_oC_all = Buf("oC_loc"), Buf("oC_all")

    def o_loc_ap(r0, r1, t0, n):
        if t0 >= SEQ:
            return oC_loc[r0:r1, t0 - SEQ:t0 - SEQ + n], b_oC_loc
        if r1 <= 128:
            p = t0 // 4096
            return oN_loc[p][r0:r1, t0 % 4096:t0 % 4096 + n], b_oN_loc[p]
        p = t0 // 2048
        return oG_loc[p][r0 - 128:r1 - 128, t0 % 2048:t0 % 2048 + n], b_oG_loc[p]

    def o_all_ap(src_, br, t0, n):
        if t0 >= SEQ:
            r0 = src_ * 384 + br * 128
            return oC_all[r0:r0 + 128, t0 - SEQ:t0 - SEQ + n], b_oC_all
        if br == 0:
            p = t0 // 4096
            return oN_all[p][src_ * 128:(src_ + 1) * 128, t0 % 4096:t0 % 4096 + n], b_oN_all[p]
        p = t0 // 2048
        r0 = src_ * 256 + (br - 1) * 128
        return oG_all[p][r0:r0 + 128, t0 % 2048:t0 % 2048 + n], b_oG_all[p]

    NTT = SEQ + CTX
    qk_scr = nc.dram_tensor("qk_scr", [2, 128, NTT], F32).ap()
    v_scr = nc.dram_tensor("v_scr", [NTT, 256], BF16).ap()
    gb_scr = nc.dram_tensor("gb_scr", [16, NTT], BF16).ap()
    b_scr = {}

    def scr_buf(kind, tb):
        return b_scr.setdefault((kind, tb), Buf("scr_%s_%d" % (kind, tb)))

    def allgather(src, dst, bsrc, bdst):
        o_ = S.op("pool", lambda e: e.collective_compute("AllGather", ALU.bypass, replica_groups=GROUPS,
                                                          ins=[src], outs=[dst]),
                  reads=[bsrc], writes=[bdst], dma=True, inc=1)
        o_.is_cc = True

    uid = [0]

    def T(name, shape, dt=F32):
        uid[0] += 1
        return Buf(name, nc.alloc_sbuf_tensor("%s_%d" % (name, uid[0]), list(shape), dt))

    class Phase:
        def __init__(self):
            self.st = contextlib.ExitStack()

        def tile(self, name, shape, dt=F32):
            uid[0] += 1
            return Buf(name, self.st.enter_context(nc.sbuf_tensor("%s_%d" % (name, uid[0]), list(shape), dt)))

        def close(self):
            S.barrier()
            self.st.close()

    PSA = [Buf("psa%d" % i, nc.alloc_psum_tensor("psa%d" % i, [128, 512], F32)) for i in range(4)]
    PS2 = Buf("ps2", nc.alloc_psum_tensor("ps2", [128, 1024], F32))
    PSB = [Buf("psb%d" % i, nc.alloc_psum_tensor("psb%d" % i, [128, 1024], BF16)) for i in range(2)]
    rr = {"a": 0, "b": 0}

    def psa():
        rr["a"] = (rr["a"] + 1) % 4
        return PSA[rr["a"]]

    def psb():
        rr["b"] = (rr["b"] + 1) % 2
        return PSB[rr["b"]]

    xres = T("xres", [128, 16, D])
    xcres = T("xcres", [128, 2, D])
    hTc = T("hTc", [128, 8, CTX], BF16)
    ident = T("ident", [128, 128], BF16)
    identf = T("identf", [128, 128])
    onesf = T("onesf", [128, 128])
    tri = T("tri", [128, 2, 128], BF16)
    bones = T("bones", [128, 128], BF16)
    sel = T("sel", [128, 4])
    cTs = T("cTs", [128, 8, 2])
    modTs = [T("modT%d" % i, [128, 24, 2]) for i in range(DEPTH)]
    weffs = [T("weff%d" % i, [128, 8, 2]) for i in range(DEPTH)]
    small = T("small", [128, 64])

    for t in range(16):
        S.dma(xres[:, t, :], x_sh[t * 128:(t + 1) * 128, :], writes=[xres], cw=True)
    S.dma(xcres[:], ctx_b.rearrange("(t p) d -> p t d", p=128), writes=[xcres])
    S.dma(ident[:], c_ident, writes=[ident])
    S.dma(identf[:], c_identf, writes=[identf])
    S.dma(tri[:], c_tri.rearrange("a p n -> p a n"), writes=[tri])
    S.dma(bones[:], c_bones, writes=[bones])
    S.dma(sel[:], sel_in, writes=[sel])
    S.dma(cTs[:], cT, writes=[cTs])
    S.I("dve", "memset", [], [onesf], onesf[:], 1.0)
    dsel = T("dsel", [128, 4, 128], BF16)
    for q_ in range(4):
        S.I("dve", "tensor_scalar", [ident, sel], [dsel], dsel[:, q_, :], ident[:], sel[:, q_:q_ + 1], None, ALU.mult)

    outs = []

    def dbg(name, buf, ap, shape, dt=F32):
        t = nc.dram_tensor("dbg_" + name, list(shape), dt, kind="ExternalOutput").ap()
        dbg_out[name] = t
        outs.append(S.dma(t, ap, reads=[buf]))

    def phase_mod(layers):
        P = Phase()
        sT = P.tile("sT", [128, 8, 2])
        wmb = [P.tile("wm%d" % i, [128, 8, 512]) for i in range(3)]
        S.I("act", "activation", [cTs], [sT], sT[:], cTs[:], AF.Silu)
        it = 0
        for l in layers:
            modT, weff = modTs[l], weffs[l]
            bmt = P.tile("bmt", [128, 24])
            nwt = P.tile("nwt", [128, 8])
            S.dma(bmt[:], b_modT[l], writes=[bmt])
            S.dma(nwt[:], norm_wT[l], writes=[nwt])
            psm = psa()
            for cc in range(6):
                wm = wmb[it % 3]
                it += 1
                S.dma(wm[:], w_mod[l][:, cc * 512:(cc + 1) * 512].rearrange("(k p) c -> p k c", p=128), writes=[wm])
                for sub in range(4):
                    c24 = cc * 4 + sub
                    for k in range(8):
                        S.I("pe", "matmul", [wm, sT], [psm], psm[:, c24 * 2:c24 * 2 + 2],
                            wm[:, k, sub * 128:(sub + 1) * 128], sT[:, k, :], start=(k == 0), stop=(k == 7))
            S.I("dve", "tensor_tensor", [psm, bmt], [modT], modT[:],
                psm[:, 0:48].rearrange("p (c t) -> p c t", t=2), bmt[:].unsqueeze(2).to_broadcast([128, 24, 2]), ALU.add)
            S.I("dve", "scalar_tensor_tensor", [modT, nwt], [weff], weff[:], modT[:, 8:16, :], 1.0,
                nwt[:].unsqueeze(2).to_broadcast([128, 8, 2]), ALU.add, ALU.mult)
        P.close()

    def phase_A(l):
        P = Phase()
        modT, weff = modTs[l], weffs[l]
        SUB = int(os.environ.get("KSUB", "9"))
        if SUB < 2:
            P.close()
            return
        ss = P.tile("ss", [128, 18])
        rstd = P.tile("rstd", [128, 18])
        junk = P.tile("junk", [128, D], BF16)
        xnb = [P.tile("xn%d" % i, [128, D], BF16) for i in range(2)]
        stg = [P.tile("stg%d" % i, [128, 8, 512], BF16) for i in range(2)]
        for tt in range(18):
            isctx = tt >= 16
            xt = xcres[:, tt - 16, :] if isctx else xres[:, tt, :]
            xb_ = xcres if isctx else xres
            tsel = 1 if isctx else 0
            S.I("act", "activation", [xb_], [junk, ss], junk[:], xt, AF.Square, accum_out=ss[:, tt:tt + 1])
            S.I("act", "activation", [ss], [rstd], rstd[:, tt:tt + 1], ss[:, tt:tt + 1], AF.Ln, bias=EPS, scale=1.0 / D)
            S.I("act", "activation", [rstd], [rstd], rstd[:, tt:tt + 1], rstd[:, tt:tt + 1], AF.Exp, scale=-0.5)
            xn = xnb[tt % 2]
            S.I("dve", "tensor_scalar", [xb_, rstd], [xn], xn[:], xt, rstd[:, tt:tt + 1], None, ALU.mult)
            pt = psb()
            for k in range(8):
                S.I("pe", "transpose", [xn, ident], [pt], pt[:, k * 128:(k + 1) * 128], xn[:, k * 128:(k + 1) * 128], ident[:])
            if isctx:
                dst, dbuf, c0 = hTc, hTc, (tt - 16) * 128
            else:
                dbuf = stg[(tt // 4) % 2]
                dst, c0 = dbuf, (tt % 4) * 128
            for k in range(8):
                eng = "act" if k % 2 == 0 else "dve"
                if eng == "act":
                    S.I("act", "activation", [pt, weff, modT], [dbuf], dst[:, k, c0:c0 + 128], pt[:, k * 128:(k + 1) * 128],
                        AF.Identity, bias=modT[:, k, tsel:tsel + 1], scale=weff[:, k, tsel:tsel + 1])
                else:
                    S.I("dve", "tensor_scalar", [pt, weff, modT], [dbuf], dst[:, k, c0:c0 + 128], pt[:, k * 128:(k + 1) * 128],
                        weff[:, k, tsel:tsel + 1], modT[:, k, tsel:tsel + 1], ALU.mult, ALU.add)
            if (not isctx) and tt % 4 == 3:
                blk = tt // 4
                S.dma(hT_loc[blk].rearrange("(k p) n -> p k n", p=128), dbuf[:],
                      reads=[dbuf], writes=[b_hT_loc[blk]])
                allgather(hT_loc[blk], hT_all[blk], b_hT_loc[blk], b_hT_all[blk])
        P.close()

    def load_hb(hb, tb):
        r, blk = tb // 4, tb % 4
        S.dma(hb[:], hT_all[blk][r * D:(r + 1) * D, :].rearrange("(k p) n -> p k n", p=128),
              reads=[b_hT_all[blk]], writes=[hb])

    def proj_fm(ps, w, c0, m, hsrc, n0, n, m0=0):
        for k in range(8):
            S.I("pe", "matmul", [w, hsrc], [ps], ps[m0:m0 + m, 0:n], w[:, k, c0:c0 + m], hsrc[:, k, n0:n0 + n],
                start=(k == 0), stop=(k == 7))

    def phase_N(l, with_ctx):
        P = Phase()
        wna = P.tile("wna", [128, 8, 384], BF16)
        S.dma(wna[:], w_na[l].rearrange("(k p) c -> p k c", p=128), writes=[wna], eng="pool")
        nag = P.tile("nag", [128, 2])
        S.dma(nag[:], na_g[l], writes=[nag])
        QT = P.tile("QT", [128, SEQ], BF16)
        KT = P.tile("KT", [128, SEQ], BF16)
        VA = P.tile("VA", [128, 64, 192], BF16)
        QTc = P.tile("QTc", [128, CTX], BF16)
        KTc = P.tile("KTc", [128, CTX], BF16)
        VAc = P.tile("VAc", [128, 2, 192], BF16)
        S.I("pool", "memset", [], [VA], VA[:, :, 64:128], 1.0)
        S.I("pool", "memset", [], [VAc], VAc[:, :, 64:128], 1.0)
        hbs = [P.tile("hb%d" % i, [128, 8, 512], BF16) for i in range(2)]
        sq = P.tile("sq", [128, 512], BF16)
        rs = P.tile("rs", [128, 512])

        def qk_norm(ps, n, dst_buf, dst_ap, gcol):
            S.I("act", "activation", [ps], [sq], sq[:, 0:n], ps[:, 0:n], AF.Square)
            p2 = psa()
            S.I("pe", "matmul", [bones, sq], [p2], p2[:, 0:n], bones[:], sq[:, 0:n], start=True, stop=True)
            S.I("act", "activation", [p2], [rs], rs[:, 0:n], p2[:, 0:n], AF.Ln, bias=EPS, scale=1.0 / 64)
            S.I("act", "activation", [rs], [rs], rs[:, 0:n], rs[:, 0:n], AF.Exp, scale=-0.5)
            S.I("dve", "scalar_tensor_tensor", [ps, nag, rs], [dst_buf], dst_ap, ps[:, 0:n], nag[:, gcol:gcol + 1],
                rs[:, 0:n], ALU.mult, ALU.mult)

        def project(hsrc, n, qdst, kdst, vdst, vt0, qb, kb, vb):
            pq = psa()
            proj_fm(pq, wna, 0, 128, hsrc, 0, n)
            qk_norm(pq, n, qb, qdst, 0)
            pk = psa()
            proj_fm(pk, wna, 128, 128, hsrc, 0, n)
            qk_norm(pk, n, kb, kdst, 1)
            pv = psa()
            nt = n // 128
            for s in range(nt):
                for k in range(8):
                    S.I("pe", "matmul", [hsrc, wna], [pv], pv[:, s * 128:(s + 1) * 128], hsrc[:, k, s * 128:(s + 1) * 128],
                        wna[:, k, 256:384], start=(k == 0), stop=(k == 7))
            pv3 = pv[:, 0:n].rearrange("p (s c) -> p s c", c=128)
            S.I("act", "activation", [pv], [vb], vdst[:, vt0:vt0 + nt, 0:64], pv3[:, :, 0:64], AF.Copy)
            S.I("dve", "tensor_copy", [pv], [vb], vdst[:, vt0:vt0 + nt, 128:192], pv3[:, :, 64:128])

        project(hTc, CTX, QTc[:], KTc[:], VAc, 0, QTc, KTc, VAc)
        for it, tb in enumerate([r_ * 4 + blk_ for blk_ in range(4) for r_ in range(4)]):
            hb = hbs[it % 2]
            load_hb(hb, tb)
            project(hb, 512, QT[:, tb * 512:(tb + 1) * 512], KT[:, tb * 512:(tb + 1) * 512], VA, tb * 4, QT, KT, VA)

        if stage == 2:
            dbg("QT", QT, QT[:], [128, SEQ], BF16)
            dbg("KT", KT, KT[:], [128, SEQ], BF16)

        msk = P.tile("msk", [128, 22 * 64], BF16)
        edg = P.tile("edg", [128, 12, 512], BF16)
        bst = [P.tile("bst%d" % i, [128, 704]) for i in range(2)]
        pts = [P.tile("pt%d" % i, [128, 512], BF16) for i in range(4)]
        ona = [P.tile("ona%d" % i, [128, 512], BF16) for i in range(2)]
        rden = P.tile("rden", [128, 512])
        lnd = P.tile("lnd", [128, 512])
        ptc = [0]

        def va_lhsT(vbuf, tile_i, h):
            return vbuf[:, tile_i, 0:128] if h == 0 else vbuf[:, tile_i, 64:192]

        sbanks = [PSA[2], PSA[3], Buf("ps2a", PS2.t[:, 0:512]), Buf("ps2b", PS2.t[:, 512:1024]),
                  Buf("psbf0", PSB[0].t[:].bitcast(F32)), Buf("psbf1", PSB[1].t[:].bitcast(F32))]
        pobanks = [PSA[0], PSA[1]]
        pts = pts + [P.tile("ptx%d" % i, [128, 512], BF16) for i in range(2)]
        LAG = 3
        items = []
        for h in range(2):
            hp = h * 64
            for m in range(16):
                qcols = slice(m * 512, (m + 1) * 512)
                keys = []
                for kt in range(8):
                    kr0 = 8 * m - 4 + 2 * kt
                    if kr0 < 0 or kr0 >= 128:
                        continue
                    if m == 0:
                        mk, mb = edg[:, kt - 2, :], edg
                    elif m == 15:
                        mk, mb = edg[:, 6 + kt, :], edg
                    else:
                        e0 = 14 - 2 * kt
                        mk, mb = msk[:, e0 * 64:(e0 + 8) * 64], msk
                    keys.append((KT, KT[hp:hp + 64, kr0 * 64:kr0 * 64 + 128], VA, kr0 // 2, mk, mb))
                for ct in range(2):
                    keys.append((KTc, KTc[hp:hp + 64, ct * 128:(ct + 1) * 128], VAc, ct, None, None))
                for i, kk in enumerate(keys):
                    items.append((h, m, i, len(keys)) + kk + (QT, QT[hp:hp + 64, qcols], 512, m * 512))
            if with_ctx:
                for ct in range(2):
                    items.append((h, 16, ct, 2, KTc, KTc[hp:hp + 64, ct * 128:(ct + 1) * 128], VAc, ct, None, None,
                                  QTc, QTc[hp:hp + 64, :], CTX, SEQ))
        cur_h = [-1]
        pend = {}
        for idx in range(len(items) + LAG):
            if idx < len(items):
                (h, m, i, nk, kbuf, kap, vbuf, vt, mk, mb, qbuf, qap, n, t0) = items[idx]
                if h != cur_h[0]:
                    cur_h[0] = h
                    for i2 in range(2):
                        b_ = bst[i2]
                        S.dma(b_[:], na_strip[l, h][:, i2 * 704:(i2 + 1) * 704], writes=[b_])
                        S.I("act", "activation", [b_], [msk], msk[:, i2 * 704:(i2 + 1) * 704], b_[:], AF.Exp)
                    for i2 in range(12):
                        b_ = bst[i2 % 2]
                        S.dma(b_[:, 0:512], na_edge[l, h, i2], writes=[b_])
                        S.I("act", "activation", [b_], [edg], edg[:, i2, :], b_[:, 0:512], AF.Exp)
                ps_ = sbanks[idx % 6]
                pt_ = pts[idx % 6]
                S.I("pe", "matmul", [kbuf, qbuf], [ps_], ps_[:, 0:n], kap, qap, start=True, stop=True)
                S.I("act", "activation", [ps_], [pt_], pt_[:, 0:n], ps_[:, 0:n], AF.Exp, scale=0.125)
                if mk is not None:
                    S.I("dve" if idx % 2 == 0 else "pool", "tensor_tensor", [pt_, mb], [pt_], pt_[:], pt_[:], mk, ALU.mult)
                pend[idx] = pt_
            j2 = idx - LAG
            if j2 >= 0:
                (h, m, i, nk, kbuf, kap, vbuf, vt, mk, mb, qbuf, qap, n, t0) = items[j2]
                hp = h * 64
                pt_ = pend.pop(j2)
                po = pobanks[(h * 17 + m) % 2]
                S.I("pe", "matmul", [vbuf, pt_], [po], po[:, 0:n], va_lhsT(vbuf, vt, h), pt_[:, 0:n],
                    start=(i == 0), stop=(i == nk - 1))
                if i == nk - 1:
                    ob = ona[(h * 17 + m) % 2]
                    dp = 64 - hp
                    S.I("act", "activation", [po], [lnd], lnd[dp:dp + 64, 0:n], po[dp:dp + 64, 0:n], AF.Ln)
                    S.I("act", "activation", [lnd], [lnd], lnd[dp:dp + 64, 0:n], lnd[dp:dp + 64, 0:n], AF.Exp, scale=-1.0)
                    S.I("dve", "tensor_copy", [lnd], [rden], rden[hp:hp + 64, 0:n], lnd[dp:dp + 64, 0:n])
                    S.I("dve", "tensor_tensor", [po, rden], [ob], ob[hp:hp + 64, 0:n], po[hp:hp + 64, 0:n],
                        rden[hp:hp + 64, 0:n], ALU.mult)
                    oap, obf = o_loc_ap(hp, hp + 64, t0, n)
                    S.dma(oap, ob[hp:hp + 64, 0:n], reads=[ob], writes=[obf])
        P.close()


    def phase_GR(l, with_ctx):
        P = Phase()
        wgr = P.tile("wgr", [128, 8, 800], BF16)
        S.dma(wgr[:], w_gr[l].rearrange("(k p) c -> p k c", p=128), writes=[wgr], eng="pool")
        wgt = P.tile("wgt", [16, 2, 64], BF16)
        S.dma(wgt[:], gla_wg[l].rearrange("a k c -> k a c"), writes=[wgt], eng="pool")
        nb = P.tile("nb", [64, 2])
        S.dma(nb[:], gla_bg[l], writes=[nb])
        S.I("act", "mul", [nb], [nb], nb[:], nb[:], -1.0)
        ong = P.tile("ong", [128, 2])
        S.dma(ong[:], on_g[l], writes=[ong])
        for p_ in range(2):
            allgather(oN_loc[p_], oN_all[p_], b_oN_loc[p_], b_oN_all[p_])
        onesb = P.tile("onesb", [128, 128], BF16)
        S.I("dve", "memset", [], [onesb], onesb[:], 1.0)
        OF = P.tile("OF", [128, 2, SEQ + CTX], BF16)
        St = P.tile("St", [128, 128])
        Sbf = P.tile("Sbf", [128, 128], BF16)
        tmpS = P.tile("tmpS", [128, 128])
        ET = [[P.tile("E%d%d" % (k_, p_), [128, 512]) for p_ in range(2)] for k_ in range(2)]
        hbs = [P.tile("hb%d" % i, [128, 8, 512], BF16) for i in range(2)]
        cosb = [P.tile("cos%d" % i, [128, 512]) for i in range(2)]
        sinb = [P.tile("sin%d" % i, [128, 512]) for i in range(2)]
        t1 = P.tile("t1", [128, 512])
        t2 = P.tile("t2", [128, 512])
        t3 = P.tile("t3", [128, 512])
        t4 = P.tile("t4", [128, 512])
        qa = P.tile("qa", [128, 512])
        ka = P.tile("ka", [128, 512])
        QP = [P.tile("QP%d" % i, [128, 512], BF16) for i in range(2)]
        KP = [P.tile("KP%d" % i, [128, 512], BF16) for i in range(2)]
        VB = [P.tile("VB%d" % i, [128, 4, 256], BF16) for i in range(2)]
        gT = P.tile("gT", [16, 512], BF16)
        gTl = [P.tile("gTl%d" % i, [16, 512], BF16) for i in range(2)]
        e1 = P.tile("e1", [64, 512])
        nl = P.tile("nl", [64, 512])
        cum = P.tile("cum", [64, 512])
        rn = P.tile("rn", [64, 512])
        onesc = P.tile("onesc", [64, 128])
        S.I("dve", "memset", [], [onesc], onesc[:], 1.0)
        Am = [P.tile("Am%d" % i, [128, 2, 128], BF16) for i in range(2)]
        ktok = [P.tile("ktok%d" % i, [128, 128], BF16) for i in range(2)]
        o32s = [P.tile("o32_%d" % i, [128, 512]) for i in range(2)]
        sqos = [P.tile("sqo%d" % i, [128, 512], BF16) for i in range(2)]
        rsos = [P.tile("rso%d" % i, [128, 512]) for i in range(2)]
        onb = [P.tile("onb%d" % i, [128, 512], BF16) for i in range(2)]
        psO = [PSA[0], PSA[1]]
        PSBf = Buf("psbf", None)
        gen = [PSA[2], PSA[3]]
        gi = [0]

        def pg():
            gi[0] = (gi[0] + 1) % 2
            return gen[gi[0]]

        cnt = [0]
        pending_ag = []
        deferred = []

        def ag_piece(p_):
            if p_ == 8:
                allgather(oC_loc, oC_all, b_oC_loc, b_oC_all)
            else:
                allgather(oG_loc[p_], oG_all[p_], b_oG_loc[p_], b_oG_all[p_])

        for direction in range(2):
            if direction == 0:
                blocks = [-1] + list(range(16))
            else:
                blocks = [-1] + list(range(15, -1, -1))
            S.I("dve", "memset", [], [St], St[:], 0.0)
            S.I("dve", "memset", [], [Sbf], Sbf[:], 0.0)
            for k_ in range(2):
                for p_ in range(2):
                    S.dma(ET[k_][p_][:], econst[direction * 2 + k_], writes=[ET[k_][p_]])
            def issue_loads(bi2):
                tb2 = blocks[bi2]
                par2 = bi2 % 2
                if direction == 1:
                    n2 = CTX if tb2 < 0 else 512
                    t02 = SEQ if tb2 < 0 else tb2 * 512
                    S.dma(cosb[par2][:, 0:n2], qk_scr[0][:, t02:t02 + n2], reads=[scr_buf("q", tb2)], writes=[cosb[par2]])
                    S.dma(sinb[par2][:, 0:n2], qk_scr[1][:, t02:t02 + n2], reads=[scr_buf("k", tb2)], writes=[sinb[par2]])
                    S.dma(gTl[par2][:, 0:n2], gb_scr[:, t02:t02 + n2], reads=[scr_buf("g", tb2)], writes=[gTl[par2]])
                    return
                if tb2 < 0:
                    return
                load_hb(hbs[par2], tb2)
                S.dma(cosb[par2][:], rope[0][:, tb2 * 512:(tb2 + 1) * 512], writes=[cosb[par2]])
                S.dma(sinb[par2][:], rope[1][:, tb2 * 512:(tb2 + 1) * 512], writes=[sinb[par2]])

            def issue_v(bi2):
                tb2 = blocks[bi2]
                par2 = bi2 % 2
                n2 = CTX if tb2 < 0 else 512
                t02 = SEQ if tb2 < 0 else tb2 * 512
                S.dma(VB[par2][:, 0:n2 // 128, :], v_scr[t02:t02 + n2, :].rearrange("(t p) c -> p t c", p=128),
                      reads=[scr_buf("v", tb2)], writes=[VB[par2]])

            def prologue(bi):
                tb = blocks[bi]
                isctx = tb < 0
                n = CTX if isctx else 512
                nch = n // 128
                par = bi % 2
                hsrc = hTc if isctx else hbs[par]
                EQ, EK = ET[0][par], ET[1][par]
                t0s = SEQ if isctx else tb * 512
                if direction == 0:
                    pG = pg()
                    proj_fm(pG, wgr, 512, 16, hsrc, 0, n)
                    S.I("act", "activation", [pG], [gT], gT[:, 0:n], pG[0:16, 0:n], AF.Copy)
                    gsrc = gT
                    pGb = pg()
                    proj_fm(pGb, wgr, 528, 16, hsrc, 0, n)
                    S.I("act", "activation", [pGb], [gTl[par]], gTl[par][:, 0:n], pGb[0:16, 0:n], AF.Copy)
                    S.dma(gb_scr[:, t0s:t0s + n], gTl[par][:, 0:n], reads=[gTl[par]], writes=[scr_buf("g", tb)])
                else:
                    gsrc = gTl[par]
                pL = pg()
                S.I("pe", "matmul", [wgt, gsrc], [pL], pL[0:64, 0:n], wgt[:, direction, :], gsrc[:, 0:n], start=True, stop=True)
                S.I("act", "activation", [pL, nb], [e1], e1[:, 0:n], pL[0:64, 0:n], AF.Exp, bias=nb[:, direction:direction + 1], scale=-1.0)
                S.I("act", "activation", [e1], [nl], nl[:, 0:n], e1[:, 0:n], AF.Ln, bias=1.0)
                for c in range(nch):
                    cs = slice(c * 128, (c + 1) * 128)
                    S.I("dve", "tensor_tensor_scan", [onesc, nl], [cum], cum[:, cs], onesc[:], nl[:, cs], 0.0, ALU.mult, ALU.add)
                if direction == 0:
                    src_, sb_ = cum, cum
                else:
                    S.I("dve", "tensor_tensor", [nl, cum], [rn], rn[:, 0:n], nl[:, 0:n], cum[:, 0:n], ALU.subtract)
                    for c in range(nch):
                        cs = slice(c * 128, (c + 1) * 128)
                        S.I("dve", "tensor_scalar", [rn, cum], [rn], rn[:, cs], rn[:, cs], cum[:, c * 128 + 127:c * 128 + 128], None, ALU.add)
                    src_, sb_ = rn, rn
                S.I("act", "activation", [sb_], [EQ], EQ[0:64, 0:n], src_[:, 0:n], AF.Exp, scale=-1.0 / 16)
                S.I("act", "activation", [sb_], [EK], EK[0:64, 0:n], src_[:, 0:n], AF.Exp, scale=1.0 / 16)
                qp, kp = QP[par], KP[par]
                for qi_, (c0, c0p, dst, E_, ta, tb_, acc) in enumerate(((0, 256, qp, EQ, t1, t2, qa), (128, 384, kp, EK, t3, t4, ka))):
                    if direction == 1:
                        stag = cosb[par] if qi_ == 0 else sinb[par]
                        S.I("dve", "tensor_tensor", [stag, E_], [dst], dst[:, 0:n], stag[:, 0:n], E_[:, 0:n], ALU.mult)
                        continue
                    pq = pg()
                    proj_fm(pq, wgr, c0, 128, hsrc, 0, n)
                    if isctx:
                        S.I("dve", "tensor_copy", [pq], [acc], acc[:, 0:n], pq[:, 0:n])
                        S.dma(qk_scr[qi_][:, t0s:t0s + n], acc[:, 0:n], reads=[acc], writes=[scr_buf("qk"[qi_], tb)])
                        S.I("dve", "tensor_tensor", [acc, E_], [dst], dst[:, 0:n], acc[:, 0:n], E_[:, 0:n], ALU.mult)
                    else:
                        S.I("dve", "tensor_tensor", [pq, cosb[par]], [ta], ta[:], pq[:, :], cosb[par][:], ALU.mult)
                        pqp = pg()
                        proj_fm(pqp, wgr, c0p, 128, hsrc, 0, n)
                        S.I("dve", "tensor_tensor", [pqp, sinb[par]], [tb_], tb_[:], pqp[:, :], sinb[par][:], ALU.mult)
                        S.I("dve", "tensor_tensor", [ta, tb_], [acc], acc[:], ta[:], tb_[:], ALU.add)
                        S.dma(qk_scr[qi_][:, t0s:t0s + n], acc[:, 0:n], reads=[acc], writes=[scr_buf("qk"[qi_], tb)])
                        S.I("dve", "tensor_tensor", [acc, E_], [dst], dst[:], acc[:], E_[:], ALU.mult)
                vb = VB[par]
                for half in range((nch + 1) // 2 if direction == 0 else 0):
                    pv = pg()
                    for s2 in range(2):
                        s_ = half * 2 + s2
                        if s_ >= nch:
                            continue
                        for k in range(8):
                            S.I("pe", "matmul", [hsrc, wgr], [pv], pv[:, s2 * 256:(s2 + 1) * 256], hsrc[:, k, s_ * 128:(s_ + 1) * 128],
                                wgr[:, k, 544:800], start=(k == 0), stop=(k == 7))
                    S.I("act", "activation", [pv], [vb], vb[:, half * 2:half * 2 + 2, :],
                        pv[:, :].rearrange("p (s c) -> p s c", c=256), AF.Copy)
                if direction == 0:
                    S.dma(v_scr[t0s:t0s + n, :].rearrange("(t p) c -> p t c", p=128), vb[:, 0:nch, :],
                          reads=[vb], writes=[scr_buf("v", tb)])

            for bi, tb in enumerate(blocks):
                isctx = tb < 0
                n = CTX if isctx else 512
                nch = n // 128
                par = bi % 2
                if bi == 0:
                    issue_loads(0)
                    issue_loads(1)
                    if direction == 1:
                        issue_v(0)
                    prologue(0)
                if direction == 1 and bi + 1 < len(blocks):
                    issue_v(bi + 1)
                if bi + 2 < len(blocks):
                    issue_loads(bi + 2)
                if bi + 1 < len(blocks):
                    prologue(bi + 1)
                for (p_, when) in list(pending_ag):
                    if when <= bi:
                        ag_piece(p_)
                        pending_ag.remove((p_, when))
                EQ, EK = ET[0][par], ET[1][par]
                qp, kp, vb = QP[par], KP[par], VB[par]
                for fn_ in deferred:
                    fn_()
                deferred = []
                chunks = list(range(nch)) if direction == 0 else list(range(nch - 1, -1, -1))
                for ci, c in enumerate(chunks):
                    cs = slice(c * 128, (c + 1) * 128)
                    cnt[0] += 1
                    am, kt_ = Am[cnt[0] % 2], ktok[cnt[0] % 2]
                    S.I("pe", "matmul", [kp, qp], [PS2], PS2[:, 0:128], kp[0:64, cs], qp[0:64, cs], start=True, stop=True)
                    S.I("pe", "matmul", [kp, qp], [PS2], PS2[:, 512:640], kp[64:128, cs], qp[64:128, cs], start=True, stop=True)
                    S.I("dve", "tensor_tensor", [PS2, tri], [am], am[:],
                        PS2[:, :].rearrange("p (a b) -> p a b", b=512)[:, :, 0:128],
                        tri[:, direction:direction + 1, :].to_broadcast([128, 2, 128]), ALU.mult)
                    pt = PSB[0]
                    S.I("pe", "transpose", [kp, ident], [pt], pt[:, 0:128], kp[:, cs], ident[:])
                    S.I("act", "activation", [pt], [kt_], kt_[:], pt[:, 0:128], AF.Copy)
                    for mx in range(2):
                        r0 = mx * 64
                        po_ = psO[mx]
                        S.I("pe", "matmul", [vb, am], [po_], po_[:, cs], vb[:, c, mx * 128:(mx + 1) * 128], am[:, mx, :], start=True, stop=False)
                        S.I("pe", "matmul", [Sbf, qp], [po_], po_[:, cs], Sbf[r0:r0 + 64, :], qp[r0:r0 + 64, cs], start=False, stop=True)
                    pd = pg()
                    S.I("pe", "matmul", [kt_, vb], [pd], pd[0:64, 0:128], kt_[:, 0:64], vb[:, c, 0:128], start=True, stop=True)
                    S.I("pe", "matmul", [kt_, vb], [pd], pd[64:128, 0:128], kt_[:, 64:128], vb[:, c, 128:256], start=True, stop=True)
                    dcol = c * 128 + 127 if direction == 0 else c * 128
                    S.I("dve", "tensor_tensor", [pd, St], [tmpS], tmpS[:], pd[:, 0:128], St[:], ALU.add)
                    S.I("dve", "tensor_scalar", [tmpS, EQ], [St], St[:], tmpS[:], EQ[:, dcol:dcol + 1], None, ALU.mult)
                    S.I("act", "activation", [St], [Sbf], Sbf[:], St[:], AF.Copy)
                t0 = SEQ if isctx else tb * 512
                if isctx and not with_ctx:
                    continue
                for mx in range(2):
                    po_ = psO[mx]
                    o32 = o32s[mx]
                    if direction == 0:
                        S.I("act", "activation", [po_], [OF], OF[:, mx, t0:t0 + n], po_[:, 0:n], AF.Copy, scale=0.125)
                    else:
                        S.I("dve", "scalar_tensor_tensor", [po_, OF], [o32], o32[:, 0:n], po_[:, 0:n], 0.125, OF[:, mx, t0:t0 + n], ALU.mult, ALU.add)

                        def norm_out(mx=mx, n=n, t0=t0):
                            o32, sqo, rso = o32s[mx], sqos[mx], rsos[mx]
                            S.I("act", "activation", [o32], [sqo], sqo[:, 0:n], o32[:, 0:n], AF.Square)
                            p2 = pg()
                            S.I("pe", "matmul", [onesb, sqo], [p2], p2[:, 0:n], onesb[:], sqo[:, 0:n], start=True, stop=True)
                            S.I("act", "activation", [p2], [rso], rso[:, 0:n], p2[:, 0:n], AF.Ln, bias=EPS, scale=1.0 / 128)
                            S.I("act", "activation", [rso], [rso], rso[:, 0:n], rso[:, 0:n], AF.Exp, scale=-0.5)
                            ob = onb[mx]
                            S.I("dve", "scalar_tensor_tensor", [o32, ong, rso], [ob], ob[:, 0:n], o32[:, 0:n], ong[:, mx:mx + 1], rso[:, 0:n], ALU.mult, ALU.mult)
                            oap, obf = o_loc_ap(128 + mx * 128, 256 + mx * 128, t0, n)
                            S.dma(oap, ob[:, 0:n], reads=[ob], writes=[obf])
                        deferred.append(norm_out)
                if direction == 1 and not isctx and tb % 4 == 0:
                    pending_ag.append((tb // 4, bi + 2))
                if direction == 1 and isctx and with_ctx:
                    pending_ag.append((8, bi + 2))
            for fn_ in deferred:
                fn_()
            deferred = []
            for (p_, when) in pending_ag:
                ag_piece(p_)
            pending_ag = []
        P.close()


    def phase_D(l, with_ctx, NH=2):
        P = Phase()
        modT = modTs[l]
        W = TOK // NH
        CW = CTX // NH
        NT = W + (CW if with_ctx else 0)
        arena = [P.tile("arena%d" % i, [128, 8, 512], BF16) for i in range(2)]
        acc = P.tile("acc", [128, 512])
        tmq = [P.tile("tmq%d" % i, [128, 512]) for i in range(2)]
        gates = []
        for t in range(2 if with_ctx else 1):
            gb = P.tile("gate%d" % t, [128, D])
            for cb in range(2):
                dg = tmq[cb]
                for k4 in range(4):
                    k = cb * 4 + k4
                    S.I("dve", "tensor_scalar", [identf, modT], [dg], dg[:, k4 * 128:(k4 + 1) * 128], identf[:],
                        modT[:, 16 + k, t:t + 1], None, ALU.mult)
                pg_ = psa()
                S.I("pe", "matmul", [onesf, dg], [pg_], pg_[:, :], onesf[:], dg[:], start=True, stop=True)
                S.I("act", "activation", [pg_], [gb], gb[:, cb * 512:(cb + 1) * 512], pg_[:, :], AF.Copy)
            gates.append(gb)
        hTd = P.tile("hTd", [128, 8, W], BF16)
        OU = P.tile("OU", [128, 12, NT], BF16)
        Y = P.tile("Y", [128, 8, NT], BF16)
        stgs = [P.tile("stg%d" % i, [128, 4, 512], BF16) for i in range(3)]
        wbrs = [P.tile("wbr%d" % i, [128, 3, 4, 128], BF16) for i in range(2)]
        wgs = [P.tile("wg%d" % i, [128, 3, 8, 128], BF16) for i in range(2)]
        szb = [P.tile("sz%d" % i, [128, 512], BF16) for i in range(2)]
        sgb = [P.tile("sg%d" % i, [128, 512]) for i in range(2)]
        cnt = [0]
        nblk = W // 512

        def load_z(z4):
            a = arena[z4 % 2]
            for k in range(8):
                S.dma(a[:, k, :], w_zg[l][k * 128:(k + 1) * 128, z4 * 512:(z4 + 1) * 512], writes=[a], eng="pool", cw=True)

        def load_oc(oc):
            wbr, wg = wbrs[oc % 2], wgs[oc % 2]
            for br in range(3):
                S.dma(wbr[:, br, :, :], w_br[l, br][:, oc * 128:(oc + 1) * 128].rearrange("(s p) c -> p s c", p=128),
                      writes=[wbr], eng="pool", cw=True)
                gc0 = 1536 + br * 1024 + oc * 128
                S.dma(wg[:, br, :, :], w_zg[l][:, gc0:gc0 + 128].rearrange("(k p) c -> p k c", p=128),
                      writes=[wg], eng="pool", cw=True)

        def load_wo():
            for cb in range(2):
                for k in range(8):
                    S.dma(arena[cb][:, k, :], w_o[l][k * 128:(k + 1) * 128, cb * 512:(cb + 1) * 512], writes=[arena[cb]], eng="pool", cw=True)

        for part in range(NH):
            blocks = []
            for b_ in range(nblk):
                blocks.append((hTd, b_ * 512, b_ * 512, 512))
            if with_ctx:
                blocks.append((hTc, part * CW, W, CW))
            load_z(0)
            load_z(1)
            for b_ in range(nblk):
                gblk = part * nblk + b_
                S.dma(hTd[:, :, b_ * 512:(b_ + 1) * 512], hT_loc[gblk].rearrange("(k p) n -> p k n", p=128),
                      reads=[b_hT_loc[gblk]], writes=[hTd], cw=True)
            for cidx in range(12):
                src_, br = cidx // 3, cidx % 3
                r0 = src_ * 384 + br * 128
                for b_ in range(nblk):
                    cnt[0] += 1
                    stg = stgs[cnt[0] % 3]
                    t0 = part * W + b_ * 512
                    for q in range(4):
                        sap, sbf = o_all_ap(src_, br, q * TOK + t0, 512)
                        S.dma(stg[:, q, :], sap, reads=[sbf], writes=[stg], cw=True)
                    dst = OU[:, cidx, b_ * 512:(b_ + 1) * 512]
                    psel = psa()
                    for q in range(4):
                        S.I("pe", "matmul", [dsel, stg], [psel], psel[:, :], dsel[:, q, :], stg[:, q, :], start=(q == 0), stop=(q == 3))
                    S.I("act", "activation", [psel], [OU], dst, psel[:, :], AF.Copy)
                if with_ctx:
                    sap, sbf = o_all_ap(src_, br, SEQ + part * CW, CW)
                    S.dma(OU[:, cidx, W:W + CW], sap, reads=[sbf], writes=[OU], cw=True)
            for z4 in range(3):
                wz4 = arena[z4 % 2]
                for sub in range(4):
                    zc = z4 * 4 + sub
                    for (hsrc, h0, o0, n) in blocks:
                        ps = psa()
                        proj_fm(ps, wz4, sub * 128, 128, hsrc, h0, n)
                        cnt[0] += 1
                        sz = szb[cnt[0] % 2]
                        S.I("act", "activation", [ps], [sz], sz[:, 0:n], ps[:, 0:n], AF.Silu)
                        S.I("dve", "tensor_tensor", [OU, sz], [OU], OU[:, zc, o0:o0 + n],
                            OU[:, zc, o0:o0 + n], sz[:, 0:n], ALU.mult)
                if z4 == 0:
                    load_z(2)
                    load_oc(0)
            load_oc(1)
            for oc in range(8):
                wbr, wg = wbrs[oc % 2], wgs[oc % 2]
                for (hsrc, h0, o0, n) in blocks:
                    for br in range(3):
                        pB = psa()
                        for s_ in range(4):
                            S.I("pe", "matmul", [wbr, OU], [pB], pB[:, 0:n], wbr[:, br, s_, :], OU[:, s_ * 3 + br, o0:o0 + n],
                                start=(s_ == 0), stop=(s_ == 3))
                        pG = psa()
                        for k in range(8):
                            S.I("pe", "matmul", [wg, hsrc], [pG], pG[:, 0:n], wg[:, br, k, :], hsrc[:, k, h0:h0 + n],
                                start=(k == 0), stop=(k == 7))
                        cnt[0] += 1
                        sg = sgb[cnt[0] % 2]
                        S.I("act", "activation", [pG], [sg], sg[:, 0:n], pG[:, 0:n], AF.Sigmoid)
                        if br == 0:
                            S.I("dve", "tensor_tensor", [pB, sg], [acc], acc[:, 0:n], pB[:, 0:n], sg[:, 0:n], ALU.mult)
                        else:
                            tq_ = tmq[br % 2]
                            S.I("dve", "tensor_tensor", [pB, sg], [tq_], tq_[:, 0:n], pB[:, 0:n], sg[:, 0:n], ALU.mult)
                            if br == 1:
                                S.I("dve", "tensor_tensor", [acc, tq_], [acc], acc[:, 0:n], acc[:, 0:n], tq_[:, 0:n], ALU.add)
                            else:
                                S.I("dve", "tensor_tensor", [acc, tq_], [Y], Y[:, oc, o0:o0 + n], acc[:, 0:n], tq_[:, 0:n], ALU.add)
                if oc == 0:
                    load_wo()
                if oc + 2 < 8:
                    load_oc(oc + 2)
            tiles = []
            for i in range(W // 128):
                tiles.append((xres, xres[:, part * (W // 128) + i, :], 0, 128, i * 128, gates[0]))
            if with_ctx:
                tok0 = part * CW
                tiles.append((xcres, xcres[tok0 % 128:tok0 % 128 + CW, tok0 // 128, :], tok0 % 128, CW, W, gates[1]))
            for (xb_, xap, p0, m, o0, gb) in tiles:
                for cb in range(2):
                    ps = psa()
                    for k in range(8):
                        S.I("pe", "matmul", [Y, arena[cb]], [ps], ps[p0:p0 + m, :], Y[:, k, o0:o0 + m], arena[cb][:, k, :],
                            start=(k == 0), stop=(k == 7))
                    cnt[0] += 1
                    tq_ = tmq[cnt[0] % 2]
                    S.I("dve", "tensor_tensor", [ps, gb], [tq_], tq_[p0:p0 + m, :], ps[p0:p0 + m, :], gb[p0:p0 + m, cb * 512:(cb + 1) * 512], ALU.mult)
                    xs = xap[:, cb * 512:(cb + 1) * 512]
                    S.I("dve", "tensor_tensor", [xb_, tq_], [xb_], xs, xs, tq_[p0:p0 + m, :], ALU.add)
        P.close()

    if stage >= 1:
        phase_mod([0])
    for l in range(depth):
        if stage == 0:
            break
        phase_A(l)
        if l == 0 and depth > 1:
            phase_mod([1])
        if stage == 1:
            break
        phase_N(l, l < DEPTH - 1)
        if stage == 2:
            break
        phase_GR(l, l < DEPTH - 1)
        if stage == 3:
            break
        phase_D(l, l < DEPTH - 1)

    if stage == 1:
        t = nc.dram_tensor("dbg_hT", [4 * D, TOK], BF16, kind="ExternalOutput").ap()
        dbg_out["hT"] = t
        tmp = T("dbgtmp", [128, 32, 512], BF16)
        for q in range(4):
            S.dma(tmp[:], hT_all[q].rearrange("(a p) n -> p a n", p=128), reads=[b_hT_all[q]], writes=[tmp])
            outs.append(S.dma(t[:, q * 512:(q + 1) * 512].rearrange("(a p) n -> p a n", p=128), tmp[:], reads=[tmp]))
        dbg("hTc", hTc, hTc[:], [128, 8, CTX], BF16)
        dbg("modT", modTs[0], modTs[0][:], [128, 24, 2])
    if stage == 4:
        dbg("xc", xcres, xcres[:], [128, 2, D])
    for t in range(16):
        outs.append(S.dma(y_out[t * 128:(t + 1) * 128, :], xres[:, t, :], reads=[xres]))
    S.emit(final_waits=outs)
    return nc, declared


def _rope_tables():
    pos = np.arange(SEQ)
    row = (pos // 64).astype(np.float64)
    col = (pos % 64).astype(np.float64)
    inv = 10000.0 ** (-np.arange(16, dtype=np.float64) / 16)
    cos = np.ones((128, SEQ), np.float64)
    sin = np.zeros((128, SEQ), np.float64)
    for d in range(64):
        half = d // 32
        dd = d % 32
        a = dd % 16
        p = row if half == 0 else col
        ang = p * inv[a]
        cos[64 + d] = np.cos(ang)
        sin[64 + d] = (-1.0 if dd < 16 else 1.0) * np.sin(ang)
    return np.stack([cos, sin]).astype(np.float32)


def _rope_perm():
    perm = np.zeros(64, np.int64)
    for d in range(64):
        base = (d // 32) * 32
        dd = d % 32
        perm[d] = base + (dd + 16 if dd < 16 else dd - 16)
    return perm


def _econst(j):
    lgf = np.log1p(-2.0 ** (-(5.0 + j)))
    lgb = np.log1p(-2.0 ** (-(5.5 + j)))
    i = np.arange(128, dtype=np.float64)
    rows = [np.exp((i + 1) * lgf), np.exp(-(i + 1) * lgf), np.exp((128 - i) * lgb), np.exp(-(128 - i) * lgb)]
    e = np.zeros((4, 128, 512), np.float32)
    for k in range(4):
        e[k, 64:128, :] = np.tile(rows[k], 4)[None, :].astype(np.float32)
    return e


def _na_bias(rpb_h):
    kc = np.arange(64)[:, None]
    qc = np.arange(64)[None, :]
    cs = np.clip(qc - 8, 0, 48)
    colok = (kc >= cs) & (kc <= cs + 15)
    cidx = np.clip(kc - qc + 15, 0, 30)
    strip = np.full((128, 22 * 64), NEG, np.float32)
    for a in range(2):
        for e in range(22):
            dr = a - e + 10
            if -4 <= dr <= 3:
                blk = np.where(colok, rpb_h[dr + 7][cidx], NEG)
                strip[a * 64:(a + 1) * 64, e * 64:(e + 1) * 64] = blk
    edge = np.full((12, 128, 512), NEG, np.float32)
    idx = 0
    for m, kts in ((0, range(2, 8)), (15, range(0, 6))):
        for kt in kts:
            for a in range(2):
                kr = 8 * m - 4 + 2 * kt + a
                for qi in range(8):
                    qr = 8 * m + qi
                    rs = min(max(qr - 4, 0), 120)
                    if rs <= kr <= rs + 7:
                        blk = np.where(colok, rpb_h[kr - qr + 7][cidx], NEG)
                        edge[idx, a * 64:(a + 1) * 64, qi * 64:(qi + 1) * 64] = blk
            idx += 1
    return strip, edge


def make_in_maps(inp):
    f32 = np.float32
    x, c, ctx, c_ctx = inp["x"], inp["c"], inp["ctx"], inp["c_ctx"]
    w_in = inp["w_in"]
    L = DEPTH
    rope = _rope_tables()
    perm = _rope_perm()
    ident = np.eye(128, dtype=f32)
    tri = np.stack([np.triu(np.ones((128, 128), f32)), np.tril(np.ones((128, 128), f32))])
    bones = np.zeros((128, 128), f32)
    bones[:64, :64] = 1
    bones[64:, 64:] = 1
    b_modT = np.ascontiguousarray(inp["b_mod"].reshape(L, 24, 128).transpose(0, 2, 1))
    norm_wT = np.ascontiguousarray(inp["norm_w"].reshape(L, 8, 128).transpose(0, 2, 1))
    w_mod = np.ascontiguousarray(inp["w_mod"])
    w_br = np.ascontiguousarray(inp["w_branch"])
    w_o = np.ascontiguousarray(inp["w_out"])
    zcols = np.concatenate([np.arange(3616 + br * 512 + s * 128, 3616 + br * 512 + s * 128 + 128)
                            for s in range(4) for br in range(3)])
    w_zg = np.ascontiguousarray(np.concatenate([w_in[:, :, zcols], w_in[:, :, 5152:8224]], axis=2))
    maps = []
    for core in range(8):
        b, j = core // 4, core % 4
        cv = np.stack([c[b], c_ctx])
        cTm = np.ascontiguousarray(cv.T.reshape(8, 128, 2).transpose(1, 0, 2))
        sl = lambda o, n: slice(o, o + n)
        w_na = np.concatenate([w_in[:, :, sl(128 * j, 128)], w_in[:, :, sl(512 + 128 * j, 128)],
                               w_in[:, :, sl(1024 + 128 * j, 128)]], axis=2)
        glaq = w_in[:, :, sl(1536 + 64 * j, 64)]
        glak = w_in[:, :, sl(1792 + 64 * j, 64)]
        retq = w_in[:, :, sl(2592 + 64 * j, 64)]
        retk = w_in[:, :, sl(2848 + 64 * j, 64)]
        w_gr = np.concatenate([glaq, retq, glak, retk, glaq, retq[:, :, perm], glak, retk[:, :, perm],
                               w_in[:, :, 2560:2592], w_in[:, :, sl(2048 + 128 * j, 128)],
                               w_in[:, :, sl(3104 + 128 * j, 128)]], axis=2)
        na_g = np.stack([np.tile(inp["na_q_norm"], (1, 2)), np.tile(inp["na_k_norm"], (1, 2))], axis=2)
        strips = np.zeros((L, 2, 128, 22 * 64), f32)
        edges = np.zeros((L, 2, 12, 128, 512), f32)
        for l in range(L):
            for h in range(2):
                strips[l, h], edges[l, h] = _na_bias(inp["na_rpb"][l, 2 * j + h])
        gla_wg = np.ascontiguousarray(inp["gla_w_gate"][:, :, :, 64 * j:64 * j + 64])
        gla_bg = np.ascontiguousarray(inp["gla_b_gate"][:, :, 64 * j:64 * j + 64].transpose(0, 2, 1))
        on_g = np.stack([inp["gla_out_norm"][:, 128 * j:128 * j + 128], inp["ret_out_norm"][:, 128 * j:128 * j + 128]], axis=2)
        selv = np.zeros((128, 4), f32)
        selv[:, j] = 1.0
        maps.append({
            "x_sh": np.ascontiguousarray(x[b, TOK * j:TOK * (j + 1)]),
            "ctx_b": np.ascontiguousarray(ctx[b]),
            "cT": cTm, "w_mod": w_mod, "b_modT": b_modT, "norm_wT": norm_wT,
            "w_na": np.ascontiguousarray(w_na), "w_gr": np.ascontiguousarray(w_gr), "w_zg": w_zg,
            "w_br": w_br, "w_o": w_o, "na_g": np.ascontiguousarray(na_g.astype(f32)),
            "na_strip": strips, "na_edge": edges, "gla_wg": gla_wg, "gla_bg": gla_bg,
            "on_g": np.ascontiguousarray(on_g.astype(f32)), "econst": _econst(j), "rope": rope, "sel": selv,
            "c_ident": ident.astype(ml_dtypes.bfloat16), "c_identf": ident,
            "c_tri": tri.astype(ml_dtypes.bfloat16), "c_bones": bones.astype(ml_dtypes.bfloat16),
        })
    return maps


_CACHE = {}


def kernel(**inputs):
    inp = {k: np.asarray(v) for k, v in inputs.items()}
    maps = make_in_maps(inp)
    if "nc" not in _CACHE:
        _CACHE["nc"] = build_program()[0]
    res = run_bass_kernel_spmd(_CACHE["nc"], maps, core_ids=list(range(8)))
    out = np.zeros((2, SEQ, D), np.float32)
    for core in range(8):
        b, j = core // 4, core % 4
        out[b, TOK * j:TOK * (j + 1)] = res.results[core]["y"]
    return out
```

```python
import os
import contextlib
import numpy as np
import ml_dtypes
import concourse.bass as bass
import concourse.mybir as mybir
from concourse.bass_utils import run_bass_kernel_spmd

F32 = mybir.dt.float32
BF16 = mybir.dt.bfloat16
AF = mybir.ActivationFunctionType
ALU = mybir.AluOpType

COMPUTE = ("pe", "act", "dve", "pool")
ALL_ENG = COMPUTE + ("sp",)

D = 1024
SEQ = 8192
CTX = 256
TOK = 2048
DEPTH = 2
GROUPS = [[0, 1, 2, 3], [4, 5, 6, 7]]
EPS = 1e-6
NEG = -30000.0


class Buf:
    __slots__ = ("name", "t", "last_ws", "readers")

    def __init__(self, name, t=None):
        self.name = name
        self.t = t
        self.last_ws = []
        self.readers = []

    def __getitem__(self, k):
        return self.t[k]


class Op:
    __slots__ = ("eng", "fn", "deps", "is_dma", "sem", "semval", "signal", "inc", "is_cc")

    def __init__(self, eng, fn, is_dma):
        self.eng = eng
        self.fn = fn
        self.deps = []
        self.is_dma = is_dma
        self.sem = None
        self.semval = None
        self.signal = False
        self.inc = 1
        self.is_cc = False


class Sched:
    def __init__(self, nc, n_dma_sems=32):
        self.nc = nc
        self.ops = {e: [] for e in ALL_ENG}
        self.n_dma_sems = n_dma_sems
        self.dma_rr = 0
        self.dma_last = [None] * n_dma_sems
        self.dma_count = [0] * n_dma_sems
        self.bar_deps = {}
        self.n_ops = 0

    def op(self, eng, fn, reads=(), writes=(), dma=False, inc=16, cw=False):
        o = Op(eng, fn, dma)
        deps = []
        for b in reads:
            deps.extend(b.last_ws)
        for b in writes:
            if not (cw and not b.readers):
                deps.extend(b.last_ws)
            deps.extend(b.readers)
        if eng in self.bar_deps:
            deps.extend(self.bar_deps.pop(eng))
        if dma:
            k = self.dma_rr
            self.dma_rr = (self.dma_rr + 1) % self.n_dma_sems
            prev = self.dma_last[k]
            if prev is not None:
                deps.append(prev)
            self.dma_last[k] = o
            self.dma_count[k] += inc
            o.sem = ("dma", k)
            o.semval = self.dma_count[k]
            o.inc = inc
            o.signal = True
        seen = set()
        for d in deps:
            if d is o or id(d) in seen:
                continue
            seen.add(id(d))
            if (not d.is_dma) and d.eng == eng:
                if eng == "pe" or eng == "sp":
                    continue
                if not any((d in b.last_ws) for b in reads):
                    continue
            o.deps.append(d)
        for b in reads:
            b.readers.append(o)
        for b in writes:
            if cw and not b.readers:
                b.last_ws.append(o)
            else:
                b.last_ws = [o]
            b.readers = []
        self.ops[eng].append(o)
        self.n_ops += 1
        return o

    def I(self, eng, name, reads, writes, *a, **kw):
        return self.op(eng, lambda e: getattr(e, name)(*a, **kw), reads, writes)

    def dma(self, out_ap, in_ap, reads=(), writes=(), eng="sp", cw=False, **kw):
        return self.op(eng, lambda e: e.dma_start(out=out_ap, in_=in_ap, **kw), reads, writes, dma=True, cw=cw)

    def barrier(self):
        last = []
        for e in ALL_ENG:
            for o in reversed(self.ops[e]):
                if not o.is_dma:
                    last.append(o)
                    break
        last.extend(o for o in self.dma_last if o is not None and not o.is_cc)
        for e in ALL_ENG:
            self.bar_deps[e] = list(last) + self.bar_deps.get(e, [])

    def emit(self, final_waits=()):
        nc = self.nc
        for e in ALL_ENG:
            for o in self.ops[e]:
                for d in o.deps:
                    if not d.is_dma:
                        d.signal = True
        for e in ALL_ENG:
            c = 0
            for o in self.ops[e]:
                if o.is_dma:
                    continue
                if o.signal:
                    c += 1
                    o.sem = ("eng", e)
                    o.semval = c
        sems = {}
        with contextlib.ExitStack() as st:
            for e in ALL_ENG:
                sems[("eng", e)] = st.enter_context(nc.semaphore("s_" + e))
            for k in range(self.n_dma_sems):
                sems[("dma", k)] = st.enter_context(nc.semaphore("s_dma%d" % k))
            block = st.enter_context(nc.Block())
            handles = {"pe": block.tensor, "act": block.scalar, "dve": block.vector,
                       "pool": block.gpsimd, "sp": block.sync}

            def make(e):
                def body(eng):
                    known = {}
                    for o in self.ops[e]:
                        need = {}
                        for d in o.deps:
                            if known.get(d.sem, 0) >= d.semval:
                                continue
                            if need.get(d.sem, 0) < d.semval:
                                need[d.sem] = d.semval
                        for s, v in need.items():
                            eng.wait_ge(sems[s], v)
                            known[s] = v
                        ins = o.fn(eng)
                        if o.signal:
                            ins.then_inc(sems[o.sem], o.inc if o.is_dma else 1)
                    if e == "sp":
                        need = {}
                        for d in final_waits:
                            if need.get(d.sem, 0) < d.semval:
                                need[d.sem] = d.semval
                        for s, v in need.items():
                            if known.get(s, 0) < v:
                                eng.wait_ge(sems[s], v)
                return body

            for e in ALL_ENG:
                handles[e](make(e))


def build_program(stage=99, depth=DEPTH):
    nc = bass.Bass("TRN2", target_bir_lowering=False)
    S = Sched(nc)
    L = DEPTH

    declared = []
    need = {"x_sh": 0, "ctx_b": 0, "cT": 0, "w_mod": 1, "b_modT": 1, "norm_wT": 1, "w_na": 2, "w_gr": 3, "w_zg": 4,
            "w_br": 4, "w_o": 4, "na_g": 2, "na_strip": 2, "na_edge": 2, "gla_wg": 3, "gla_bg": 3, "on_g": 3,
            "econst": 3, "rope": 3, "sel": 0, "c_ident": 0, "c_identf": 0, "c_tri": 0, "c_bones": 0}

    def din(name, shape, dt=F32):
        if stage < need[name]:
            return None
        declared.append(name)
        return nc.dram_tensor(name, list(shape), dt, kind="ExternalInput").ap()

    x_sh = din("x_sh", [TOK, D])
    ctx_b = din("ctx_b", [CTX, D])
    cT = din("cT", [128, 8, 2])
    w_mod = din("w_mod", [L, D, 3 * D])
    b_modT = din("b_modT", [L, 128, 24])
    norm_wT = din("norm_wT", [L, 128, 8])
    w_na = din("w_na", [L, D, 384])
    w_gr = din("w_gr", [L, D, 800])
    w_zg = din("w_zg", [L, D, 4608])
    w_br = din("w_br", [L, 3, 512, D])
    w_o = din("w_o", [L, D, D])
    na_g = din("na_g", [L, 128, 2])
    na_strip = din("na_strip", [L, 2, 128, 22 * 64])
    na_edge = din("na_edge", [L, 2, 12, 128, 512])
    gla_wg = din("gla_wg", [L, 2, 16, 64])
    gla_bg = din("gla_bg", [L, 64, 2])
    on_g = din("on_g", [L, 128, 2])
    econst = din("econst", [4, 128, 512])
    rope = din("rope", [2, 128, SEQ])
    sel_in = din("sel", [128, 4])
    c_ident = din("c_ident", [128, 128], BF16)
    c_identf = din("c_identf", [128, 128])
    c_tri = din("c_tri", [2, 128, 128], BF16)
    c_bones = din("c_bones", [128, 128], BF16)
    y_out = nc.dram_tensor("y", [TOK, D], F32, kind="ExternalOutput").ap()
    dbg_out = {}

    hT_loc = [nc.dram_tensor("hT_loc%d" % i, [D, 512], BF16).ap() for i in range(4)]
    hT_all = [nc.dram_tensor("hT_all%d" % i, [4 * D, 512], BF16).ap() for i in range(4)]
    b_hT_loc = [Buf("hT_loc%d" % i) for i in range(4)]
    b_hT_all = [Buf("hT_all%d" % i) for i in range(4)]
    OW = [1024] * 8 + [CTX]
    o_loc = [nc.dram_tensor("o_loc%d" % i, [384, OW[i]], BF16).ap() for i in range(9)]
    o_all = [nc.dram_tensor("o_all%d" % i, [4 * 384, OW[i]], BF16).ap() for i in range(9)]
    b_o_loc = [Buf("o_loc%d" % i) for i in range(9)]
    b_o_all = [Buf("o_all%d" % i) for i in range(9)]

    def o_loc_ap(r0, r1, t0, n):
        if t0 >= SEQ:
            return o_loc[8][r0:r1, t0 - SEQ:t0 - SEQ + n], b_o_loc[8]
        p = t0 // 1024
        return o_loc[p][r0:r1, t0 % 1024:t0 % 1024 + n], b_o_loc[p]

    NTT = SEQ + CTX
    qk_scr = nc.dram_tensor("qk_scr", [2, 128, NTT], F32).ap()
    v_scr = nc.dram_tensor("v_scr", [NTT, 256], BF16).ap()
    gb_scr = nc.dram_tensor("gb_scr", [16, NTT], BF16).ap()
    b_scr = {}

    def scr_buf(kind, tb):
        return b_scr.setdefault((kind, tb), Buf("scr_%s_%d" % (kind, tb)))

    def allgather(src, dst, bsrc, bdst):
        o_ = S.op("pool", lambda e: e.collective_compute("AllGather", ALU.bypass, replica_groups=GROUPS,
                                                          ins=[src], outs=[dst]),
                  reads=[bsrc], writes=[bdst], dma=True, inc=1)
        o_.is_cc = True

    uid = [0]

    def T(name, shape, dt=F32):
        uid[0] += 1
        return Buf(name, nc.alloc_sbuf_tensor("%s_%d" % (name, uid[0]), list(shape), dt))

    class Phase:
        def __init__(self):
            self.st = contextlib.ExitStack()

        def tile(self, name, shape, dt=F32):
            uid[0] += 1
            return Buf(name, self.st.enter_context(nc.sbuf_tensor("%s_%d" % (name, uid[0]), list(shape), dt)))

        def close(self):
            S.barrier()
            self.st.close()

    PSA = [Buf("psa%d" % i, nc.alloc_psum_tensor("psa%d" % i, [128, 512], F32)) for i in range(4)]
    PS2 = Buf("ps2", nc.alloc_psum_tensor("ps2", [128, 1024], F32))
    PSB = [Buf("psb%d" % i, nc.alloc_psum_tensor("psb%d" % i, [128, 1024], BF16)) for i in range(2)]
    rr = {"a": 0, "b": 0}

    def psa():
        rr["a"] = (rr["a"] + 1) % 4
        return PSA[rr["a"]]

    def psb():
        rr["b"] = (rr["b"] + 1) % 2
        return PSB[rr["b"]]

    xres = T("xres", [128, 16, D])
    xcres = T("xcres", [128, 2, D])
    hTc = T("hTc", [128, 8, CTX], BF16)
    ident = T("ident", [128, 128], BF16)
    identf = T("identf", [128, 128])
    onesf = T("onesf", [128, 128])
    tri = T("tri", [128, 2, 128], BF16)
    bones = T("bones", [128, 128], BF16)
    sel = T("sel", [128, 4])
    cTs = T("cTs", [128, 8, 2])
    modTs = [T("modT%d" % i, [128, 24, 2]) for i in range(DEPTH)]
    weffs = [T("weff%d" % i, [128, 8, 2]) for i in range(DEPTH)]
    small = T("small", [128, 64])

    for t in range(16):
        S.dma(xres[:, t, :], x_sh[t * 128:(t + 1) * 128, :], writes=[xres], cw=True)
    S.dma(xcres[:], ctx_b.rearrange("(t p) d -> p t d", p=128), writes=[xcres])
    S.dma(ident[:], c_ident, writes=[ident])
    S.dma(identf[:], c_identf, writes=[identf])
    S.dma(tri[:], c_tri.rearrange("a p n -> p a n"), writes=[tri])
    S.dma(bones[:], c_bones, writes=[bones])
    S.dma(sel[:], sel_in, writes=[sel])
    S.dma(cTs[:], cT, writes=[cTs])
    S.I("dve", "memset", [], [onesf], onesf[:], 1.0)
    dsel = T("dsel", [128, 4, 128], BF16)
    for q_ in range(4):
        S.I("dve", "tensor_scalar", [ident, sel], [dsel], dsel[:, q_, :], ident[:], sel[:, q_:q_ + 1], None, ALU.mult)

    outs = []

    def dbg(name, buf, ap, shape, dt=F32):
        t = nc.dram_tensor("dbg_" + name, list(shape), dt, kind="ExternalOutput").ap()
        dbg_out[name] = t
        outs.append(S.dma(t, ap, reads=[buf]))

    def phase_mod(layers):
        P = Phase()
        sT = P.tile("sT", [128, 8, 2])
        wmb = [P.tile("wm%d" % i, [128, 8, 512]) for i in range(3)]
        S.I("act", "activation", [cTs], [sT], sT[:], cTs[:], AF.Silu)
        it = 0
        for l in layers:
            modT, weff = modTs[l], weffs[l]
            bmt = P.tile("bmt", [128, 24])
            nwt = P.tile("nwt", [128, 8])
            S.dma(bmt[:], b_modT[l], writes=[bmt])
            S.dma(nwt[:], norm_wT[l], writes=[nwt])
            psm = psa()
            for cc in range(6):
                wm = wmb[it % 3]
                it += 1
                S.dma(wm[:], w_mod[l][:, cc * 512:(cc + 1) * 512].rearrange("(k p) c -> p k c", p=128), writes=[wm])
                for sub in range(4):
                    c24 = cc * 4 + sub
                    for k in range(8):
                        S.I("pe", "matmul", [wm, sT], [psm], psm[:, c24 * 2:c24 * 2 + 2],
                            wm[:, k, sub * 128:(sub + 1) * 128], sT[:, k, :], start=(k == 0), stop=(k == 7))
            S.I("dve", "tensor_tensor", [psm, bmt], [modT], modT[:],
                psm[:, 0:48].rearrange("p (c t) -> p c t", t=2), bmt[:].unsqueeze(2).to_broadcast([128, 24, 2]), ALU.add)
            S.I("dve", "scalar_tensor_tensor", [modT, nwt], [weff], weff[:], modT[:, 8:16, :], 1.0,
                nwt[:].unsqueeze(2).to_broadcast([128, 8, 2]), ALU.add, ALU.mult)
        P.close()

    def phase_A(l):
        P = Phase()
        modT, weff = modTs[l], weffs[l]
        SUB = int(os.environ.get("KSUB", "9"))
        if SUB < 2:
            P.close()
            return
        ss = P.tile("ss", [128, 18])
        rstd = P.tile("rstd", [128, 18])
        junk = P.tile("junk", [128, D], BF16)
        xnb = [P.tile("xn%d" % i, [128, D], BF16) for i in range(2)]
        stg = [P.tile("stg%d" % i, [128, 8, 512], BF16) for i in range(2)]
        for tt in range(18):
            isctx = tt >= 16
            xt = xcres[:, tt - 16, :] if isctx else xres[:, tt, :]
            xb_ = xcres if isctx else xres
            tsel = 1 if isctx else 0
            S.I("act", "activation", [xb_], [junk, ss], junk[:], xt, AF.Square, accum_out=ss[:, tt:tt + 1])
            S.I("act", "activation", [ss], [rstd], rstd[:, tt:tt + 1], ss[:, tt:tt + 1], AF.Ln, bias=EPS, scale=1.0 / D)
            S.I("act", "activation", [rstd], [rstd], rstd[:, tt:tt + 1], rstd[:, tt:tt + 1], AF.Exp, scale=-0.5)
            xn = xnb[tt % 2]
            S.I("dve", "tensor_scalar", [xb_, rstd], [xn], xn[:], xt, rstd[:, tt:tt + 1], None, ALU.mult)
            pt = psb()
            for k in range(8):
                S.I("pe", "transpose", [xn, ident], [pt], pt[:, k * 128:(k + 1) * 128], xn[:, k * 128:(k + 1) * 128], ident[:])
            if isctx:
                dst, dbuf, c0 = hTc, hTc, (tt - 16) * 128
            else:
                dbuf = stg[(tt // 4) % 2]
                dst, c0 = dbuf, (tt % 4) * 128
            for k in range(8):
                eng = "act" if k % 2 == 0 else "dve"
                if eng == "act":
                    S.I("act", "activation", [pt, weff, modT], [dbuf], dst[:, k, c0:c0 + 128], pt[:, k * 128:(k + 1) * 128],
                        AF.Identity, bias=modT[:, k, tsel:tsel + 1], scale=weff[:, k, tsel:tsel + 1])
                else:
                    S.I("dve", "tensor_scalar", [pt, weff, modT], [dbuf], dst[:, k, c0:c0 + 128], pt[:, k * 128:(k + 1) * 128],
                        weff[:, k, tsel:tsel + 1], modT[:, k, tsel:tsel + 1], ALU.mult, ALU.add)
            if (not isctx) and tt % 4 == 3:
                blk = tt // 4
                S.dma(hT_loc[blk].rearrange("(k p) n -> p k n", p=128), dbuf[:],
                      reads=[dbuf], writes=[b_hT_loc[blk]])
                allgather(hT_loc[blk], hT_all[blk], b_hT_loc[blk], b_hT_all[blk])
        P.close()

    def load_hb(hb, tb):
        r, blk = tb // 4, tb % 4
        S.dma(hb[:], hT_all[blk][r * D:(r + 1) * D, :].rearrange("(k p) n -> p k n", p=128),
              reads=[b_hT_all[blk]], writes=[hb])

    def proj_fm(ps, w, c0, m, hsrc, n0, n, m0=0):
        for k in range(8):
            S.I("pe", "matmul", [w, hsrc], [ps], ps[m0:m0 + m, 0:n], w[:, k, c0:c0 + m], hsrc[:, k, n0:n0 + n],
                start=(k == 0), stop=(k == 7))

    def phase_N(l, with_ctx):
        P = Phase()
        wna = P.tile("wna", [128, 8, 384], BF16)
        S.dma(wna[:], w_na[l].rearrange("(k p) c -> p k c", p=128), writes=[wna], eng="pool")
        nag = P.tile("nag", [128, 2])
        S.dma(nag[:], na_g[l], writes=[nag])
        QT = P.tile("QT", [128, SEQ], BF16)
        KT = P.tile("KT", [128, SEQ], BF16)
        VA = P.tile("VA", [128, 64, 192], BF16)
        QTc = P.tile("QTc", [128, CTX], BF16)
        KTc = P.tile("KTc", [128, CTX], BF16)
        VAc = P.tile("VAc", [128, 2, 192], BF16)
        S.I("pool", "memset", [], [VA], VA[:, :, 64:128], 1.0)
        S.I("pool", "memset", [], [VAc], VAc[:, :, 64:128], 1.0)
        hbs = [P.tile("hb%d" % i, [128, 8, 512], BF16) for i in range(2)]
        sq = P.tile("sq", [128, 512], BF16)
        rs = P.tile("rs", [128, 512])

        def qk_norm(ps, n, dst_buf, dst_ap, gcol):
            S.I("act", "activation", [ps], [sq], sq[:, 0:n], ps[:, 0:n], AF.Square)
            p2 = psa()
            S.I("pe", "matmul", [bones, sq], [p2], p2[:, 0:n], bones[:], sq[:, 0:n], start=True, stop=True)
            S.I("act", "activation", [p2], [rs], rs[:, 0:n], p2[:, 0:n], AF.Ln, bias=EPS, scale=1.0 / 64)
            S.I("act", "activation", [rs], [rs], rs[:, 0:n], rs[:, 0:n], AF.Exp, scale=-0.5)
            S.I("dve", "scalar_tensor_tensor", [ps, nag, rs], [dst_buf], dst_ap, ps[:, 0:n], nag[:, gcol:gcol + 1],
                rs[:, 0:n], ALU.mult, ALU.mult)

        def project(hsrc, n, qdst, kdst, vdst, vt0, qb, kb, vb):
            pq = psa()
            proj_fm(pq, wna, 0, 128, hsrc, 0, n)
            qk_norm(pq, n, qb, qdst, 0)
            pk = psa()
            proj_fm(pk, wna, 128, 128, hsrc, 0, n)
            qk_norm(pk, n, kb, kdst, 1)
            pv = psa()
            nt = n // 128
            for s in range(nt):
                for k in range(8):
                    S.I("pe", "matmul", [hsrc, wna], [pv], pv[:, s * 128:(s + 1) * 128], hsrc[:, k, s * 128:(s + 1) * 128],
                        wna[:, k, 256:384], start=(k == 0), stop=(k == 7))
            pv3 = pv[:, 0:n].rearrange("p (s c) -> p s c", c=128)
            S.I("act", "activation", [pv], [vb], vdst[:, vt0:vt0 + nt, 0:64], pv3[:, :, 0:64], AF.Copy)
            S.I("dve", "tensor_copy", [pv], [vb], vdst[:, vt0:vt0 + nt, 128:192], pv3[:, :, 64:128])

        project(hTc, CTX, QTc[:], KTc[:], VAc, 0, QTc, KTc, VAc)
        for it, tb in enumerate([r_ * 4 + blk_ for blk_ in range(4) for r_ in range(4)]):
            hb = hbs[it % 2]
            load_hb(hb, tb)
            project(hb, 512, QT[:, tb * 512:(tb + 1) * 512], KT[:, tb * 512:(tb + 1) * 512], VA, tb * 4, QT, KT, VA)

        if stage == 2:
            dbg("QT", QT, QT[:], [128, SEQ], BF16)
            dbg("KT", KT, KT[:], [128, SEQ], BF16)

        msk = P.tile("msk", [128, 22 * 64], BF16)
        edg = P.tile("edg", [128, 12, 512], BF16)
        bst = [P.tile("bst%d" % i, [128, 704]) for i in range(2)]
        pts = [P.tile("pt%d" % i, [128, 512], BF16) for i in range(4)]
        ona = [P.tile("ona%d" % i, [128, 512], BF16) for i in range(2)]
        rden = P.tile("rden", [128, 512])
        lnd = P.tile("lnd", [128, 512])
        ptc = [0]

        def va_lhsT(vbuf, tile_i, h):
            return vbuf[:, tile_i, 0:128] if h == 0 else vbuf[:, tile_i, 64:192]

        sbanks = [PSA[2], PSA[3], Buf("ps2a", PS2.t[:, 0:512]), Buf("ps2b", PS2.t[:, 512:1024]),
                  Buf("psbf0", PSB[0].t[:].bitcast(F32)), Buf("psbf1", PSB[1].t[:].bitcast(F32))]
        pobanks = [PSA[0], PSA[1]]
        pts = pts + [P.tile("ptx%d" % i, [128, 512], BF16) for i in range(2)]
        LAG = 3
        items = []
        for h in range(2):
            hp = h * 64
            for m in range(16):
                qcols = slice(m * 512, (m + 1) * 512)
                keys = []
                for kt in range(8):
                    kr0 = 8 * m - 4 + 2 * kt
                    if kr0 < 0 or kr0 >= 128:
                        continue
                    if m == 0:
                        mk, mb = edg[:, kt - 2, :], edg
                    elif m == 15:
                        mk, mb = edg[:, 6 + kt, :], edg
                    else:
                        e0 = 14 - 2 * kt
                        mk, mb = msk[:, e0 * 64:(e0 + 8) * 64], msk
                    keys.append((KT, KT[hp:hp + 64, kr0 * 64:kr0 * 64 + 128], VA, kr0 // 2, mk, mb))
                for ct in range(2):
                    keys.append((KTc, KTc[hp:hp + 64, ct * 128:(ct + 1) * 128], VAc, ct, None, None))
                for i, kk in enumerate(keys):
                    items.append((h, m, i, len(keys)) + kk + (QT, QT[hp:hp + 64, qcols], 512, m * 512))
            if with_ctx:
                for ct in range(2):
                    items.append((h, 16, ct, 2, KTc, KTc[hp:hp + 64, ct * 128:(ct + 1) * 128], VAc, ct, None, None,
                                  QTc, QTc[hp:hp + 64, :], CTX, SEQ))
        cur_h = [-1]
        pend = {}
        for idx in range(len(items) + LAG):
            if idx < len(items):
                (h, m, i, nk, kbuf, kap, vbuf, vt, mk, mb, qbuf, qap, n, t0) = items[idx]
                if h != cur_h[0]:
                    cur_h[0] = h
                    for i2 in range(2):
                        b_ = bst[i2]
                        S.dma(b_[:], na_strip[l, h][:, i2 * 704:(i2 + 1) * 704], writes=[b_])
                        S.I("act", "activation", [b_], [msk], msk[:, i2 * 704:(i2 + 1) * 704], b_[:], AF.Exp)
                    for i2 in range(12):
                        b_ = bst[i2 % 2]
                        S.dma(b_[:, 0:512], na_edge[l, h, i2], writes=[b_])
                        S.I("act", "activation", [b_], [edg], edg[:, i2, :], b_[:, 0:512], AF.Exp)
                ps_ = sbanks[idx % 6]
                pt_ = pts[idx % 6]
                S.I("pe", "matmul", [kbuf, qbuf], [ps_], ps_[:, 0:n], kap, qap, start=True, stop=True)
                S.I("act", "activation", [ps_], [pt_], pt_[:, 0:n], ps_[:, 0:n], AF.Exp, scale=0.125)
                if mk is not None:
                    S.I("dve" if idx % 2 == 0 else "pool", "tensor_tensor", [pt_, mb], [pt_], pt_[:], pt_[:], mk, ALU.mult)
                pend[idx] = pt_
            j2 = idx - LAG
            if j2 >= 0:
                (h, m, i, nk, kbuf, kap, vbuf, vt, mk, mb, qbuf, qap, n, t0) = items[j2]
                hp = h * 64
                pt_ = pend.pop(j2)
                po = pobanks[(h * 17 + m) % 2]
                S.I("pe", "matmul", [vbuf, pt_], [po], po[:, 0:n], va_lhsT(vbuf, vt, h), pt_[:, 0:n],
                    start=(i == 0), stop=(i == nk - 1))
                if i == nk - 1:
                    ob = ona[(h * 17 + m) % 2]
                    dp = 64 - hp
                    S.I("act", "activation", [po], [lnd], lnd[dp:dp + 64, 0:n], po[dp:dp + 64, 0:n], AF.Ln)
                    S.I("act", "activation", [lnd], [lnd], lnd[dp:dp + 64, 0:n], lnd[dp:dp + 64, 0:n], AF.Exp, scale=-1.0)
                    S.I("dve", "tensor_copy", [lnd], [rden], rden[hp:hp + 64, 0:n], lnd[dp:dp + 64, 0:n])
                    S.I("dve", "tensor_tensor", [po, rden], [ob], ob[hp:hp + 64, 0:n], po[hp:hp + 64, 0:n],
                        rden[hp:hp + 64, 0:n], ALU.mult)
                    oap, obf = o_loc_ap(hp, hp + 64, t0, n)
                    S.dma(oap, ob[hp:hp + 64, 0:n], reads=[ob], writes=[obf])
        P.close()


    def phase_GR(l, with_ctx):
        P = Phase()
        wgr = P.tile("wgr", [128, 8, 800], BF16)
        S.dma(wgr[:], w_gr[l].rearrange("(k p) c -> p k c", p=128), writes=[wgr], eng="pool")
        wgt = P.tile("wgt", [16, 2, 64], BF16)
        S.dma(wgt[:], gla_wg[l].rearrange("a k c -> k a c"), writes=[wgt], eng="pool")
        nb = P.tile("nb", [64, 2])
        S.dma(nb[:], gla_bg[l], writes=[nb])
        S.I("act", "mul", [nb], [nb], nb[:], nb[:], -1.0)
        ong = P.tile("ong", [128, 2])
        S.dma(ong[:], on_g[l], writes=[ong])
        onesb = P.tile("onesb", [128, 128], BF16)
        S.I("dve", "memset", [], [onesb], onesb[:], 1.0)
        OF = P.tile("OF", [128, 2, SEQ + CTX], BF16)
        St = P.tile("St", [128, 128])
        Sbf = P.tile("Sbf", [128, 128], BF16)
        tmpS = P.tile("tmpS", [128, 128])
        ET = [[P.tile("E%d%d" % (k_, p_), [128, 512]) for p_ in range(2)] for k_ in range(2)]
        hbs = [P.tile("hb%d" % i, [128, 8, 512], BF16) for i in range(2)]
        cosb = [P.tile("cos%d" % i, [128, 512]) for i in range(2)]
        sinb = [P.tile("sin%d" % i, [128, 512]) for i in range(2)]
        t1 = P.tile("t1", [128, 512])
        t2 = P.tile("t2", [128, 512])
        t3 = P.tile("t3", [128, 512])
        t4 = P.tile("t4", [128, 512])
        qa = P.tile("qa", [128, 512])
        ka = P.tile("ka", [128, 512])
        QP = [P.tile("QP%d" % i, [128, 512], BF16) for i in range(2)]
        KP = [P.tile("KP%d" % i, [128, 512], BF16) for i in range(2)]
        VB = [P.tile("VB%d" % i, [128, 4, 256], BF16) for i in range(2)]
        gT = P.tile("gT", [16, 512], BF16)
        gTl = [P.tile("gTl%d" % i, [16, 512], BF16) for i in range(2)]
        e1 = P.tile("e1", [64, 512])
        nl = P.tile("nl", [64, 512])
        cum = P.tile("cum", [64, 512])
        rn = P.tile("rn", [64, 512])
        onesc = P.tile("onesc", [64, 128])
        S.I("dve", "memset", [], [onesc], onesc[:], 1.0)
        Am = [P.tile("Am%d" % i, [128, 2, 128], BF16) for i in range(2)]
        ktok = [P.tile("ktok%d" % i, [128, 128], BF16) for i in range(2)]
        o32s = [P.tile("o32_%d" % i, [128, 512]) for i in range(2)]
        sqos = [P.tile("sqo%d" % i, [128, 512], BF16) for i in range(2)]
        rsos = [P.tile("rso%d" % i, [128, 512]) for i in range(2)]
        onb = [P.tile("onb%d" % i, [128, 512], BF16) for i in range(2)]
        psO = [PSA[0], PSA[1]]
        PSBf = Buf("psbf", None)
        gen = [PSA[2], PSA[3]]
        gi = [0]

        def pg():
            gi[0] = (gi[0] + 1) % 2
            return gen[gi[0]]

        cnt = [0]
        pending_ag = []
        deferred = []
        for direction in range(2):
            if direction == 0:
                blocks = [-1] + list(range(16))
            else:
                blocks = [-1] + list(range(15, -1, -1))
            S.I("dve", "memset", [], [St], St[:], 0.0)
            S.I("dve", "memset", [], [Sbf], Sbf[:], 0.0)
            for k_ in range(2):
                for p_ in range(2):
                    S.dma(ET[k_][p_][:], econst[direction * 2 + k_], writes=[ET[k_][p_]])
            def issue_loads(bi2):
                tb2 = blocks[bi2]
                par2 = bi2 % 2
                if direction == 1:
                    n2 = CTX if tb2 < 0 else 512
                    t02 = SEQ if tb2 < 0 else tb2 * 512
                    S.dma(cosb[par2][:, 0:n2], qk_scr[0][:, t02:t02 + n2], reads=[scr_buf("q", tb2)], writes=[cosb[par2]])
                    S.dma(sinb[par2][:, 0:n2], qk_scr[1][:, t02:t02 + n2], reads=[scr_buf("k", tb2)], writes=[sinb[par2]])
                    S.dma(gTl[par2][:, 0:n2], gb_scr[:, t02:t02 + n2], reads=[scr_buf("g", tb2)], writes=[gTl[par2]])
                    return
                if tb2 < 0:
                    return
                load_hb(hbs[par2], tb2)
                S.dma(cosb[par2][:], rope[0][:, tb2 * 512:(tb2 + 1) * 512], writes=[cosb[par2]])
                S.dma(sinb[par2][:], rope[1][:, tb2 * 512:(tb2 + 1) * 512], writes=[sinb[par2]])

            def issue_v(bi2):
                tb2 = blocks[bi2]
                par2 = bi2 % 2
                n2 = CTX if tb2 < 0 else 512
                t02 = SEQ if tb2 < 0 else tb2 * 512
                S.dma(VB[par2][:, 0:n2 // 128, :], v_scr[t02:t02 + n2, :].rearrange("(t p) c -> p t c", p=128),
                      reads=[scr_buf("v", tb2)], writes=[VB[par2]])

            def prologue(bi):
                tb = blocks[bi]
                isctx = tb < 0
                n = CTX if isctx else 512
                nch = n // 128
                par = bi % 2
                hsrc = hTc if isctx else hbs[par]
                EQ, EK = ET[0][par], ET[1][par]
                t0s = SEQ if isctx else tb * 512
                if direction == 0:
                    pG = pg()
                    proj_fm(pG, wgr, 512, 16, hsrc, 0, n)
                    S.I("act", "activation", [pG], [gT], gT[:, 0:n], pG[0:16, 0:n], AF.Copy)
                    gsrc = gT
                    pGb = pg()
                    proj_fm(pGb, wgr, 528, 16, hsrc, 0, n)
                    S.I("act", "activation", [pGb], [gTl[par]], gTl[par][:, 0:n], pGb[0:16, 0:n], AF.Copy)
                    S.dma(gb_scr[:, t0s:t0s + n], gTl[par][:, 0:n], reads=[gTl[par]], writes=[scr_buf("g", tb)])
                else:
                    gsrc = gTl[par]
                pL = pg()
                S.I("pe", "matmul", [wgt, gsrc], [pL], pL[0:64, 0:n], wgt[:, direction, :], gsrc[:, 0:n], start=True, stop=True)
                S.I("act", "activation", [pL, nb], [e1], e1[:, 0:n], pL[0:64, 0:n], AF.Exp, bias=nb[:, direction:direction + 1], scale=-1.0)
                S.I("act", "activation", [e1], [nl], nl[:, 0:n], e1[:, 0:n], AF.Ln, bias=1.0)
                for c in range(nch):
                    cs = slice(c * 128, (c + 1) * 128)
                    S.I("dve", "tensor_tensor_scan", [onesc, nl], [cum], cum[:, cs], onesc[:], nl[:, cs], 0.0, ALU.mult, ALU.add)
                if direction == 0:
                    src_, sb_ = cum, cum
                else:
                    S.I("dve", "tensor_tensor", [nl, cum], [rn], rn[:, 0:n], nl[:, 0:n], cum[:, 0:n], ALU.subtract)
                    for c in range(nch):
                        cs = slice(c * 128, (c + 1) * 128)
                        S.I("dve", "tensor_scalar", [rn, cum], [rn], rn[:, cs], rn[:, cs], cum[:, c * 128 + 127:c * 128 + 128], None, ALU.add)
                    src_, sb_ = rn, rn
                S.I("act", "activation", [sb_], [EQ], EQ[0:64, 0:n], src_[:, 0:n], AF.Exp, scale=-1.0 / 16)
                S.I("act", "activation", [sb_], [EK], EK[0:64, 0:n], src_[:, 0:n], AF.Exp, scale=1.0 / 16)
                qp, kp = QP[par], KP[par]
                for qi_, (c0, c0p, dst, E_, ta, tb_, acc) in enumerate(((0, 256, qp, EQ, t1, t2, qa), (128, 384, kp, EK, t3, t4, ka))):
                    if direction == 1:
                        stag = cosb[par] if qi_ == 0 else sinb[par]
                        S.I("dve", "tensor_tensor", [stag, E_], [dst], dst[:, 0:n], stag[:, 0:n], E_[:, 0:n], ALU.mult)
                        continue
                    pq = pg()
                    proj_fm(pq, wgr, c0, 128, hsrc, 0, n)
                    if isctx:
                        S.I("dve", "tensor_copy", [pq], [acc], acc[:, 0:n], pq[:, 0:n])
                        S.dma(qk_scr[qi_][:, t0s:t0s + n], acc[:, 0:n], reads=[acc], writes=[scr_buf("qk"[qi_], tb)])
                        S.I("dve", "tensor_tensor", [acc, E_], [dst], dst[:, 0:n], acc[:, 0:n], E_[:, 0:n], ALU.mult)
                    else:
                        S.I("dve", "tensor_tensor", [pq, cosb[par]], [ta], ta[:], pq[:, :], cosb[par][:], ALU.mult)
                        pqp = pg()
                        proj_fm(pqp, wgr, c0p, 128, hsrc, 0, n)
                        S.I("dve", "tensor_tensor", [pqp, sinb[par]], [tb_], tb_[:], pqp[:, :], sinb[par][:], ALU.mult)
                        S.I("dve", "tensor_tensor", [ta, tb_], [acc], acc[:], ta[:], tb_[:], ALU.add)
                        S.dma(qk_scr[qi_][:, t0s:t0s + n], acc[:, 0:n], reads=[acc], writes=[scr_buf("qk"[qi_], tb)])
                        S.I("dve", "tensor_tensor", [acc, E_], [dst], dst[:], acc[:], E_[:], ALU.mult)
                vb = VB[par]
                for half in range((nch + 1) // 2 if direction == 0 else 0):
                    pv = pg()
                    for s2 in range(2):
                        s_ = half * 2 + s2
                        if s_ >= nch:
                            continue
                        for k in range(8):
                            S.I("pe", "matmul", [hsrc, wgr], [pv], pv[:, s2 * 256:(s2 + 1) * 256], hsrc[:, k, s_ * 128:(s_ + 1) * 128],
                                wgr[:, k, 544:800], start=(k == 0), stop=(k == 7))
                    S.I("act", "activation", [pv], [vb], vb[:, half * 2:half * 2 + 2, :],
                        pv[:, :].rearrange("p (s c) -> p s c", c=256), AF.Copy)
                if direction == 0:
                    S.dma(v_scr[t0s:t0s + n, :].rearrange("(t p) c -> p t c", p=128), vb[:, 0:nch, :],
                          reads=[vb], writes=[scr_buf("v", tb)])

            for bi, tb in enumerate(blocks):
                isctx = tb < 0
                n = CTX if isctx else 512
                nch = n // 128
                par = bi % 2
                if bi == 0:
                    issue_loads(0)
                    issue_loads(1)
                    if direction == 1:
                        issue_v(0)
                    prologue(0)
                if direction == 1 and bi + 1 < len(blocks):
                    issue_v(bi + 1)
                if bi + 2 < len(blocks):
                    issue_loads(bi + 2)
                if bi + 1 < len(blocks):
                    prologue(bi + 1)
                for (p_, when) in list(pending_ag):
                    if when <= bi:
                        allgather(o_loc[p_], o_all[p_], b_o_loc[p_], b_o_all[p_])
                        pending_ag.remove((p_, when))
                EQ, EK = ET[0][par], ET[1][par]
                qp, kp, vb = QP[par], KP[par], VB[par]
                for fn_ in deferred:
                    fn_()
                deferred = []
                chunks = list(range(nch)) if direction == 0 else list(range(nch - 1, -1, -1))
                for ci, c in enumerate(chunks):
                    cs = slice(c * 128, (c + 1) * 128)
                    cnt[0] += 1
                    am, kt_ = Am[cnt[0] % 2], ktok[cnt[0] % 2]
                    S.I("pe", "matmul", [kp, qp], [PS2], PS2[:, 0:128], kp[0:64, cs], qp[0:64, cs], start=True, stop=True)
                    S.I("pe", "matmul", [kp, qp], [PS2], PS2[:, 512:640], kp[64:128, cs], qp[64:128, cs], start=True, stop=True)
                    S.I("dve", "tensor_tensor", [PS2, tri], [am], am[:],
                        PS2[:, :].rearrange("p (a b) -> p a b", b=512)[:, :, 0:128],
                        tri[:, direction:direction + 1, :].to_broadcast([128, 2, 128]), ALU.mult)
                    pt = PSB[0]
                    S.I("pe", "transpose", [kp, ident], [pt], pt[:, 0:128], kp[:, cs], ident[:])
                    S.I("act", "activation", [pt], [kt_], kt_[:], pt[:, 0:128], AF.Copy)
                    for mx in range(2):
                        r0 = mx * 64
                        po_ = psO[mx]
                        S.I("pe", "matmul", [vb, am], [po_], po_[:, cs], vb[:, c, mx * 128:(mx + 1) * 128], am[:, mx, :], start=True, stop=False)
                        S.I("pe", "matmul", [Sbf, qp], [po_], po_[:, cs], Sbf[r0:r0 + 64, :], qp[r0:r0 + 64, cs], start=False, stop=True)
                    pd = pg()
                    S.I("pe", "matmul", [kt_, vb], [pd], pd[0:64, 0:128], kt_[:, 0:64], vb[:, c, 0:128], start=True, stop=True)
                    S.I("pe", "matmul", [kt_, vb], [pd], pd[64:128, 0:128], kt_[:, 64:128], vb[:, c, 128:256], start=True, stop=True)
                    dcol = c * 128 + 127 if direction == 0 else c * 128
                    S.I("dve", "tensor_tensor", [pd, St], [tmpS], tmpS[:], pd[:, 0:128], St[:], ALU.add)
                    S.I("act", "activation", [tmpS, EQ], [Sbf], Sbf[:], tmpS[:], AF.Copy, scale=EQ[:, dcol:dcol + 1])
                    S.I("dve", "tensor_scalar", [tmpS, EQ], [St], St[:], tmpS[:], EQ[:, dcol:dcol + 1], None, ALU.mult)
                t0 = SEQ if isctx else tb * 512
                if isctx and not with_ctx:
                    continue
                for mx in range(2):
                    po_ = psO[mx]
                    o32 = o32s[mx]
                    if direction == 0:
                        S.I("act", "activation", [po_], [OF], OF[:, mx, t0:t0 + n], po_[:, 0:n], AF.Copy, scale=0.125)
                    else:
                        S.I("dve", "scalar_tensor_tensor", [po_, OF], [o32], o32[:, 0:n], po_[:, 0:n], 0.125, OF[:, mx, t0:t0 + n], ALU.mult, ALU.add)

                        def norm_out(mx=mx, n=n, t0=t0):
                            o32, sqo, rso = o32s[mx], sqos[mx], rsos[mx]
                            S.I("act", "activation", [o32], [sqo], sqo[:, 0:n], o32[:, 0:n], AF.Square)
                            p2 = pg()
                            S.I("pe", "matmul", [onesb, sqo], [p2], p2[:, 0:n], onesb[:], sqo[:, 0:n], start=True, stop=True)
                            S.I("act", "activation", [p2], [rso], rso[:, 0:n], p2[:, 0:n], AF.Ln, bias=EPS, scale=1.0 / 128)
                            S.I("act", "activation", [rso], [rso], rso[:, 0:n], rso[:, 0:n], AF.Exp, scale=-0.5)
                            ob = onb[mx]
                            S.I("dve", "scalar_tensor_tensor", [o32, ong, rso], [ob], ob[:, 0:n], o32[:, 0:n], ong[:, mx:mx + 1], rso[:, 0:n], ALU.mult, ALU.mult)
                            oap, obf = o_loc_ap(128 + mx * 128, 256 + mx * 128, t0, n)
                            S.dma(oap, ob[:, 0:n], reads=[ob], writes=[obf])
                        deferred.append(norm_out)
                if direction == 1 and not isctx and tb % 2 == 0:
                    pending_ag.append((tb // 2, bi + 2))
                if direction == 1 and isctx:
                    pending_ag.append((8, bi + 2))
            for fn_ in deferred:
                fn_()
            deferred = []
            for (p_, when) in pending_ag:
                allgather(o_loc[p_], o_all[p_], b_o_loc[p_], b_o_all[p_])
            pending_ag = []
        P.close()


    def phase_D(l, with_ctx, NH=2):
        P = Phase()
        modT = modTs[l]
        W = TOK // NH
        CW = CTX // NH
        NT = W + (CW if with_ctx else 0)
        arena = [P.tile("arena%d" % i, [128, 8, 512], BF16) for i in range(2)]
        acc = P.tile("acc", [128, 512])
        tmq = [P.tile("tmq%d" % i, [128, 512]) for i in range(2)]
        gates = []
        for t in range(2 if with_ctx else 1):
            gb = P.tile("gate%d" % t, [128, D])
            for cb in range(2):
                dg = tmq[cb]
                for k4 in range(4):
                    k = cb * 4 + k4
                    S.I("dve", "tensor_scalar", [identf, modT], [dg], dg[:, k4 * 128:(k4 + 1) * 128], identf[:],
                        modT[:, 16 + k, t:t + 1], None, ALU.mult)
                pg_ = psa()
                S.I("pe", "matmul", [onesf, dg], [pg_], pg_[:, :], onesf[:], dg[:], start=True, stop=True)
                S.I("act", "activation", [pg_], [gb], gb[:, cb * 512:(cb + 1) * 512], pg_[:, :], AF.Copy)
            gates.append(gb)
        hTd = P.tile("hTd", [128, 8, W], BF16)
        OU = P.tile("OU", [128, 12, NT], BF16)
        Y = P.tile("Y", [128, 8, NT], BF16)
        stgs = [P.tile("stg%d" % i, [128, 4, 512], BF16) for i in range(3)]
        wbrs = [P.tile("wbr%d" % i, [128, 3, 4, 128], BF16) for i in range(2)]
        wgs = [P.tile("wg%d" % i, [128, 3, 8, 128], BF16) for i in range(2)]
        szb = [P.tile("sz%d" % i, [128, 512], BF16) for i in range(2)]
        sgb = [P.tile("sg%d" % i, [128, 512]) for i in range(2)]
        cnt = [0]
        nblk = W // 512

        def load_z(z4):
            a = arena[z4 % 2]
            for k in range(8):
                S.dma(a[:, k, :], w_zg[l][k * 128:(k + 1) * 128, z4 * 512:(z4 + 1) * 512], writes=[a], eng="pool", cw=True)

        def load_oc(oc):
            wbr, wg = wbrs[oc % 2], wgs[oc % 2]
            for br in range(3):
                S.dma(wbr[:, br, :, :], w_br[l, br][:, oc * 128:(oc + 1) * 128].rearrange("(s p) c -> p s c", p=128),
                      writes=[wbr], eng="pool", cw=True)
                gc0 = 1536 + br * 1024 + oc * 128
                S.dma(wg[:, br, :, :], w_zg[l][:, gc0:gc0 + 128].rearrange("(k p) c -> p k c", p=128),
                      writes=[wg], eng="pool", cw=True)

        def load_wo():
            for cb in range(2):
                for k in range(8):
                    S.dma(arena[cb][:, k, :], w_o[l][k * 128:(k + 1) * 128, cb * 512:(cb + 1) * 512], writes=[arena[cb]], eng="pool", cw=True)

        for part in range(NH):
            blocks = []
            for b_ in range(nblk):
                blocks.append((hTd, b_ * 512, b_ * 512, 512))
            if with_ctx:
                blocks.append((hTc, part * CW, W, CW))
            load_z(0)
            load_z(1)
            for b_ in range(nblk):
                gblk = part * nblk + b_
                S.dma(hTd[:, :, b_ * 512:(b_ + 1) * 512], hT_loc[gblk].rearrange("(k p) n -> p k n", p=128),
                      reads=[b_hT_loc[gblk]], writes=[hTd], cw=True)
            for cidx in range(12):
                src_, br = cidx // 3, cidx % 3
                r0 = src_ * 384 + br * 128
                for b_ in range(nblk):
                    cnt[0] += 1
                    stg = stgs[cnt[0] % 3]
                    t0 = part * W + b_ * 512
                    for q in range(4):
                        tq = q * TOK + t0
                        pc = tq // 1024
                        S.dma(stg[:, q, :], o_all[pc][r0:r0 + 128, tq % 1024:tq % 1024 + 512], reads=[b_o_all[pc]], writes=[stg], cw=True)
                    dst = OU[:, cidx, b_ * 512:(b_ + 1) * 512]
                    psel = psa()
                    for q in range(4):
                        S.I("pe", "matmul", [dsel, stg], [psel], psel[:, :], dsel[:, q, :], stg[:, q, :], start=(q == 0), stop=(q == 3))
                    S.I("act", "activation", [psel], [OU], dst, psel[:, :], AF.Copy)
                if with_ctx:
                    S.dma(OU[:, cidx, W:W + CW], o_all[8][r0:r0 + 128, part * CW:(part + 1) * CW], reads=[b_o_all[8]], writes=[OU], cw=True)
            for z4 in range(3):
                wz4 = arena[z4 % 2]
                for sub in range(4):
                    zc = z4 * 4 + sub
                    for (hsrc, h0, o0, n) in blocks:
                        ps = psa()
                        proj_fm(ps, wz4, sub * 128, 128, hsrc, h0, n)
                        cnt[0] += 1
                        sz = szb[cnt[0] % 2]
                        S.I("act", "activation", [ps], [sz], sz[:, 0:n], ps[:, 0:n], AF.Silu)
                        S.I("dve", "tensor_tensor", [OU, sz], [OU], OU[:, zc, o0:o0 + n],
                            OU[:, zc, o0:o0 + n], sz[:, 0:n], ALU.mult)
                if z4 == 0:
                    load_z(2)
                    load_oc(0)
            load_oc(1)
            for oc in range(8):
                wbr, wg = wbrs[oc % 2], wgs[oc % 2]
                for (hsrc, h0, o0, n) in blocks:
                    for br in range(3):
                        pB = psa()
                        for s_ in range(4):
                            S.I("pe", "matmul", [wbr, OU], [pB], pB[:, 0:n], wbr[:, br, s_, :], OU[:, s_ * 3 + br, o0:o0 + n],
                                start=(s_ == 0), stop=(s_ == 3))
                        pG = psa()
                        for k in range(8):
                            S.I("pe", "matmul", [wg, hsrc], [pG], pG[:, 0:n], wg[:, br, k, :], hsrc[:, k, h0:h0 + n],
                                start=(k == 0), stop=(k == 7))
                        cnt[0] += 1
                        sg = sgb[cnt[0] % 2]
                        S.I("act", "activation", [pG], [sg], sg[:, 0:n], pG[:, 0:n], AF.Sigmoid)
                        if br == 0:
                            S.I("dve", "tensor_tensor", [pB, sg], [acc], acc[:, 0:n], pB[:, 0:n], sg[:, 0:n], ALU.mult)
                        else:
                            tq_ = tmq[br % 2]
                            S.I("dve", "tensor_tensor", [pB, sg], [tq_], tq_[:, 0:n], pB[:, 0:n], sg[:, 0:n], ALU.mult)
                            if br == 1:
                                S.I("dve", "tensor_tensor", [acc, tq_], [acc], acc[:, 0:n], acc[:, 0:n], tq_[:, 0:n], ALU.add)
                            else:
                                S.I("dve", "tensor_tensor", [acc, tq_], [Y], Y[:, oc, o0:o0 + n], acc[:, 0:n], tq_[:, 0:n], ALU.add)
                if oc == 0:
                    load_wo()
                if oc + 2 < 8:
                    load_oc(oc + 2)
            tiles = []
            for i in range(W // 128):
                tiles.append((xres, xres[:, part * (W // 128) + i, :], 0, 128, i * 128, gates[0]))
            if with_ctx:
                tok0 = part * CW
                tiles.append((xcres, xcres[tok0 % 128:tok0 % 128 + CW, tok0 // 128, :], tok0 % 128, CW, W, gates[1]))
            for (xb_, xap, p0, m, o0, gb) in tiles:
                for cb in range(2):
                    ps = psa()
                    for k in range(8):
                        S.I("pe", "matmul", [Y, arena[cb]], [ps], ps[p0:p0 + m, :], Y[:, k, o0:o0 + m], arena[cb][:, k, :],
                            start=(k == 0), stop=(k == 7))
                    cnt[0] += 1
                    tq_ = tmq[cnt[0] % 2]
                    S.I("dve", "tensor_tensor", [ps, gb], [tq_], tq_[p0:p0 + m, :], ps[p0:p0 + m, :], gb[p0:p0 + m, cb * 512:(cb + 1) * 512], ALU.mult)
                    xs = xap[:, cb * 512:(cb + 1) * 512]
                    S.I("dve", "tensor_tensor", [xb_, tq_], [xb_], xs, xs, tq_[p0:p0 + m, :], ALU.add)
        P.close()

    if stage >= 1:
        phase_mod([0])
    for l in range(depth):
        if stage == 0:
            break
        phase_A(l)
        if l == 0 and depth > 1:
            phase_mod([1])
        if stage == 1:
            break
        phase_N(l, l < DEPTH - 1)
        if stage == 2:
            break
        phase_GR(l, l < DEPTH - 1)
        if stage == 3:
            break
        phase_D(l, l < DEPTH - 1)

    if stage == 1:
        t = nc.dram_tensor("dbg_hT", [4 * D, TOK], BF16, kind="ExternalOutput").ap()
        dbg_out["hT"] = t
        tmp = T("dbgtmp", [128, 32, 512], BF16)
        for q in range(4):
            S.dma(tmp[:], hT_all[q].rearrange("(a p) n -> p a n", p=128), reads=[b_hT_all[q]], writes=[tmp])
            outs.append(S.dma(t[:, q * 512:(q + 1) * 512].rearrange("(a p) n -> p a n", p=128), tmp[:], reads=[tmp]))
        dbg("hTc", hTc, hTc[:], [128, 8, CTX], BF16)
        dbg("modT", modTs[0], modTs[0][:], [128, 24, 2])
    if stage == 3:
        for i_ in range(3):
            tmp = T("dbgtmp%d" % i_, [128, SEQ + CTX], BF16)
            for p in range(9):
                S.dma(tmp[:, p * 1024:p * 1024 + OW[p]], o_loc[p][i_ * 128:(i_ + 1) * 128, :], reads=[b_o_loc[p]], writes=[tmp])
            dbg("o%d" % i_, tmp, tmp[:], [128, SEQ + CTX], BF16)
    if stage == 2:
        tmp = T("dbgtmp", [128, SEQ + CTX], BF16)
        for p in range(9):
            S.dma(tmp[:, p * 1024:p * 1024 + OW[p]], o_loc[p][0:128, :], reads=[b_o_loc[p]], writes=[tmp])
        dbg("ona", tmp, tmp[:], [128, SEQ + CTX], BF16)
    if stage == 4:
        dbg("xc", xcres, xcres[:], [128, 2, D])
    for t in range(16):
        outs.append(S.dma(y_out[t * 128:(t + 1) * 128, :], xres[:, t, :], reads=[xres]))
    S.emit(final_waits=outs)
    return nc, declared


def _rope_tables():
    pos = np.arange(SEQ)
    row = (pos // 64).astype(np.float64)
    col = (pos % 64).astype(np.float64)
    inv = 10000.0 ** (-np.arange(16, dtype=np.float64) / 16)
    cos = np.ones((128, SEQ), np.float64)
    sin = np.zeros((128, SEQ), np.float64)
    for d in range(64):
        half = d // 32
        dd = d % 32
        a = dd % 16
        p = row if half == 0 else col
        ang = p * inv[a]
        cos[64 + d] = np.cos(ang)
        sin[64 + d] = (-1.0 if dd < 16 else 1.0) * np.sin(ang)
    return np.stack([cos, sin]).astype(np.float32)


def _rope_perm():
    perm = np.zeros(64, np.int64)
    for d in range(64):
        base = (d // 32) * 32
        dd = d % 32
        perm[d] = base + (dd + 16 if dd < 16 else dd - 16)
    return perm


def _econst(j):
    lgf = np.log1p(-2.0 ** (-(5.0 + j)))
    lgb = np.log1p(-2.0 ** (-(5.5 + j)))
    i = np.arange(128, dtype=np.float64)
    rows = [np.exp((i + 1) * lgf), np.exp(-(i + 1) * lgf), np.exp((128 - i) * lgb), np.exp(-(128 - i) * lgb)]
    e = np.zeros((4, 128, 512), np.float32)
    for k in range(4):
        e[k, 64:128, :] = np.tile(rows[k], 4)[None, :].astype(np.float32)
    return e


def _na_bias(rpb_h):
    kc = np.arange(64)[:, None]
    qc = np.arange(64)[None, :]
    cs = np.clip(qc - 8, 0, 48)
    colok = (kc >= cs) & (kc <= cs + 15)
    cidx = np.clip(kc - qc + 15, 0, 30)
    strip = np.full((128, 22 * 64), NEG, np.float32)
    for a in range(2):
        for e in range(22):
            dr = a - e + 10
            if -4 <= dr <= 3:
                blk = np.where(colok, rpb_h[dr + 7][cidx], NEG)
                strip[a * 64:(a + 1) * 64, e * 64:(e + 1) * 64] = blk
    edge = np.full((12, 128, 512), NEG, np.float32)
    idx = 0
    for m, kts in ((0, range(2, 8)), (15, range(0, 6))):
        for kt in kts:
            for a in range(2):
                kr = 8 * m - 4 + 2 * kt + a
                for qi in range(8):
                    qr = 8 * m + qi
                    rs = min(max(qr - 4, 0), 120)
                    if rs <= kr <= rs + 7:
                        blk = np.where(colok, rpb_h[kr - qr + 7][cidx], NEG)
                        edge[idx, a * 64:(a + 1) * 64, qi * 64:(qi + 1) * 64] = blk
            idx += 1
    return strip, edge


def make_in_maps(inp):
    f32 = np.float32
    x, c, ctx, c_ctx = inp["x"], inp["c"], inp["ctx"], inp["c_ctx"]
    w_in = inp["w_in"]
    L = DEPTH
    rope = _rope_tables()
    perm = _rope_perm()
    ident = np.eye(128, dtype=f32)
    tri = np.stack([np.triu(np.ones((128, 128), f32)), np.tril(np.ones((128, 128), f32))])
    bones = np.zeros((128, 128), f32)
    bones[:64, :64] = 1
    bones[64:, 64:] = 1
    b_modT = np.ascontiguousarray(inp["b_mod"].reshape(L, 24, 128).transpose(0, 2, 1))
    norm_wT = np.ascontiguousarray(inp["norm_w"].reshape(L, 8, 128).transpose(0, 2, 1))
    w_mod = np.ascontiguousarray(inp["w_mod"])
    w_br = np.ascontiguousarray(inp["w_branch"])
    w_o = np.ascontiguousarray(inp["w_out"])
    zcols = np.concatenate([np.arange(3616 + br * 512 + s * 128, 3616 + br * 512 + s * 128 + 128)
                            for s in range(4) for br in range(3)])
    w_zg = np.ascontiguousarray(np.concatenate([w_in[:, :, zcols], w_in[:, :, 5152:8224]], axis=2))
    maps = []
    for core in range(8):
        b, j = core // 4, core % 4
        cv = np.stack([c[b], c_ctx])
        cTm = np.ascontiguousarray(cv.T.reshape(8, 128, 2).transpose(1, 0, 2))
        sl = lambda o, n: slice(o, o + n)
        w_na = np.concatenate([w_in[:, :, sl(128 * j, 128)], w_in[:, :, sl(512 + 128 * j, 128)],
                               w_in[:, :, sl(1024 + 128 * j, 128)]], axis=2)
        glaq = w_in[:, :, sl(1536 + 64 * j, 64)]
        glak = w_in[:, :, sl(1792 + 64 * j, 64)]
        retq = w_in[:, :, sl(2592 + 64 * j, 64)]
        retk = w_in[:, :, sl(2848 + 64 * j, 64)]
        w_gr = np.concatenate([glaq, retq, glak, retk, glaq, retq[:, :, perm], glak, retk[:, :, perm],
                               w_in[:, :, 2560:2592], w_in[:, :, sl(2048 + 128 * j, 128)],
                               w_in[:, :, sl(3104 + 128 * j, 128)]], axis=2)
        na_g = np.stack([np.tile(inp["na_q_norm"], (1, 2)), np.tile(inp["na_k_norm"], (1, 2))], axis=2)
        strips = np.zeros((L, 2, 128, 22 * 64), f32)
        edges = np.zeros((L, 2, 12, 128, 512), f32)
        for l in range(L):
            for h in range(2):
                strips[l, h], edges[l, h] = _na_bias(inp["na_rpb"][l, 2 * j + h])
        gla_wg = np.ascontiguousarray(inp["gla_w_gate"][:, :, :, 64 * j:64 * j + 64])
        gla_bg = np.ascontiguousarray(inp["gla_b_gate"][:, :, 64 * j:64 * j + 64].transpose(0, 2, 1))
        on_g = np.stack([inp["gla_out_norm"][:, 128 * j:128 * j + 128], inp["ret_out_norm"][:, 128 * j:128 * j + 128]], axis=2)
        selv = np.zeros((128, 4), f32)
        selv[:, j] = 1.0
        maps.append({
            "x_sh": np.ascontiguousarray(x[b, TOK * j:TOK * (j + 1)]),
            "ctx_b": np.ascontiguousarray(ctx[b]),
            "cT": cTm, "w_mod": w_mod, "b_modT": b_modT, "norm_wT": norm_wT,
            "w_na": np.ascontiguousarray(w_na), "w_gr": np.ascontiguousarray(w_gr), "w_zg": w_zg,
            "w_br": w_br, "w_o": w_o, "na_g": np.ascontiguousarray(na_g.astype(f32)),
            "na_strip": strips, "na_edge": edges, "gla_wg": gla_wg, "gla_bg": gla_bg,
            "on_g": np.ascontiguousarray(on_g.astype(f32)), "econst": _econst(j), "rope": rope, "sel": selv,
            "c_ident": ident.astype(ml_dtypes.bfloat16), "c_identf": ident,
            "c_tri": tri.astype(ml_dtypes.bfloat16), "c_bones": bones.astype(ml_dtypes.bfloat16),
        })
    return maps


_CACHE = {}


def kernel(**inputs):
    inp = {k: np.asarray(v) for k, v in inputs.items()}
    maps = make_in_maps(inp)
    if "nc" not in _CACHE:
        _CACHE["nc"] = build_program()[0]
    res = run_bass_kernel_spmd(_CACHE["nc"], maps, core_ids=list(range(8)))
    out = np.zeros((2, SEQ, D), np.float32)
    for core in range(8):
        b, j = core // 4, core % 4
        out[b, TOK * j:TOK * (j + 1)] = res.results[core]["y"]
    return out
```

```python
import os
import contextlib
import numpy as np
import ml_dtypes
import concourse.bass as bass
import concourse.mybir as mybir
from concourse.bass_utils import run_bass_kernel_spmd

F32 = mybir.dt.float32
BF16 = mybir.dt.bfloat16
AF = mybir.ActivationFunctionType
ALU = mybir.AluOpType

COMPUTE = ("pe", "act", "dve", "pool")
ALL_ENG = COMPUTE + ("sp",)

D = 1024
SEQ = 8192
CTX = 256
TOK = 2048
DEPTH = 2
GROUPS = [[0, 1, 2, 3], [4, 5, 6, 7]]
EPS = 1e-6
NEG = -30000.0


class Buf:
    __slots__ = ("name", "t", "last_ws", "readers")

    def __init__(self, name, t=None):
        self.name = name
        self.t = t
        self.last_ws = []
        self.readers = []

    def __getitem__(self, k):
        return self.t[k]


class Op:
    __slots__ = ("eng", "fn", "deps", "is_dma", "sem", "semval", "signal", "inc", "is_cc")

    def __init__(self, eng, fn, is_dma):
        self.eng = eng
        self.fn = fn
        self.deps = []
        self.is_dma = is_dma
        self.sem = None
        self.semval = None
        self.signal = False
        self.inc = 1
        self.is_cc = False


class Sched:
    def __init__(self, nc, n_dma_sems=32):
        self.nc = nc
        self.ops = {e: [] for e in ALL_ENG}
        self.n_dma_sems = n_dma_sems
        self.dma_rr = 0
        self.dma_last = [None] * n_dma_sems
        self.dma_count = [0] * n_dma_sems
        self.bar_deps = {}
        self.n_ops = 0

    def op(self, eng, fn, reads=(), writes=(), dma=False, inc=16, cw=False):
        o = Op(eng, fn, dma)
        deps = []
        for b in reads:
            deps.extend(b.last_ws)
        for b in writes:
            if not (cw and not b.readers):
                deps.extend(b.last_ws)
            deps.extend(b.readers)
        if eng in self.bar_deps:
            deps.extend(self.bar_deps.pop(eng))
        if dma:
            k = self.dma_rr
            self.dma_rr = (self.dma_rr + 1) % self.n_dma_sems
            prev = self.dma_last[k]
            if prev is not None:
                deps.append(prev)
            self.dma_last[k] = o
            self.dma_count[k] += inc
            o.sem = ("dma", k)
            o.semval = self.dma_count[k]
            o.inc = inc
            o.signal = True
        seen = set()
        for d in deps:
            if d is o or id(d) in seen:
                continue
            seen.add(id(d))
            if (not d.is_dma) and d.eng == eng:
                if eng == "pe" or eng == "sp":
                    continue
                if not any((d in b.last_ws) for b in reads):
                    continue
            o.deps.append(d)
        for b in reads:
            b.readers.append(o)
        for b in writes:
            if cw and not b.readers:
                b.last_ws.append(o)
            else:
                b.last_ws = [o]
            b.readers = []
        self.ops[eng].append(o)
        self.n_ops += 1
        return o

    def I(self, eng, name, reads, writes, *a, **kw):
        return self.op(eng, lambda e: getattr(e, name)(*a, **kw), reads, writes)

    def dma(self, out_ap, in_ap, reads=(), writes=(), eng="sp", cw=False, **kw):
        return self.op(eng, lambda e: e.dma_start(out=out_ap, in_=in_ap, **kw), reads, writes, dma=True, cw=cw)

    def barrier(self):
        last = []
        for e in ALL_ENG:
            for o in reversed(self.ops[e]):
                if not o.is_dma:
                    last.append(o)
                    break
        last.extend(o for o in self.dma_last if o is not None and not o.is_cc)
        for e in ALL_ENG:
            self.bar_deps[e] = list(last) + self.bar_deps.get(e, [])

    def emit(self, final_waits=()):
        nc = self.nc
        for e in ALL_ENG:
            for o in self.ops[e]:
                for d in o.deps:
                    if not d.is_dma:
                        d.signal = True
        for e in ALL_ENG:
            c = 0
            for o in self.ops[e]:
                if o.is_dma:
                    continue
                if o.signal:
                    c += 1
                    o.sem = ("eng", e)
                    o.semval = c
        sems = {}
        with contextlib.ExitStack() as st:
            for e in ALL_ENG:
                sems[("eng", e)] = st.enter_context(nc.semaphore("s_" + e))
            for k in range(self.n_dma_sems):
                sems[("dma", k)] = st.enter_context(nc.semaphore("s_dma%d" % k))
            block = st.enter_context(nc.Block())
            handles = {"pe": block.tensor, "act": block.scalar, "dve": block.vector,
                       "pool": block.gpsimd, "sp": block.sync}

            def make(e):
                def body(eng):
                    known = {}
                    for o in self.ops[e]:
                        need = {}
                        for d in o.deps:
                            if known.get(d.sem, 0) >= d.semval:
                                continue
                            if need.get(d.sem, 0) < d.semval:
                                need[d.sem] = d.semval
                        for s, v in need.items():
                            eng.wait_ge(sems[s], v)
                            known[s] = v
                        ins = o.fn(eng)
                        if o.signal:
                            ins.then_inc(sems[o.sem], o.inc if o.is_dma else 1)
                    if e == "sp":
                        need = {}
                        for d in final_waits:
                            if need.get(d.sem, 0) < d.semval:
                                need[d.sem] = d.semval
                        for s, v in need.items():
                            if known.get(s, 0) < v:
                                eng.wait_ge(sems[s], v)
                return body

            for e in ALL_ENG:
                handles[e](make(e))


def build_program(stage=99, depth=DEPTH):
    nc = bass.Bass("TRN2", target_bir_lowering=False)
    S = Sched(nc)
    L = DEPTH

    declared = []
    need = {"x_sh": 0, "ctx_b": 0, "cT": 0, "w_mod": 1, "b_modT": 1, "norm_wT": 1, "w_na": 2, "w_gr": 3, "w_zg": 4,
            "w_br": 4, "w_o": 4, "na_g": 2, "na_strip": 2, "na_edge": 2, "gla_wg": 3, "gla_bg": 3, "on_g": 3,
            "econst": 3, "rope": 3, "sel": 0, "c_ident": 0, "c_identf": 0, "c_tri": 0, "c_bones": 0}

    def din(name, shape, dt=F32):
        if stage < need[name]:
            return None
        declared.append(name)
        return nc.dram_tensor(name, list(shape), dt, kind="ExternalInput").ap()

    x_sh = din("x_sh", [TOK, D])
    ctx_b = din("ctx_b", [CTX, D])
    cT = din("cT", [128, 8, 2])
    w_mod = din("w_mod", [L, D, 3 * D])
    b_modT = din("b_modT", [L, 128, 24])
    norm_wT = din("norm_wT", [L, 128, 8])
    w_na = din("w_na", [L, D, 384])
    w_gr = din("w_gr", [L, D, 800])
    w_zg = din("w_zg", [L, D, 4608])
    w_br = din("w_br", [L, 3, 512, D])
    w_o = din("w_o", [L, D, D])
    na_g = din("na_g", [L, 128, 2])
    na_strip = din("na_strip", [L, 2, 128, 22 * 64])
    na_edge = din("na_edge", [L, 2, 12, 128, 512])
    gla_wg = din("gla_wg", [L, 2, 16, 64])
    gla_bg = din("gla_bg", [L, 64, 2])
    on_g = din("on_g", [L, 128, 2])
    econst = din("econst", [4, 128, 512])
    rope = din("rope", [2, 128, SEQ])
    sel_in = din("sel", [128, 4])
    c_ident = din("c_ident", [128, 128], BF16)
    c_identf = din("c_identf", [128, 128])
    c_tri = din("c_tri", [2, 128, 128], BF16)
    c_bones = din("c_bones", [128, 128], BF16)
    y_out = nc.dram_tensor("y", [TOK, D], F32, kind="ExternalOutput").ap()
    dbg_out = {}

    hT_loc = [nc.dram_tensor("hT_loc%d" % i, [D, 512], BF16).ap() for i in range(4)]
    hT_all = [nc.dram_tensor("hT_all%d" % i, [4 * D, 512], BF16).ap() for i in range(4)]
    b_hT_loc = [Buf("hT_loc%d" % i) for i in range(4)]
    b_hT_all = [Buf("hT_all%d" % i) for i in range(4)]
    OW = [1024] * 8 + [CTX]
    o_loc = [nc.dram_tensor("o_loc%d" % i, [384, OW[i]], BF16).ap() for i in range(9)]
    o_all = [nc.dram_tensor("o_all%d" % i, [4 * 384, OW[i]], BF16).ap() for i in range(9)]
    b_o_loc = [Buf("o_loc%d" % i) for i in range(9)]
    b_o_all = [Buf("o_all%d" % i) for i in range(9)]

    def o_loc_ap(r0, r1, t0, n):
        if t0 >= SEQ:
            return o_loc[8][r0:r1, t0 - SEQ:t0 - SEQ + n], b_o_loc[8]
        p = t0 // 1024
        return o_loc[p][r0:r1, t0 % 1024:t0 % 1024 + n], b_o_loc[p]

    NTT = SEQ + CTX
    qk_scr = nc.dram_tensor("qk_scr", [2, 128, NTT], F32).ap()
    v_scr = nc.dram_tensor("v_scr", [NTT, 256], BF16).ap()
    gb_scr = nc.dram_tensor("gb_scr", [16, NTT], BF16).ap()
    b_scr = {}

    def scr_buf(kind, tb):
        return b_scr.setdefault((kind, tb), Buf("scr_%s_%d" % (kind, tb)))

    def allgather(src, dst, bsrc, bdst):
        o_ = S.op("pool", lambda e: e.collective_compute("AllGather", ALU.bypass, replica_groups=GROUPS,
                                                          ins=[src], outs=[dst]),
                  reads=[bsrc], writes=[bdst], dma=True, inc=1)
        o_.is_cc = True

    uid = [0]

    def T(name, shape, dt=F32):
        uid[0] += 1
        return Buf(name, nc.alloc_sbuf_tensor("%s_%d" % (name, uid[0]), list(shape), dt))

    class Phase:
        def __init__(self):
            self.st = contextlib.ExitStack()

        def tile(self, name, shape, dt=F32):
            uid[0] += 1
            return Buf(name, self.st.enter_context(nc.sbuf_tensor("%s_%d" % (name, uid[0]), list(shape), dt)))

        def close(self):
            S.barrier()
            self.st.close()

    PSA = [Buf("psa%d" % i, nc.alloc_psum_tensor("psa%d" % i, [128, 512], F32)) for i in range(4)]
    PS2 = Buf("ps2", nc.alloc_psum_tensor("ps2", [128, 1024], F32))
    PSB = [Buf("psb%d" % i, nc.alloc_psum_tensor("psb%d" % i, [128, 1024], BF16)) for i in range(2)]
    rr = {"a": 0, "b": 0}

    def psa():
        rr["a"] = (rr["a"] + 1) % 4
        return PSA[rr["a"]]

    def psb():
        rr["b"] = (rr["b"] + 1) % 2
        return PSB[rr["b"]]

    xres = T("xres", [128, 16, D])
    xcres = T("xcres", [128, 2, D])
    hTc = T("hTc", [128, 8, CTX], BF16)
    ident = T("ident", [128, 128], BF16)
    identf = T("identf", [128, 128])
    onesf = T("onesf", [128, 128])
    tri = T("tri", [128, 2, 128], BF16)
    bones = T("bones", [128, 128], BF16)
    sel = T("sel", [128, 4])
    cTs = T("cTs", [128, 8, 2])
    modTs = [T("modT%d" % i, [128, 24, 2]) for i in range(DEPTH)]
    weffs = [T("weff%d" % i, [128, 8, 2]) for i in range(DEPTH)]
    small = T("small", [128, 64])

    for t in range(16):
        S.dma(xres[:, t, :], x_sh[t * 128:(t + 1) * 128, :], writes=[xres], cw=True)
    S.dma(xcres[:], ctx_b.rearrange("(t p) d -> p t d", p=128), writes=[xcres])
    S.dma(ident[:], c_ident, writes=[ident])
    S.dma(identf[:], c_identf, writes=[identf])
    S.dma(tri[:], c_tri.rearrange("a p n -> p a n"), writes=[tri])
    S.dma(bones[:], c_bones, writes=[bones])
    S.dma(sel[:], sel_in, writes=[sel])
    S.dma(cTs[:], cT, writes=[cTs])
    S.I("dve", "memset", [], [onesf], onesf[:], 1.0)
    dsel = T("dsel", [128, 4, 128], BF16)
    for q_ in range(4):
        S.I("dve", "tensor_scalar", [ident, sel], [dsel], dsel[:, q_, :], ident[:], sel[:, q_:q_ + 1], None, ALU.mult)

    outs = []

    def dbg(name, buf, ap, shape, dt=F32):
        t = nc.dram_tensor("dbg_" + name, list(shape), dt, kind="ExternalOutput").ap()
        dbg_out[name] = t
        outs.append(S.dma(t, ap, reads=[buf]))

    def phase_mod(layers):
        P = Phase()
        sT = P.tile("sT", [128, 8, 2])
        wmb = [P.tile("wm%d" % i, [128, 8, 512]) for i in range(3)]
        S.I("act", "activation", [cTs], [sT], sT[:], cTs[:], AF.Silu)
        it = 0
        for l in layers:
            modT, weff = modTs[l], weffs[l]
            bmt = P.tile("bmt", [128, 24])
            nwt = P.tile("nwt", [128, 8])
            S.dma(bmt[:], b_modT[l], writes=[bmt])
            S.dma(nwt[:], norm_wT[l], writes=[nwt])
            psm = psa()
            for cc in range(6):
                wm = wmb[it % 3]
                it += 1
                S.dma(wm[:], w_mod[l][:, cc * 512:(cc + 1) * 512].rearrange("(k p) c -> p k c", p=128), writes=[wm])
                for sub in range(4):
                    c24 = cc * 4 + sub
                    for k in range(8):
                        S.I("pe", "matmul", [wm, sT], [psm], psm[:, c24 * 2:c24 * 2 + 2],
                            wm[:, k, sub * 128:(sub + 1) * 128], sT[:, k, :], start=(k == 0), stop=(k == 7))
            S.I("dve", "tensor_tensor", [psm, bmt], [modT], modT[:],
                psm[:, 0:48].rearrange("p (c t) -> p c t", t=2), bmt[:].unsqueeze(2).to_broadcast([128, 24, 2]), ALU.add)
            S.I("dve", "scalar_tensor_tensor", [modT, nwt], [weff], weff[:], modT[:, 8:16, :], 1.0,
                nwt[:].unsqueeze(2).to_broadcast([128, 8, 2]), ALU.add, ALU.mult)
        P.close()

    def phase_A(l):
        P = Phase()
        modT, weff = modTs[l], weffs[l]
        SUB = int(os.environ.get("KSUB", "9"))
        if SUB < 2:
            P.close()
            return
        ss = P.tile("ss", [128, 18])
        rstd = P.tile("rstd", [128, 18])
        junk = P.tile("junk", [128, D], BF16)
        xnb = [P.tile("xn%d" % i, [128, D], BF16) for i in range(2)]
        stg = [P.tile("stg%d" % i, [128, 8, 512], BF16) for i in range(2)]
        for tt in range(18):
            isctx = tt >= 16
            xt = xcres[:, tt - 16, :] if isctx else xres[:, tt, :]
            xb_ = xcres if isctx else xres
            tsel = 1 if isctx else 0
            S.I("act", "activation", [xb_], [junk, ss], junk[:], xt, AF.Square, accum_out=ss[:, tt:tt + 1])
            S.I("act", "activation", [ss], [rstd], rstd[:, tt:tt + 1], ss[:, tt:tt + 1], AF.Ln, bias=EPS, scale=1.0 / D)
            S.I("act", "activation", [rstd], [rstd], rstd[:, tt:tt + 1], rstd[:, tt:tt + 1], AF.Exp, scale=-0.5)
            xn = xnb[tt % 2]
            S.I("dve", "tensor_scalar", [xb_, rstd], [xn], xn[:], xt, rstd[:, tt:tt + 1], None, ALU.mult)
            pt = psb()
            for k in range(8):
                S.I("pe", "transpose", [xn, ident], [pt], pt[:, k * 128:(k + 1) * 128], xn[:, k * 128:(k + 1) * 128], ident[:])
            if isctx:
                dst, dbuf, c0 = hTc, hTc, (tt - 16) * 128
            else:
                dbuf = stg[(tt // 4) % 2]
                dst, c0 = dbuf, (tt % 4) * 128
            for k in range(8):
                eng = "act" if k % 2 == 0 else "dve"
                if eng == "act":
                    S.I("act", "activation", [pt, weff, modT], [dbuf], dst[:, k, c0:c0 + 128], pt[:, k * 128:(k + 1) * 128],
                        AF.Identity, bias=modT[:, k, tsel:tsel + 1], scale=weff[:, k, tsel:tsel + 1])
                else:
                    S.I("dve", "tensor_scalar", [pt, weff, modT], [dbuf], dst[:, k, c0:c0 + 128], pt[:, k * 128:(k + 1) * 128],
                        weff[:, k, tsel:tsel + 1], modT[:, k, tsel:tsel + 1], ALU.mult, ALU.add)
            if (not isctx) and tt % 4 == 3:
                blk = tt // 4
                S.dma(hT_loc[blk].rearrange("(k p) n -> p k n", p=128), dbuf[:],
                      reads=[dbuf], writes=[b_hT_loc[blk]])
                allgather(hT_loc[blk], hT_all[blk], b_hT_loc[blk], b_hT_all[blk])
        P.close()

    def load_hb(hb, tb):
        r, blk = tb // 4, tb % 4
        S.dma(hb[:], hT_all[blk][r * D:(r + 1) * D, :].rearrange("(k p) n -> p k n", p=128),
              reads=[b_hT_all[blk]], writes=[hb])

    def proj_fm(ps, w, c0, m, hsrc, n0, n, m0=0):
        for k in range(8):
            S.I("pe", "matmul", [w, hsrc], [ps], ps[m0:m0 + m, 0:n], w[:, k, c0:c0 + m], hsrc[:, k, n0:n0 + n],
                start=(k == 0), stop=(k == 7))

    def phase_N(l, with_ctx):
        P = Phase()
        wna = P.tile("wna", [128, 8, 384], BF16)
        S.dma(wna[:], w_na[l].rearrange("(k p) c -> p k c", p=128), writes=[wna], eng="pool")
        nag = P.tile("nag", [128, 2])
        S.dma(nag[:], na_g[l], writes=[nag])
        QT = P.tile("QT", [128, SEQ], BF16)
        KT = P.tile("KT", [128, SEQ], BF16)
        VA = P.tile("VA", [128, 64, 192], BF16)
        QTc = P.tile("QTc", [128, CTX], BF16)
        KTc = P.tile("KTc", [128, CTX], BF16)
        VAc = P.tile("VAc", [128, 2, 192], BF16)
        S.I("pool", "memset", [], [VA], VA[:, :, 64:128], 1.0)
        S.I("pool", "memset", [], [VAc], VAc[:, :, 64:128], 1.0)
        hbs = [P.tile("hb%d" % i, [128, 8, 512], BF16) for i in range(2)]
        sq = P.tile("sq", [128, 512], BF16)
        rs = P.tile("rs", [128, 512])

        def qk_norm(ps, n, dst_buf, dst_ap, gcol):
            S.I("act", "activation", [ps], [sq], sq[:, 0:n], ps[:, 0:n], AF.Square)
            p2 = psa()
            S.I("pe", "matmul", [bones, sq], [p2], p2[:, 0:n], bones[:], sq[:, 0:n], start=True, stop=True)
            S.I("act", "activation", [p2], [rs], rs[:, 0:n], p2[:, 0:n], AF.Ln, bias=EPS, scale=1.0 / 64)
            S.I("act", "activation", [rs], [rs], rs[:, 0:n], rs[:, 0:n], AF.Exp, scale=-0.5)
            S.I("dve", "scalar_tensor_tensor", [ps, nag, rs], [dst_buf], dst_ap, ps[:, 0:n], nag[:, gcol:gcol + 1],
                rs[:, 0:n], ALU.mult, ALU.mult)

        def project(hsrc, n, qdst, kdst, vdst, vt0, qb, kb, vb):
            pq = psa()
            proj_fm(pq, wna, 0, 128, hsrc, 0, n)
            qk_norm(pq, n, qb, qdst, 0)
            pk = psa()
            proj_fm(pk, wna, 128, 128, hsrc, 0, n)
            qk_norm(pk, n, kb, kdst, 1)
            pv = psa()
            nt = n // 128
            for s in range(nt):
                for k in range(8):
                    S.I("pe", "matmul", [hsrc, wna], [pv], pv[:, s * 128:(s + 1) * 128], hsrc[:, k, s * 128:(s + 1) * 128],
                        wna[:, k, 256:384], start=(k == 0), stop=(k == 7))
            pv3 = pv[:, 0:n].rearrange("p (s c) -> p s c", c=128)
            S.I("act", "activation", [pv], [vb], vdst[:, vt0:vt0 + nt, 0:64], pv3[:, :, 0:64], AF.Copy)
            S.I("dve", "tensor_copy", [pv], [vb], vdst[:, vt0:vt0 + nt, 128:192], pv3[:, :, 64:128])

        project(hTc, CTX, QTc[:], KTc[:], VAc, 0, QTc, KTc, VAc)
        for it, tb in enumerate([r_ * 4 + blk_ for blk_ in range(4) for r_ in range(4)]):
            hb = hbs[it % 2]
            load_hb(hb, tb)
            project(hb, 512, QT[:, tb * 512:(tb + 1) * 512], KT[:, tb * 512:(tb + 1) * 512], VA, tb * 4, QT, KT, VA)

        if stage == 2:
            dbg("QT", QT, QT[:], [128, SEQ], BF16)
            dbg("KT", KT, KT[:], [128, SEQ], BF16)

        msk = P.tile("msk", [128, 22 * 64], BF16)
        edg = P.tile("edg", [128, 12, 512], BF16)
        bst = [P.tile("bst%d" % i, [128, 704]) for i in range(2)]
        pts = [P.tile("pt%d" % i, [128, 512], BF16) for i in range(4)]
        ona = [P.tile("ona%d" % i, [128, 512], BF16) for i in range(2)]
        rden = P.tile("rden", [128, 512])
        lnd = P.tile("lnd", [128, 512])
        ptc = [0]

        def va_lhsT(vbuf, tile_i, h):
            return vbuf[:, tile_i, 0:128] if h == 0 else vbuf[:, tile_i, 64:192]

        sbanks = [PSA[2], PSA[3], Buf("ps2a", PS2.t[:, 0:512]), Buf("ps2b", PS2.t[:, 512:1024]),
                  Buf("psbf0", PSB[0].t[:].bitcast(F32)), Buf("psbf1", PSB[1].t[:].bitcast(F32))]
        pobanks = [PSA[0], PSA[1]]
        pts = pts + [P.tile("ptx%d" % i, [128, 512], BF16) for i in range(2)]
        LAG = 3
        items = []
        for h in range(2):
            hp = h * 64
            for m in range(16):
                qcols = slice(m * 512, (m + 1) * 512)
                keys = []
                for kt in range(8):
                    kr0 = 8 * m - 4 + 2 * kt
                    if kr0 < 0 or kr0 >= 128:
                        continue
                    if m == 0:
                        mk, mb = edg[:, kt - 2, :], edg
                    elif m == 15:
                        mk, mb = edg[:, 6 + kt, :], edg
                    else:
                        e0 = 14 - 2 * kt
                        mk, mb = msk[:, e0 * 64:(e0 + 8) * 64], msk
                    keys.append((KT, KT[hp:hp + 64, kr0 * 64:kr0 * 64 + 128], VA, kr0 // 2, mk, mb))
                for ct in range(2):
                    keys.append((KTc, KTc[hp:hp + 64, ct * 128:(ct + 1) * 128], VAc, ct, None, None))
                for i, kk in enumerate(keys):
                    items.append((h, m, i, len(keys)) + kk + (QT, QT[hp:hp + 64, qcols], 512, m * 512))
            if with_ctx:
                for ct in range(2):
                    items.append((h, 16, ct, 2, KTc, KTc[hp:hp + 64, ct * 128:(ct + 1) * 128], VAc, ct, None, None,
                                  QTc, QTc[hp:hp + 64, :], CTX, SEQ))
        cur_h = [-1]
        pend = {}
        for idx in range(len(items) + LAG):
            if idx < len(items):
                (h, m, i, nk, kbuf, kap, vbuf, vt, mk, mb, qbuf, qap, n, t0) = items[idx]
                if h != cur_h[0]:
                    cur_h[0] = h
                    for i2 in range(2):
                        b_ = bst[i2]
                        S.dma(b_[:], na_strip[l, h][:, i2 * 704:(i2 + 1) * 704], writes=[b_])
                        S.I("act", "activation", [b_], [msk], msk[:, i2 * 704:(i2 + 1) * 704], b_[:], AF.Exp)
                    for i2 in range(12):
                        b_ = bst[i2 % 2]
                        S.dma(b_[:, 0:512], na_edge[l, h, i2], writes=[b_])
                        S.I("act", "activation", [b_], [edg], edg[:, i2, :], b_[:, 0:512], AF.Exp)
                ps_ = sbanks[idx % 6]
                pt_ = pts[idx % 6]
                S.I("pe", "matmul", [kbuf, qbuf], [ps_], ps_[:, 0:n], kap, qap, start=True, stop=True)
                S.I("act", "activation", [ps_], [pt_], pt_[:, 0:n], ps_[:, 0:n], AF.Exp, scale=0.125)
                if mk is not None:
                    S.I("dve" if idx % 2 == 0 else "pool", "tensor_tensor", [pt_, mb], [pt_], pt_[:], pt_[:], mk, ALU.mult)
                pend[idx] = pt_
            j2 = idx - LAG
            if j2 >= 0:
                (h, m, i, nk, kbuf, kap, vbuf, vt, mk, mb, qbuf, qap, n, t0) = items[j2]
                hp = h * 64
                pt_ = pend.pop(j2)
                po = pobanks[(h * 17 + m) % 2]
                S.I("pe", "matmul", [vbuf, pt_], [po], po[:, 0:n], va_lhsT(vbuf, vt, h), pt_[:, 0:n],
                    start=(i == 0), stop=(i == nk - 1))
                if i == nk - 1:
                    ob = ona[(h * 17 + m) % 2]
                    dp = 64 - hp
                    S.I("act", "activation", [po], [lnd], lnd[dp:dp + 64, 0:n], po[dp:dp + 64, 0:n], AF.Ln)
                    S.I("act", "activation", [lnd], [lnd], lnd[dp:dp + 64, 0:n], lnd[dp:dp + 64, 0:n], AF.Exp, scale=-1.0)
                    S.I("dve", "tensor_copy", [lnd], [rden], rden[hp:hp + 64, 0:n], lnd[dp:dp + 64, 0:n])
                    S.I("dve", "tensor_tensor", [po, rden], [ob], ob[hp:hp + 64, 0:n], po[hp:hp + 64, 0:n],
                        rden[hp:hp + 64, 0:n], ALU.mult)
                    oap, obf = o_loc_ap(hp, hp + 64, t0, n)
                    S.dma(oap, ob[hp:hp + 64, 0:n], reads=[ob], writes=[obf])
        P.close()


    def phase_GR(l, with_ctx):
        P = Phase()
        wgr = P.tile("wgr", [128, 8, 800], BF16)
        S.dma(wgr[:], w_gr[l].rearrange("(k p) c -> p k c", p=128), writes=[wgr], eng="pool")
        wgt = P.tile("wgt", [16, 2, 64], BF16)
        S.dma(wgt[:], gla_wg[l].rearrange("a k c -> k a c"), writes=[wgt], eng="pool")
        nb = P.tile("nb", [64, 2])
        S.dma(nb[:], gla_bg[l], writes=[nb])
        S.I("act", "mul", [nb], [nb], nb[:], nb[:], -1.0)
        ong = P.tile("ong", [128, 2])
        S.dma(ong[:], on_g[l], writes=[ong])
        onesb = P.tile("onesb", [128, 128], BF16)
        S.I("dve", "memset", [], [onesb], onesb[:], 1.0)
        OF = P.tile("OF", [128, 2, SEQ + CTX], BF16)
        St = P.tile("St", [128, 128])
        Sbf = P.tile("Sbf", [128, 128], BF16)
        tmpS = P.tile("tmpS", [128, 128])
        ET = [[P.tile("E%d%d" % (k_, p_), [128, 512]) for p_ in range(2)] for k_ in range(2)]
        hbs = [P.tile("hb%d" % i, [128, 8, 512], BF16) for i in range(2)]
        cosb = [P.tile("cos%d" % i, [128, 512]) for i in range(2)]
        sinb = [P.tile("sin%d" % i, [128, 512]) for i in range(2)]
        t1 = P.tile("t1", [128, 512])
        t2 = P.tile("t2", [128, 512])
        t3 = P.tile("t3", [128, 512])
        t4 = P.tile("t4", [128, 512])
        qa = P.tile("qa", [128, 512])
        ka = P.tile("ka", [128, 512])
        QP = [P.tile("QP%d" % i, [128, 512], BF16) for i in range(2)]
        KP = [P.tile("KP%d" % i, [128, 512], BF16) for i in range(2)]
        VB = [P.tile("VB%d" % i, [128, 4, 256], BF16) for i in range(2)]
        gT = P.tile("gT", [16, 512], BF16)
        gTl = [P.tile("gTl%d" % i, [16, 512], BF16) for i in range(2)]
        e1 = P.tile("e1", [64, 512])
        nl = P.tile("nl", [64, 512])
        cum = P.tile("cum", [64, 512])
        rn = P.tile("rn", [64, 512])
        onesc = P.tile("onesc", [64, 128])
        S.I("dve", "memset", [], [onesc], onesc[:], 1.0)
        Am = [P.tile("Am%d" % i, [128, 2, 128], BF16) for i in range(2)]
        ktok = [P.tile("ktok%d" % i, [128, 128], BF16) for i in range(2)]
        o32s = [P.tile("o32_%d" % i, [128, 512]) for i in range(2)]
        sqos = [P.tile("sqo%d" % i, [128, 512], BF16) for i in range(2)]
        rsos = [P.tile("rso%d" % i, [128, 512]) for i in range(2)]
        onb = [P.tile("onb%d" % i, [128, 512], BF16) for i in range(2)]
        psO = [PSA[0], PSA[1]]
        PSBf = Buf("psbf", None)
        gen = [PSA[2], PSA[3]]
        gi = [0]

        def pg():
            gi[0] = (gi[0] + 1) % 2
            return gen[gi[0]]

        cnt = [0]
        pending_ag = []
        deferred = []
        for direction in range(2):
            if direction == 0:
                blocks = [-1] + list(range(16))
            else:
                blocks = [-1] + list(range(15, -1, -1))
            S.I("dve", "memset", [], [St], St[:], 0.0)
            S.I("dve", "memset", [], [Sbf], Sbf[:], 0.0)
            for k_ in range(2):
                for p_ in range(2):
                    S.dma(ET[k_][p_][:], econst[direction * 2 + k_], writes=[ET[k_][p_]])
            def issue_loads(bi2):
                tb2 = blocks[bi2]
                par2 = bi2 % 2
                if direction == 1:
                    n2 = CTX if tb2 < 0 else 512
                    t02 = SEQ if tb2 < 0 else tb2 * 512
                    S.dma(cosb[par2][:, 0:n2], qk_scr[0][:, t02:t02 + n2], reads=[scr_buf("q", tb2)], writes=[cosb[par2]])
                    S.dma(sinb[par2][:, 0:n2], qk_scr[1][:, t02:t02 + n2], reads=[scr_buf("k", tb2)], writes=[sinb[par2]])
                    S.dma(gTl[par2][:, 0:n2], gb_scr[:, t02:t02 + n2], reads=[scr_buf("g", tb2)], writes=[gTl[par2]])
                    return
                if tb2 < 0:
                    return
                load_hb(hbs[par2], tb2)
                S.dma(cosb[par2][:], rope[0][:, tb2 * 512:(tb2 + 1) * 512], writes=[cosb[par2]])
                S.dma(sinb[par2][:], rope[1][:, tb2 * 512:(tb2 + 1) * 512], writes=[sinb[par2]])

            def issue_v(bi2):
                tb2 = blocks[bi2]
                par2 = bi2 % 2
                n2 = CTX if tb2 < 0 else 512
                t02 = SEQ if tb2 < 0 else tb2 * 512
                S.dma(VB[par2][:, 0:n2 // 128, :], v_scr[t02:t02 + n2, :].rearrange("(t p) c -> p t c", p=128),
                      reads=[scr_buf("v", tb2)], writes=[VB[par2]])

            def prologue(bi):
                tb = blocks[bi]
                isctx = tb < 0
                n = CTX if isctx else 512
                nch = n // 128
                par = bi % 2
                hsrc = hTc if isctx else hbs[par]
                EQ, EK = ET[0][par], ET[1][par]
                t0s = SEQ if isctx else tb * 512
                pqp_q = None
                if direction == 0:
                    pqp_q = pg()
                    proj_fm(pqp_q, wgr, 256, 128, hsrc, 0, n)
                    S.I("act", "activation", [pqp_q], [gT], gT[:, 0:n], pqp_q[0:16, 0:n], AF.Copy)
                    gsrc = gT
                    S.I("dve", "tensor_copy", [pqp_q], [gTl[par]], gTl[par][:, 0:n], pqp_q[32:48, 0:n])
                    S.dma(gb_scr[:, t0s:t0s + n], gTl[par][:, 0:n], reads=[gTl[par]], writes=[scr_buf("g", tb)])
                    if not isctx:
                        S.I("dve", "tensor_tensor", [pqp_q, sinb[par]], [t2], t2[:], pqp_q[:, :], sinb[par][:], ALU.mult)
                else:
                    gsrc = gTl[par]
                pL = pg()
                S.I("pe", "matmul", [wgt, gsrc], [pL], pL[0:64, 0:n], wgt[:, direction, :], gsrc[:, 0:n], start=True, stop=True)
                S.I("act", "activation", [pL, nb], [e1], e1[:, 0:n], pL[0:64, 0:n], AF.Exp, bias=nb[:, direction:direction + 1], scale=-1.0)
                S.I("act", "activation", [e1], [nl], nl[:, 0:n], e1[:, 0:n], AF.Ln, bias=1.0)
                for c in range(nch):
                    cs = slice(c * 128, (c + 1) * 128)
                    S.I("dve", "tensor_tensor_scan", [onesc, nl], [cum], cum[:, cs], onesc[:], nl[:, cs], 0.0, ALU.mult, ALU.add)
                if direction == 0:
                    src_, sb_ = cum, cum
                else:
                    S.I("dve", "tensor_tensor", [nl, cum], [rn], rn[:, 0:n], nl[:, 0:n], cum[:, 0:n], ALU.subtract)
                    for c in range(nch):
                        cs = slice(c * 128, (c + 1) * 128)
                        S.I("dve", "tensor_scalar", [rn, cum], [rn], rn[:, cs], rn[:, cs], cum[:, c * 128 + 127:c * 128 + 128], None, ALU.add)
                    src_, sb_ = rn, rn
                S.I("act", "activation", [sb_], [EQ], EQ[0:64, 0:n], src_[:, 0:n], AF.Exp, scale=-1.0 / 16)
                S.I("act", "activation", [sb_], [EK], EK[0:64, 0:n], src_[:, 0:n], AF.Exp, scale=1.0 / 16)
                qp, kp = QP[par], KP[par]
                for qi_, (c0, c0p, dst, E_, ta, tb_, acc) in enumerate(((0, 256, qp, EQ, t1, t2, qa), (128, 384, kp, EK, t3, t4, ka))):
                    if direction == 1:
                        stag = cosb[par] if qi_ == 0 else sinb[par]
                        S.I("dve", "tensor_tensor", [stag, E_], [dst], dst[:, 0:n], stag[:, 0:n], E_[:, 0:n], ALU.mult)
                        continue
                    pq = pg()
                    proj_fm(pq, wgr, c0, 128, hsrc, 0, n)
                    if isctx:
                        S.I("dve", "tensor_copy", [pq], [acc], acc[:, 0:n], pq[:, 0:n])
                        S.dma(qk_scr[qi_][:, t0s:t0s + n], acc[:, 0:n], reads=[acc], writes=[scr_buf("qk"[qi_], tb)])
                        S.I("dve", "tensor_tensor", [acc, E_], [dst], dst[:, 0:n], acc[:, 0:n], E_[:, 0:n], ALU.mult)
                    else:
                        S.I("dve", "tensor_tensor", [pq, cosb[par]], [ta], ta[:], pq[:, :], cosb[par][:], ALU.mult)
                        if qi_ == 1:
                            pqp = pg()
                            proj_fm(pqp, wgr, c0p, 128, hsrc, 0, n)
                            S.I("dve", "tensor_tensor", [pqp, sinb[par]], [tb_], tb_[:], pqp[:, :], sinb[par][:], ALU.mult)
                        S.I("dve", "tensor_tensor", [ta, tb_], [acc], acc[:], ta[:], tb_[:], ALU.add)
                        S.dma(qk_scr[qi_][:, t0s:t0s + n], acc[:, 0:n], reads=[acc], writes=[scr_buf("qk"[qi_], tb)])
                        S.I("dve", "tensor_tensor", [acc, E_], [dst], dst[:], acc[:], E_[:], ALU.mult)
                vb = VB[par]
                for half in range((nch + 1) // 2 if direction == 0 else 0):
                    pv = pg()
                    for s2 in range(2):
                        s_ = half * 2 + s2
                        if s_ >= nch:
                            continue
                        for k in range(8):
                            S.I("pe", "matmul", [hsrc, wgr], [pv], pv[:, s2 * 256:(s2 + 1) * 256], hsrc[:, k, s_ * 128:(s_ + 1) * 128],
                                wgr[:, k, 544:800], start=(k == 0), stop=(k == 7))
                    S.I("act", "activation", [pv], [vb], vb[:, half * 2:half * 2 + 2, :],
                        pv[:, :].rearrange("p (s c) -> p s c", c=256), AF.Copy)
                if direction == 0:
                    S.dma(v_scr[t0s:t0s + n, :].rearrange("(t p) c -> p t c", p=128), vb[:, 0:nch, :],
                          reads=[vb], writes=[scr_buf("v", tb)])

            for bi, tb in enumerate(blocks):
                isctx = tb < 0
                n = CTX if isctx else 512
                nch = n // 128
                par = bi % 2
                if bi == 0:
                    issue_loads(0)
                    issue_loads(1)
                    if direction == 1:
                        issue_v(0)
                    prologue(0)
                if direction == 1 and bi + 1 < len(blocks):
                    issue_v(bi + 1)
                if bi + 2 < len(blocks):
                    issue_loads(bi + 2)
                if bi + 1 < len(blocks):
                    prologue(bi + 1)
                for (p_, when) in list(pending_ag):
                    if when <= bi:
                        allgather(o_loc[p_], o_all[p_], b_o_loc[p_], b_o_all[p_])
                        pending_ag.remove((p_, when))
                EQ, EK = ET[0][par], ET[1][par]
                qp, kp, vb = QP[par], KP[par], VB[par]
                for fn_ in deferred:
                    fn_()
                deferred = []
                chunks = list(range(nch)) if direction == 0 else list(range(nch - 1, -1, -1))
                for ci, c in enumerate(chunks):
                    cs = slice(c * 128, (c + 1) * 128)
                    cnt[0] += 1
                    am, kt_ = Am[cnt[0] % 2], ktok[cnt[0] % 2]
                    S.I("pe", "matmul", [kp, qp], [PS2], PS2[:, 0:128], kp[0:64, cs], qp[0:64, cs], start=True, stop=True)
                    S.I("pe", "matmul", [kp, qp], [PS2], PS2[:, 512:640], kp[64:128, cs], qp[64:128, cs], start=True, stop=True)
                    S.I("dve", "tensor_tensor", [PS2, tri], [am], am[:],
                        PS2[:, :].rearrange("p (a b) -> p a b", b=512)[:, :, 0:128],
                        tri[:, direction:direction + 1, :].to_broadcast([128, 2, 128]), ALU.mult)
                    pt = PSB[0]
                    S.I("pe", "transpose", [kp, ident], [pt], pt[:, 0:128], kp[:, cs], ident[:])
                    S.I("act", "activation", [pt], [kt_], kt_[:], pt[:, 0:128], AF.Copy)
                    for mx in range(2):
                        r0 = mx * 64
                        po_ = psO[mx]
                        S.I("pe", "matmul", [vb, am], [po_], po_[:, cs], vb[:, c, mx * 128:(mx + 1) * 128], am[:, mx, :], start=True, stop=False)
                        S.I("pe", "matmul", [Sbf, qp], [po_], po_[:, cs], Sbf[r0:r0 + 64, :], qp[r0:r0 + 64, cs], start=False, stop=True)
                    pd = pg()
                    S.I("pe", "matmul", [kt_, vb], [pd], pd[0:64, 0:128], kt_[:, 0:64], vb[:, c, 0:128], start=True, stop=True)
                    S.I("pe", "matmul", [kt_, vb], [pd], pd[64:128, 0:128], kt_[:, 64:128], vb[:, c, 128:256], start=True, stop=True)
                    dcol = c * 128 + 127 if direction == 0 else c * 128
                    S.I("dve", "tensor_tensor", [pd, St], [tmpS], tmpS[:], pd[:, 0:128], St[:], ALU.add)
                    S.I("act", "activation", [tmpS, EQ], [Sbf], Sbf[:], tmpS[:], AF.Copy, scale=EQ[:, dcol:dcol + 1])
                    S.I("dve", "tensor_scalar", [tmpS, EQ], [St], St[:], tmpS[:], EQ[:, dcol:dcol + 1], None, ALU.mult)
                t0 = SEQ if isctx else tb * 512
                if isctx and not with_ctx:
                    continue
                for mx in range(2):
                    po_ = psO[mx]
                    o32 = o32s[mx]
                    if direction == 0:
                        S.I("act", "activation", [po_], [OF], OF[:, mx, t0:t0 + n], po_[:, 0:n], AF.Copy, scale=0.125)
                    else:
                        S.I("dve", "scalar_tensor_tensor", [po_, OF], [o32], o32[:, 0:n], po_[:, 0:n], 0.125, OF[:, mx, t0:t0 + n], ALU.mult, ALU.add)

                        def norm_out(mx=mx, n=n, t0=t0):
                            o32, sqo, rso = o32s[mx], sqos[mx], rsos[mx]
                            S.I("act", "activation", [o32], [sqo], sqo[:, 0:n], o32[:, 0:n], AF.Square)
                            p2 = pg()
                            S.I("pe", "matmul", [onesb, sqo], [p2], p2[:, 0:n], onesb[:], sqo[:, 0:n], start=True, stop=True)
                            S.I("act", "activation", [p2], [rso], rso[:, 0:n], p2[:, 0:n], AF.Ln, bias=EPS, scale=1.0 / 128)
                            S.I("act", "activation", [rso], [rso], rso[:, 0:n], rso[:, 0:n], AF.Exp, scale=-0.5)
                            ob = onb[mx]
                            S.I("dve", "scalar_tensor_tensor", [o32, ong, rso], [ob], ob[:, 0:n], o32[:, 0:n], ong[:, mx:mx + 1], rso[:, 0:n], ALU.mult, ALU.mult)
                            oap, obf = o_loc_ap(128 + mx * 128, 256 + mx * 128, t0, n)
                            S.dma(oap, ob[:, 0:n], reads=[ob], writes=[obf])
                        deferred.append(norm_out)
                if direction == 1 and not isctx and tb % 2 == 0:
                    pending_ag.append((tb // 2, bi + 2))
                if direction == 1 and isctx:
                    pending_ag.append((8, bi + 2))
            for fn_ in deferred:
                fn_()
            deferred = []
            for (p_, when) in pending_ag:
                allgather(o_loc[p_], o_all[p_], b_o_loc[p_], b_o_all[p_])
            pending_ag = []
        P.close()


    def phase_D(l, with_ctx, NH=2):
        P = Phase()
        modT = modTs[l]
        W = TOK // NH
        CW = CTX // NH
        NT = W + (CW if with_ctx else 0)
        arena = [P.tile("arena%d" % i, [128, 8, 512], BF16) for i in range(2)]
        acc = P.tile("acc", [128, 512])
        tmq = [P.tile("tmq%d" % i, [128, 512]) for i in range(2)]
        gates = []
        for t in range(2 if with_ctx else 1):
            gb = P.tile("gate%d" % t, [128, D])
            for cb in range(2):
                dg = tmq[cb]
                for k4 in range(4):
                    k = cb * 4 + k4
                    S.I("dve", "tensor_scalar", [identf, modT], [dg], dg[:, k4 * 128:(k4 + 1) * 128], identf[:],
                        modT[:, 16 + k, t:t + 1], None, ALU.mult)
                pg_ = psa()
                S.I("pe", "matmul", [onesf, dg], [pg_], pg_[:, :], onesf[:], dg[:], start=True, stop=True)
                S.I("act", "activation", [pg_], [gb], gb[:, cb * 512:(cb + 1) * 512], pg_[:, :], AF.Copy)
            gates.append(gb)
        hTd = P.tile("hTd", [128, 8, W], BF16)
        OU = P.tile("OU", [128, 12, NT], BF16)
        Y = P.tile("Y", [128, 8, NT], BF16)
        stgs = [P.tile("stg%d" % i, [128, 4, 512], BF16) for i in range(3)]
        wbrs = [P.tile("wbr%d" % i, [128, 3, 4, 128], BF16) for i in range(2)]
        wgs = [P.tile("wg%d" % i, [128, 3, 8, 128], BF16) for i in range(2)]
        szb = [P.tile("sz%d" % i, [128, 512], BF16) for i in range(2)]
        sgb = [P.tile("sg%d" % i, [128, 512]) for i in range(2)]
        cnt = [0]
        nblk = W // 512

        def load_z(z4):
            a = arena[z4 % 2]
            for k in range(8):
                S.dma(a[:, k, :], w_zg[l][k * 128:(k + 1) * 128, z4 * 512:(z4 + 1) * 512], writes=[a], eng="pool", cw=True)

        def load_oc(oc):
            wbr, wg = wbrs[oc % 2], wgs[oc % 2]
            for br in range(3):
                S.dma(wbr[:, br, :, :], w_br[l, br][:, oc * 128:(oc + 1) * 128].rearrange("(s p) c -> p s c", p=128),
                      writes=[wbr], eng="pool", cw=True)
                gc0 = 1536 + br * 1024 + oc * 128
                S.dma(wg[:, br, :, :], w_zg[l][:, gc0:gc0 + 128].rearrange("(k p) c -> p k c", p=128),
                      writes=[wg], eng="pool", cw=True)

        def load_wo():
            for cb in range(2):
                for k in range(8):
                    S.dma(arena[cb][:, k, :], w_o[l][k * 128:(k + 1) * 128, cb * 512:(cb + 1) * 512], writes=[arena[cb]], eng="pool", cw=True)

        for part in range(NH):
            blocks = []
            for b_ in range(nblk):
                blocks.append((hTd, b_ * 512, b_ * 512, 512))
            if with_ctx:
                blocks.append((hTc, part * CW, W, CW))
            load_z(0)
            load_z(1)
            for b_ in range(nblk):
                gblk = part * nblk + b_
                S.dma(hTd[:, :, b_ * 512:(b_ + 1) * 512], hT_loc[gblk].rearrange("(k p) n -> p k n", p=128),
                      reads=[b_hT_loc[gblk]], writes=[hTd], cw=True)
            for cidx in range(12):
                src_, br = cidx // 3, cidx % 3
                r0 = src_ * 384 + br * 128
                for b_ in range(nblk):
                    cnt[0] += 1
                    stg = stgs[cnt[0] % 3]
                    t0 = part * W + b_ * 512
                    for q in range(4):
                        tq = q * TOK + t0
                        pc = tq // 1024
                        S.dma(stg[:, q, :], o_all[pc][r0:r0 + 128, tq % 1024:tq % 1024 + 512], reads=[b_o_all[pc]], writes=[stg], cw=True)
                    dst = OU[:, cidx, b_ * 512:(b_ + 1) * 512]
                    psel = psa()
                    for q in range(4):
                        S.I("pe", "matmul", [dsel, stg], [psel], psel[:, :], dsel[:, q, :], stg[:, q, :], start=(q == 0), stop=(q == 3))
                    S.I("act", "activation", [psel], [OU], dst, psel[:, :], AF.Copy)
                if with_ctx:
                    S.dma(OU[:, cidx, W:W + CW], o_all[8][r0:r0 + 128, part * CW:(part + 1) * CW], reads=[b_o_all[8]], writes=[OU], cw=True)
            for z4 in range(3):
                wz4 = arena[z4 % 2]
                for sub in range(4):
                    zc = z4 * 4 + sub
                    for (hsrc, h0, o0, n) in blocks:
                        ps = psa()
                        proj_fm(ps, wz4, sub * 128, 128, hsrc, h0, n)
                        cnt[0] += 1
                        sz = szb[cnt[0] % 2]
                        S.I("act", "activation", [ps], [sz], sz[:, 0:n], ps[:, 0:n], AF.Silu)
                        S.I("dve", "tensor_tensor", [OU, sz], [OU], OU[:, zc, o0:o0 + n],
                            OU[:, zc, o0:o0 + n], sz[:, 0:n], ALU.mult)
                if z4 == 0:
                    load_z(2)
                    load_oc(0)
            load_oc(1)
            for oc in range(8):
                wbr, wg = wbrs[oc % 2], wgs[oc % 2]
                for (hsrc, h0, o0, n) in blocks:
                    for br in range(3):
                        pB = psa()
                        for s_ in range(4):
                            S.I("pe", "matmul", [wbr, OU], [pB], pB[:, 0:n], wbr[:, br, s_, :], OU[:, s_ * 3 + br, o0:o0 + n],
                                start=(s_ == 0), stop=(s_ == 3))
                        pG = psa()
                        for k in range(8):
                            S.I("pe", "matmul", [wg, hsrc], [pG], pG[:, 0:n], wg[:, br, k, :], hsrc[:, k, h0:h0 + n],
                                start=(k == 0), stop=(k == 7))
                        cnt[0] += 1
                        sg = sgb[cnt[0] % 2]
                        S.I("act", "activation", [pG], [sg], sg[:, 0:n], pG[:, 0:n], AF.Sigmoid)
                        if br == 0:
                            S.I("dve", "tensor_tensor", [pB, sg], [acc], acc[:, 0:n], pB[:, 0:n], sg[:, 0:n], ALU.mult)
                        else:
                            tq_ = tmq[br % 2]
                            S.I("dve", "tensor_tensor", [pB, sg], [tq_], tq_[:, 0:n], pB[:, 0:n], sg[:, 0:n], ALU.mult)
                            if br == 1:
                                S.I("dve", "tensor_tensor", [acc, tq_], [acc], acc[:, 0:n], acc[:, 0:n], tq_[:, 0:n], ALU.add)
                            else:
                                S.I("dve", "tensor_tensor", [acc, tq_], [Y], Y[:, oc, o0:o0 + n], acc[:, 0:n], tq_[:, 0:n], ALU.add)
                if oc == 0:
                    load_wo()
                if oc + 2 < 8:
                    load_oc(oc + 2)
            tiles = []
            for i in range(W // 128):
                tiles.append((xres, xres[:, part * (W // 128) + i, :], 0, 128, i * 128, gates[0]))
            if with_ctx:
                tok0 = part * CW
                tiles.append((xcres, xcres[tok0 % 128:tok0 % 128 + CW, tok0 // 128, :], tok0 % 128, CW, W, gates[1]))
            for (xb_, xap, p0, m, o0, gb) in tiles:
                for cb in range(2):
                    ps = psa()
                    for k in range(8):
                        S.I("pe", "matmul", [Y, arena[cb]], [ps], ps[p0:p0 + m, :], Y[:, k, o0:o0 + m], arena[cb][:, k, :],
                            start=(k == 0), stop=(k == 7))
                    cnt[0] += 1
                    tq_ = tmq[cnt[0] % 2]
                    S.I("dve", "tensor_tensor", [ps, gb], [tq_], tq_[p0:p0 + m, :], ps[p0:p0 + m, :], gb[p0:p0 + m, cb * 512:(cb + 1) * 512], ALU.mult)
                    xs = xap[:, cb * 512:(cb + 1) * 512]
                    S.I("dve", "tensor_tensor", [xb_, tq_], [xb_], xs, xs, tq_[p0:p0 + m, :], ALU.add)
        P.close()

    if stage >= 1:
        phase_mod([0])
    for l in range(depth):
        if stage == 0:
            break
        phase_A(l)
        if l == 0 and depth > 1:
            phase_mod([1])
        if stage == 1:
            break
        phase_N(l, l < DEPTH - 1)
        if stage == 2:
            break
        phase_GR(l, l < DEPTH - 1)
        if stage == 3:
            break
        phase_D(l, l < DEPTH - 1)

    if stage == 1:
        t = nc.dram_tensor("dbg_hT", [4 * D, TOK], BF16, kind="ExternalOutput").ap()
        dbg_out["hT"] = t
        tmp = T("dbgtmp", [128, 32, 512], BF16)
        for q in range(4):
            S.dma(tmp[:], hT_all[q].rearrange("(a p) n -> p a n", p=128), reads=[b_hT_all[q]], writes=[tmp])
            outs.append(S.dma(t[:, q * 512:(q + 1) * 512].rearrange("(a p) n -> p a n", p=128), tmp[:], reads=[tmp]))
        dbg("hTc", hTc, hTc[:], [128, 8, CTX], BF16)
        dbg("modT", modTs[0], modTs[0][:], [128, 24, 2])
    if stage == 3:
        for i_ in range(3):
            tmp = T("dbgtmp%d" % i_, [128, SEQ + CTX], BF16)
            for p in range(9):
                S.dma(tmp[:, p * 1024:p * 1024 + OW[p]], o_loc[p][i_ * 128:(i_ + 1) * 128, :], reads=[b_o_loc[p]], writes=[tmp])
            dbg("o%d" % i_, tmp, tmp[:], [128, SEQ + CTX], BF16)
    if stage == 2:
        tmp = T("dbgtmp", [128, SEQ + CTX], BF16)
        for p in range(9):
            S.dma(tmp[:, p * 1024:p * 1024 + OW[p]], o_loc[p][0:128, :], reads=[b_o_loc[p]], writes=[tmp])
        dbg("ona", tmp, tmp[:], [128, SEQ + CTX], BF16)
    if stage == 4:
        dbg("xc", xcres, xcres[:], [128, 2, D])
    for t in range(16):
        outs.append(S.dma(y_out[t * 128:(t + 1) * 128, :], xres[:, t, :], reads=[xres]))
    S.emit(final_waits=outs)
    return nc, declared


def _rope_tables():
    pos = np.arange(SEQ)
    row = (pos // 64).astype(np.float64)
    col = (pos % 64).astype(np.float64)
    inv = 10000.0 ** (-np.arange(16, dtype=np.float64) / 16)
    cos = np.ones((128, SEQ), np.float64)
    sin = np.zeros((128, SEQ), np.float64)
    for d in range(64):
        half = d // 32
        dd = d % 32
        a = dd % 16
        p = row if half == 0 else col
        ang = p * inv[a]
        cos[64 + d] = np.cos(ang)
        sin[64 + d] = (-1.0 if dd < 16 else 1.0) * np.sin(ang)
    return np.stack([cos, sin]).astype(np.float32)


def _rope_perm():
    perm = np.zeros(64, np.int64)
    for d in range(64):
        base = (d // 32) * 32
        dd = d % 32
        perm[d] = base + (dd + 16 if dd < 16 else dd - 16)
    return perm


def _econst(j):
    lgf = np.log1p(-2.0 ** (-(5.0 + j)))
    lgb = np.log1p(-2.0 ** (-(5.5 + j)))
    i = np.arange(128, dtype=np.float64)
    rows = [np.exp((i + 1) * lgf), np.exp(-(i + 1) * lgf), np.exp((128 - i) * lgb), np.exp(-(128 - i) * lgb)]
    e = np.zeros((4, 128, 512), np.float32)
    for k in range(4):
        e[k, 64:128, :] = np.tile(rows[k], 4)[None, :].astype(np.float32)
    return e


def _na_bias(rpb_h):
    kc = np.arange(64)[:, None]
    qc = np.arange(64)[None, :]
    cs = np.clip(qc - 8, 0, 48)
    colok = (kc >= cs) & (kc <= cs + 15)
    cidx = np.clip(kc - qc + 15, 0, 30)
    strip = np.full((128, 22 * 64), NEG, np.float32)
    for a in range(2):
        for e in range(22):
            dr = a - e + 10
            if -4 <= dr <= 3:
                blk = np.where(colok, rpb_h[dr + 7][cidx], NEG)
                strip[a * 64:(a + 1) * 64, e * 64:(e + 1) * 64] = blk
    edge = np.full((12, 128, 512), NEG, np.float32)
    idx = 0
    for m, kts in ((0, range(2, 8)), (15, range(0, 6))):
        for kt in kts:
            for a in range(2):
                kr = 8 * m - 4 + 2 * kt + a
                for qi in range(8):
                    qr = 8 * m + qi
                    rs = min(max(qr - 4, 0), 120)
                    if rs <= kr <= rs + 7:
                        blk = np.where(colok, rpb_h[kr - qr + 7][cidx], NEG)
                        edge[idx, a * 64:(a + 1) * 64, qi * 64:(qi + 1) * 64] = blk
            idx += 1
    return strip, edge


def make_in_maps(inp):
    f32 = np.float32
    x, c, ctx, c_ctx = inp["x"], inp["c"], inp["ctx"], inp["c_ctx"]
    w_in = inp["w_in"]
    L = DEPTH
    rope = _rope_tables()
    perm = _rope_perm()
    ident = np.eye(128, dtype=f32)
    tri = np.stack([np.triu(np.ones((128, 128), f32)), np.tril(np.ones((128, 128), f32))])
    bones = np.zeros((128, 128), f32)
    bones[:64, :64] = 1
    bones[64:, 64:] = 1
    b_modT = np.ascontiguousarray(inp["b_mod"].reshape(L, 24, 128).transpose(0, 2, 1))
    norm_wT = np.ascontiguousarray(inp["norm_w"].reshape(L, 8, 128).transpose(0, 2, 1))
    w_mod = np.ascontiguousarray(inp["w_mod"])
    w_br = np.ascontiguousarray(inp["w_branch"])
    w_o = np.ascontiguousarray(inp["w_out"])
    zcols = np.concatenate([np.arange(3616 + br * 512 + s * 128, 3616 + br * 512 + s * 128 + 128)
                            for s in range(4) for br in range(3)])
    w_zg = np.ascontiguousarray(np.concatenate([w_in[:, :, zcols], w_in[:, :, 5152:8224]], axis=2))
    maps = []
    for core in range(8):
        b, j = core // 4, core % 4
        cv = np.stack([c[b], c_ctx])
        cTm = np.ascontiguousarray(cv.T.reshape(8, 128, 2).transpose(1, 0, 2))
        sl = lambda o, n: slice(o, o + n)
        w_na = np.concatenate([w_in[:, :, sl(128 * j, 128)], w_in[:, :, sl(512 + 128 * j, 128)],
                               w_in[:, :, sl(1024 + 128 * j, 128)]], axis=2)
        glaq = w_in[:, :, sl(1536 + 64 * j, 64)]
        glak = w_in[:, :, sl(1792 + 64 * j, 64)]
        retq = w_in[:, :, sl(2592 + 64 * j, 64)]
        retk = w_in[:, :, sl(2848 + 64 * j, 64)]
        z16 = np.zeros_like(w_in[:, :, 0:16])
        gpad = np.concatenate([w_in[:, :, 2560:2576], z16, w_in[:, :, 2576:2592], z16], axis=2)
        w_gr = np.concatenate([glaq, retq, glak, retk, gpad, retq[:, :, perm], glak, retk[:, :, perm],
                               w_in[:, :, 2560:2592], w_in[:, :, sl(2048 + 128 * j, 128)],
                               w_in[:, :, sl(3104 + 128 * j, 128)]], axis=2)
        na_g = np.stack([np.tile(inp["na_q_norm"], (1, 2)), np.tile(inp["na_k_norm"], (1, 2))], axis=2)
        strips = np.zeros((L, 2, 128, 22 * 64), f32)
        edges = np.zeros((L, 2, 12, 128, 512), f32)
        for l in range(L):
            for h in range(2):
                strips[l, h], edges[l, h] = _na_bias(inp["na_rpb"][l, 2 * j + h])
        gla_wg = np.ascontiguousarray(inp["gla_w_gate"][:, :, :, 64 * j:64 * j + 64])
        gla_bg = np.ascontiguousarray(inp["gla_b_gate"][:, :, 64 * j:64 * j + 64].transpose(0, 2, 1))
        on_g = np.stack([inp["gla_out_norm"][:, 128 * j:128 * j + 128], inp["ret_out_norm"][:, 128 * j:128 * j + 128]], axis=2)
        selv = np.zeros((128, 4), f32)
        selv[:, j] = 1.0
        maps.append({
            "x_sh": np.ascontiguousarray(x[b, TOK * j:TOK * (j + 1)]),
            "ctx_b": np.ascontiguousarray(ctx[b]),
            "cT": cTm, "w_mod": w_mod, "b_modT": b_modT, "norm_wT": norm_wT,
            "w_na": np.ascontiguousarray(w_na), "w_gr": np.ascontiguousarray(w_gr), "w_zg": w_zg,
            "w_br": w_br, "w_o": w_o, "na_g": np.ascontiguousarray(na_g.astype(f32)),
            "na_strip": strips, "na_edge": edges, "gla_wg": gla_wg, "gla_bg": gla_bg,
            "on_g": np.ascontiguousarray(on_g.astype(f32)), "econst": _econst(j), "rope": rope, "sel": selv,
            "c_ident": ident.astype(ml_dtypes.bfloat16), "c_identf": ident,
            "c_tri": tri.astype(ml_dtypes.bfloat16), "c_bones": bones.astype(ml_dtypes.bfloat16),
        })
    return maps


_CACHE = {}


def kernel(**inputs):
    inp = {k: np.asarray(v) for k, v in inputs.items()}
    maps = make_in_maps(inp)
    if "nc" not in _CACHE:
        _CACHE["nc"] = build_program()[0]
    res = run_bass_kernel_spmd(_CACHE["nc"], maps, core_ids=list(range(8)))
    out = np.zeros((2, SEQ, D), np.float32)
    for core in range(8):
        b, j = core // 4, core % 4
        out[b, TOK * j:TOK * (j + 1)] = res.results[core]["y"]
    return out
```

```python
import os
import contextlib
import numpy as np
import ml_dtypes
import concourse.bass as bass
import concourse.mybir as mybir
from concourse.bass_utils import run_bass_kernel_spmd

F32 = mybir.dt.float32
BF16 = mybir.dt.bfloat16
AF = mybir.ActivationFunctionType
ALU = mybir.AluOpType

COMPUTE = ("pe", "act", "dve", "pool")
ALL_ENG = COMPUTE + ("sp",)

D = 1024
SEQ = 8192
CTX = 256
TOK = 2048
DEPTH = 2
GROUPS = [[0, 1, 2, 3], [4, 5, 6, 7]]
EPS = 1e-6
NEG = -30000.0


class Buf:
    __slots__ = ("name", "t", "last_ws", "readers")

    def __init__(self, name, t=None):
        self.name = name
        self.t = t
        self.last_ws = []
        self.readers = []

    def __getitem__(self, k):
        return self.t[k]


class Op:
    __slots__ = ("eng", "fn", "deps", "is_dma", "sem", "semval", "signal", "inc", "is_cc")

    def __init__(self, eng, fn, is_dma):
        self.eng = eng
        self.fn = fn
        self.deps = []
        self.is_dma = is_dma
        self.sem = None
        self.semval = None
        self.signal = False
        self.inc = 1
        self.is_cc = False


class Sched:
    def __init__(self, nc, n_dma_sems=32):
        self.nc = nc
        self.ops = {e: [] for e in ALL_ENG}
        self.n_dma_sems = n_dma_sems
        self.dma_rr = 0
        self.dma_last = [None] * n_dma_sems
        self.dma_count = [0] * n_dma_sems
        self.bar_deps = {}
        self.n_ops = 0

    def op(self, eng, fn, reads=(), writes=(), dma=False, inc=16, cw=False):
        o = Op(eng, fn, dma)
        deps = []
        for b in reads:
            deps.extend(b.last_ws)
        for b in writes:
            if not (cw and not b.readers):
                deps.extend(b.last_ws)
            deps.extend(b.readers)
        if eng in self.bar_deps:
            deps.extend(self.bar_deps.pop(eng))
        if dma:
            k = self.dma_rr
            self.dma_rr = (self.dma_rr + 1) % self.n_dma_sems
            prev = self.dma_last[k]
            if prev is not None:
                deps.append(prev)
            self.dma_last[k] = o
            self.dma_count[k] += inc
            o.sem = ("dma", k)
            o.semval = self.dma_count[k]
            o.inc = inc
            o.signal = True
        seen = set()
        for d in deps:
            if d is o or id(d) in seen:
                continue
            seen.add(id(d))
            if (not d.is_dma) and d.eng == eng:
                if eng == "pe" or eng == "sp":
                    continue
                if not any((d in b.last_ws) for b in reads):
                    continue
            o.deps.append(d)
        for b in reads:
            b.readers.append(o)
        for b in writes:
            if cw and not b.readers:
                b.last_ws.append(o)
            else:
                b.last_ws = [o]
            b.readers = []
        self.ops[eng].append(o)
        self.n_ops += 1
        return o

    def I(self, eng, name, reads, writes, *a, **kw):
        return self.op(eng, lambda e: getattr(e, name)(*a, **kw), reads, writes)

    def dma(self, out_ap, in_ap, reads=(), writes=(), eng="sp", cw=False, **kw):
        return self.op(eng, lambda e: e.dma_start(out=out_ap, in_=in_ap, **kw), reads, writes, dma=True, cw=cw)

    def barrier(self):
        last = []
        for e in ALL_ENG:
            for o in reversed(self.ops[e]):
                if not o.is_dma:
                    last.append(o)
                    break
        last.extend(o for o in self.dma_last if o is not None and not o.is_cc)
        for e in ALL_ENG:
            self.bar_deps[e] = list(last) + self.bar_deps.get(e, [])

    def emit(self, final_waits=()):
        nc = self.nc
        for e in ALL_ENG:
            for o in self.ops[e]:
                for d in o.deps:
                    if not d.is_dma:
                        d.signal = True
        for e in ALL_ENG:
            c = 0
            for o in self.ops[e]:
                if o.is_dma:
                    continue
                if o.signal:
                    c += 1
                    o.sem = ("eng", e)
                    o.semval = c
        sems = {}
        with contextlib.ExitStack() as st:
            for e in ALL_ENG:
                sems[("eng", e)] = st.enter_context(nc.semaphore("s_" + e))
            for k in range(self.n_dma_sems):
                sems[("dma", k)] = st.enter_context(nc.semaphore("s_dma%d" % k))
            block = st.enter_context(nc.Block())
            handles = {"pe": block.tensor, "act": block.scalar, "dve": block.vector,
                       "pool": block.gpsimd, "sp": block.sync}

            def make(e):
                def body(eng):
                    known = {}
                    for o in self.ops[e]:
                        need = {}
                        for d in o.deps:
                            if known.get(d.sem, 0) >= d.semval:
                                continue
                            if need.get(d.sem, 0) < d.semval:
                                need[d.sem] = d.semval
                        for s, v in need.items():
                            eng.wait_ge(sems[s], v)
                            known[s] = v
                        ins = o.fn(eng)
                        if o.signal:
                            ins.then_inc(sems[o.sem], o.inc if o.is_dma else 1)
                    if e == "sp":
                        need = {}
                        for d in final_waits:
                            if need.get(d.sem, 0) < d.semval:
                                need[d.sem] = d.semval
                        for s, v in need.items():
                            if known.get(s, 0) < v:
                                eng.wait_ge(sems[s], v)
                return body

            for e in ALL_ENG:
                handles[e](make(e))


def build_program(stage=99, depth=DEPTH):
    nc = bass.Bass("TRN2", target_bir_lowering=False)
    S = Sched(nc)
    L = DEPTH

    declared = []
    need = {"x_sh": 0, "ctx_b": 0, "cT": 0, "w_mod": 1, "b_modT": 1, "norm_wT": 1, "w_na": 2, "w_gr": 3, "w_zg": 4,
            "w_br": 4, "w_o": 4, "na_g": 2, "na_strip": 2, "na_edge": 2, "gla_wg": 3, "gla_bg": 3, "on_g": 3,
            "econst": 3, "rope": 3, "sel": 0, "c_ident": 0, "c_identf": 0, "c_tri": 0, "c_bones": 0}

    def din(name, shape, dt=F32):
        if stage < need[name]:
            return None
        declared.append(name)
        return nc.dram_tensor(name, list(shape), dt, kind="ExternalInput").ap()

    x_sh = din("x_sh", [TOK, D])
    ctx_b = din("ctx_b", [CTX, D])
    cT = din("cT", [128, 8, 2])
    w_mod = din("w_mod", [L, D, 3 * D])
    b_modT = din("b_modT", [L, 128, 24])
    norm_wT = din("norm_wT", [L, 128, 8])
    w_na = din("w_na", [L, D, 384])
    w_gr = din("w_gr", [L, D, 800])
    w_zg = din("w_zg", [L, D, 4608])
    w_br = din("w_br", [L, 3, 512, D])
    w_o = din("w_o", [L, D, D])
    na_g = din("na_g", [L, 128, 2])
    na_strip = din("na_strip", [L, 2, 128, 22 * 64])
    na_edge = din("na_edge", [L, 2, 12, 128, 512])
    gla_wg = din("gla_wg", [L, 2, 16, 64])
    gla_bg = din("gla_bg", [L, 64, 2])
    on_g = din("on_g", [L, 128, 2])
    econst = din("econst", [4, 128, 512])
    rope = din("rope", [2, 128, SEQ])
    sel_in = din("sel", [128, 4])
    c_ident = din("c_ident", [128, 128], BF16)
    c_identf = din("c_identf", [128, 128])
    c_tri = din("c_tri", [2, 128, 128], BF16)
    c_bones = din("c_bones", [128, 128], BF16)
    y_out = nc.dram_tensor("y", [TOK, D], F32, kind="ExternalOutput").ap()
    dbg_out = {}

    hT_loc = [nc.dram_tensor("hT_loc%d" % i, [D, 512], BF16).ap() for i in range(4)]
    hT_all = [nc.dram_tensor("hT_all%d" % i, [4 * D, 512], BF16).ap() for i in range(4)]
    b_hT_loc = [Buf("hT_loc%d" % i) for i in range(4)]
    b_hT_all = [Buf("hT_all%d" % i) for i in range(4)]
    OW = [1024] * 8 + [CTX]
    o_loc = [nc.dram_tensor("o_loc%d" % i, [384, OW[i]], BF16).ap() for i in range(9)]
    o_all = [nc.dram_tensor("o_all%d" % i, [4 * 384, OW[i]], BF16).ap() for i in range(9)]
    b_o_loc = [Buf("o_loc%d" % i) for i in range(9)]
    b_o_all = [Buf("o_all%d" % i) for i in range(9)]

    def o_loc_ap(r0, r1, t0, n):
        if t0 >= SEQ:
            return o_loc[8][r0:r1, t0 - SEQ:t0 - SEQ + n], b_o_loc[8]
        p = t0 // 1024
        return o_loc[p][r0:r1, t0 % 1024:t0 % 1024 + n], b_o_loc[p]

    NTT = SEQ + CTX
    qk_scr = nc.dram_tensor("qk_scr", [2, 128, NTT], F32).ap()
    v_scr = nc.dram_tensor("v_scr", [NTT, 256], BF16).ap()
    gb_scr = nc.dram_tensor("gb_scr", [16, NTT], BF16).ap()
    b_scr = {}

    def scr_buf(kind, tb):
        return b_scr.setdefault((kind, tb), Buf("scr_%s_%d" % (kind, tb)))

    def allgather(src, dst, bsrc, bdst):
        o_ = S.op("pool", lambda e: e.collective_compute("AllGather", ALU.bypass, replica_groups=GROUPS,
                                                          ins=[src], outs=[dst]),
                  reads=[bsrc], writes=[bdst], dma=True, inc=1)
        o_.is_cc = True

    uid = [0]

    def T(name, shape, dt=F32):
        uid[0] += 1
        return Buf(name, nc.alloc_sbuf_tensor("%s_%d" % (name, uid[0]), list(shape), dt))

    class Phase:
        def __init__(self):
            self.st = contextlib.ExitStack()

        def tile(self, name, shape, dt=F32):
            uid[0] += 1
            return Buf(name, self.st.enter_context(nc.sbuf_tensor("%s_%d" % (name, uid[0]), list(shape), dt)))

        def close(self):
            S.barrier()
            self.st.close()

    PSA = [Buf("psa%d" % i, nc.alloc_psum_tensor("psa%d" % i, [128, 512], F32)) for i in range(4)]
    PS2 = Buf("ps2", nc.alloc_psum_tensor("ps2", [128, 1024], F32))
    PSB = [Buf("psb%d" % i, nc.alloc_psum_tensor("psb%d" % i, [128, 1024], BF16)) for i in range(2)]
    rr = {"a": 0, "b": 0}

    def psa():
        rr["a"] = (rr["a"] + 1) % 4
        return PSA[rr["a"]]

    def psb():
        rr["b"] = (rr["b"] + 1) % 2
        return PSB[rr["b"]]

    xres = T("xres", [128, 16, D])
    xcres = T("xcres", [128, 2, D])
    hTc = T("hTc", [128, 8, CTX], BF16)
    ident = T("ident", [128, 128], BF16)
    identf = T("identf", [128, 128])
    onesf = T("onesf", [128, 128])
    tri = T("tri", [128, 2, 128], BF16)
    bones = T("bones", [128, 128], BF16)
    sel = T("sel", [128, 4])
    cTs = T("cTs", [128, 8, 2])
    modTs = [T("modT%d" % i, [128, 24, 2]) for i in range(DEPTH)]
    weffs = [T("weff%d" % i, [128, 8, 2]) for i in range(DEPTH)]
    small = T("small", [128, 64])

    for t in range(16):
        S.dma(xres[:, t, :], x_sh[t * 128:(t + 1) * 128, :], writes=[xres], cw=True)
    S.dma(xcres[:], ctx_b.rearrange("(t p) d -> p t d", p=128), writes=[xcres])
    S.dma(ident[:], c_ident, writes=[ident])
    S.dma(identf[:], c_identf, writes=[identf])
    S.dma(tri[:], c_tri.rearrange("a p n -> p a n"), writes=[tri])
    S.dma(bones[:], c_bones, writes=[bones])
    S.dma(sel[:], sel_in, writes=[sel])
    S.dma(cTs[:], cT, writes=[cTs])
    S.I("dve", "memset", [], [onesf], onesf[:], 1.0)
    dsel = T("dsel", [128, 4, 128], BF16)
    for q_ in range(4):
        S.I("dve", "tensor_scalar", [ident, sel], [dsel], dsel[:, q_, :], ident[:], sel[:, q_:q_ + 1], None, ALU.mult)

    outs = []

    def dbg(name, buf, ap, shape, dt=F32):
        t = nc.dram_tensor("dbg_" + name, list(shape), dt, kind="ExternalOutput").ap()
        dbg_out[name] = t
        outs.append(S.dma(t, ap, reads=[buf]))

    def phase_mod(layers):
        P = Phase()
        sT = P.tile("sT", [128, 8, 2])
        wmb = [P.tile("wm%d" % i, [128, 8, 512]) for i in range(3)]
        S.I("act", "activation", [cTs], [sT], sT[:], cTs[:], AF.Silu)
        it = 0
        for l in layers:
            modT, weff = modTs[l], weffs[l]
            bmt = P.tile("bmt", [128, 24])
            nwt = P.tile("nwt", [128, 8])
            S.dma(bmt[:], b_modT[l], writes=[bmt])
            S.dma(nwt[:], norm_wT[l], writes=[nwt])
            psm = psa()
            for cc in range(6):
                wm = wmb[it % 3]
                it += 1
                S.dma(wm[:], w_mod[l][:, cc * 512:(cc + 1) * 512].rearrange("(k p) c -> p k c", p=128), writes=[wm])
                for sub in range(4):
                    c24 = cc * 4 + sub
                    for k in range(8):
                        S.I("pe", "matmul", [wm, sT], [psm], psm[:, c24 * 2:c24 * 2 + 2],
                            wm[:, k, sub * 128:(sub + 1) * 128], sT[:, k, :], start=(k == 0), stop=(k == 7))
            S.I("dve", "tensor_tensor", [psm, bmt], [modT], modT[:],
                psm[:, 0:48].rearrange("p (c t) -> p c t", t=2), bmt[:].unsqueeze(2).to_broadcast([128, 24, 2]), ALU.add)
            S.I("dve", "scalar_tensor_tensor", [modT, nwt], [weff], weff[:], modT[:, 8:16, :], 1.0,
                nwt[:].unsqueeze(2).to_broadcast([128, 8, 2]), ALU.add, ALU.mult)
        P.close()

    def phase_A(l):
        P = Phase()
        modT, weff = modTs[l], weffs[l]
        SUB = int(os.environ.get("KSUB", "9"))
        if SUB < 2:
            P.close()
            return
        ss = P.tile("ss", [128, 18])
        rstd = P.tile("rstd", [128, 18])
        junk = P.tile("junk", [128, D], BF16)
        xnb = [P.tile("xn%d" % i, [128, D], BF16) for i in range(2)]
        stg = [P.tile("stg%d" % i, [128, 8, 512], BF16) for i in range(2)]
        for tt in range(18):
            isctx = tt >= 16
            xt = xcres[:, tt - 16, :] if isctx else xres[:, tt, :]
            xb_ = xcres if isctx else xres
            tsel = 1 if isctx else 0
            S.I("act", "activation", [xb_], [junk, ss], junk[:], xt, AF.Square, accum_out=ss[:, tt:tt + 1])
            S.I("act", "activation", [ss], [rstd], rstd[:, tt:tt + 1], ss[:, tt:tt + 1], AF.Ln, bias=EPS, scale=1.0 / D)
            S.I("act", "activation", [rstd], [rstd], rstd[:, tt:tt + 1], rstd[:, tt:tt + 1], AF.Exp, scale=-0.5)
            xn = xnb[tt % 2]
            S.I("dve", "tensor_scalar", [xb_, rstd], [xn], xn[:], xt, rstd[:, tt:tt + 1], None, ALU.mult)
            pt = psb()
            for k in range(8):
                S.I("pe", "transpose", [xn, ident], [pt], pt[:, k * 128:(k + 1) * 128], xn[:, k * 128:(k + 1) * 128], ident[:])
            if isctx:
                dst, dbuf, c0 = hTc, hTc, (tt - 16) * 128
            else:
                dbuf = stg[(tt // 4) % 2]
                dst, c0 = dbuf, (tt % 4) * 128
            for k in range(8):
                eng = "act" if k % 2 == 0 else "dve"
                if eng == "act":
                    S.I("act", "activation", [pt, weff, modT], [dbuf], dst[:, k, c0:c0 + 128], pt[:, k * 128:(k + 1) * 128],
                        AF.Identity, bias=modT[:, k, tsel:tsel + 1], scale=weff[:, k, tsel:tsel + 1])
                else:
                    S.I("dve", "tensor_scalar", [pt, weff, modT], [dbuf], dst[:, k, c0:c0 + 128], pt[:, k * 128:(k + 1) * 128],
                        weff[:, k, tsel:tsel + 1], modT[:, k, tsel:tsel + 1], ALU.mult, ALU.add)
            if (not isctx) and tt % 4 == 3:
                blk = tt // 4
                S.dma(hT_loc[blk].rearrange("(k p) n -> p k n", p=128), dbuf[:],
                      reads=[dbuf], writes=[b_hT_loc[blk]])
                allgather(hT_loc[blk], hT_all[blk], b_hT_loc[blk], b_hT_all[blk])
        P.close()

    def load_hb(hb, tb):
        r, blk = tb // 4, tb % 4
        S.dma(hb[:], hT_all[blk][r * D:(r + 1) * D, :].rearrange("(k p) n -> p k n", p=128),
              reads=[b_hT_all[blk]], writes=[hb])

    def proj_fm(ps, w, c0, m, hsrc, n0, n, m0=0):
        for k in range(8):
            S.I("pe", "matmul", [w, hsrc], [ps], ps[m0:m0 + m, 0:n], w[:, k, c0:c0 + m], hsrc[:, k, n0:n0 + n],
                start=(k == 0), stop=(k == 7))

    def phase_N(l, with_ctx):
        P = Phase()
        wna = P.tile("wna", [128, 8, 384], BF16)
        S.dma(wna[:], w_na[l].rearrange("(k p) c -> p k c", p=128), writes=[wna], eng="pool")
        nag = P.tile("nag", [128, 2])
        S.dma(nag[:], na_g[l], writes=[nag])
        QT = P.tile("QT", [128, SEQ], BF16)
        KT2 = [P.tile("KT%d" % i, [128, SEQ], BF16) for i in range(2)]
        KT = KT2[0]
        VA = P.tile("VA", [128, 64, 192], BF16)
        QTc = P.tile("QTc", [128, CTX], BF16)
        KTc2 = [P.tile("KTc%d" % i, [128, CTX], BF16) for i in range(2)]
        KTc = KTc2[0]
        S.I("pool", "memset", [], [KT2[0]], KT2[0][64:128, :], 0.0)
        S.I("pool", "memset", [], [KT2[1]], KT2[1][0:64, :], 0.0)
        S.I("pool", "memset", [], [KTc2[0]], KTc2[0][64:128, :], 0.0)
        S.I("pool", "memset", [], [KTc2[1]], KTc2[1][0:64, :], 0.0)
        VAc = P.tile("VAc", [128, 2, 192], BF16)
        S.I("pool", "memset", [], [VA], VA[:, :, 64:128], 1.0)
        S.I("pool", "memset", [], [VAc], VAc[:, :, 64:128], 1.0)
        hbs = [P.tile("hb%d" % i, [128, 8, 512], BF16) for i in range(2)]
        sq = P.tile("sq", [128, 512], BF16)
        rs = P.tile("rs", [128, 512])

        def qk_norm(ps, n, dst_buf, dst_ap, gcol):
            S.I("act", "activation", [ps], [sq], sq[:, 0:n], ps[:, 0:n], AF.Square)
            p2 = psa()
            S.I("pe", "matmul", [bones, sq], [p2], p2[:, 0:n], bones[:], sq[:, 0:n], start=True, stop=True)
            S.I("act", "activation", [p2], [rs], rs[:, 0:n], p2[:, 0:n], AF.Ln, bias=EPS, scale=1.0 / 64)
            S.I("act", "activation", [rs], [rs], rs[:, 0:n], rs[:, 0:n], AF.Exp, scale=-0.5)
            if isinstance(dst_buf, list):
                for hh in range(2):
                    r0_ = hh * 64
                    S.I("dve", "scalar_tensor_tensor", [ps, nag, rs], [dst_buf[hh]], dst_ap[hh][r0_:r0_ + 64, :], ps[r0_:r0_ + 64, 0:n],
                        nag[r0_:r0_ + 64, gcol:gcol + 1], rs[r0_:r0_ + 64, 0:n], ALU.mult, ALU.mult)
            else:
                S.I("dve", "scalar_tensor_tensor", [ps, nag, rs], [dst_buf], dst_ap, ps[:, 0:n], nag[:, gcol:gcol + 1],
                    rs[:, 0:n], ALU.mult, ALU.mult)

        def project(hsrc, n, qdst, kdst, vdst, vt0, qb, kb, vb):
            pq = psa()
            proj_fm(pq, wna, 0, 128, hsrc, 0, n)
            qk_norm(pq, n, qb, qdst, 0)
            pk = psa()
            proj_fm(pk, wna, 128, 128, hsrc, 0, n)
            qk_norm(pk, n, kb, kdst, 1)
            pv = psa()
            nt = n // 128
            for s in range(nt):
                for k in range(8):
                    S.I("pe", "matmul", [hsrc, wna], [pv], pv[:, s * 128:(s + 1) * 128], hsrc[:, k, s * 128:(s + 1) * 128],
                        wna[:, k, 256:384], start=(k == 0), stop=(k == 7))
            pv3 = pv[:, 0:n].rearrange("p (s c) -> p s c", c=128)
            S.I("act", "activation", [pv], [vb], vdst[:, vt0:vt0 + nt, 0:64], pv3[:, :, 0:64], AF.Copy)
            S.I("dve", "tensor_copy", [pv], [vb], vdst[:, vt0:vt0 + nt, 128:192], pv3[:, :, 64:128])

        project(hTc, CTX, QTc[:], [KTc2[0][:], KTc2[1][:]], VAc, 0, QTc, KTc2, VAc)
        for it, tb in enumerate([r_ * 4 + blk_ for blk_ in range(4) for r_ in range(4)]):
            hb = hbs[it % 2]
            load_hb(hb, tb)
            project(hb, 512, QT[:, tb * 512:(tb + 1) * 512], [KT2[0][:, tb * 512:(tb + 1) * 512], KT2[1][:, tb * 512:(tb + 1) * 512]],
                    VA, tb * 4, QT, KT2, VA)

        if stage == 2:
            dbg("QT", QT, QT[:], [128, SEQ], BF16)
            dbg("KT", KT, KT[:], [128, SEQ], BF16)

        msk = P.tile("msk", [128, 22 * 64], BF16)
        edg = P.tile("edg", [128, 12, 512], BF16)
        bst = [P.tile("bst0", [128, 704])] * 2
        pts = [P.tile("pt%d" % i, [128, 512], BF16) for i in range(4)]
        ona = [P.tile("ona%d" % i, [128, 512], BF16) for i in range(2)]
        rden = P.tile("rden", [128, 512])
        lnd = P.tile("lnd", [128, 512])
        ptc = [0]

        def va_lhsT(vbuf, tile_i, h):
            return vbuf[:, tile_i, 0:128] if h == 0 else vbuf[:, tile_i, 64:192]

        sbanks = [PSA[2], PSA[3], Buf("ps2a", PS2.t[:, 0:512]), Buf("ps2b", PS2.t[:, 512:1024]),
                  Buf("psbf0", PSB[0].t[:].bitcast(F32)), Buf("psbf1", PSB[1].t[:].bitcast(F32))]
        pobanks = [PSA[0], PSA[1]]
        pts = pts + [P.tile("ptx%d" % i, [128, 512], BF16) for i in range(1)]
        LAG = 3
        items = []
        for h in range(2):
            hp = h * 64
            for m in range(16):
                qcols = slice(m * 512, (m + 1) * 512)
                keys = []
                for kt in range(8):
                    kr0 = 8 * m - 4 + 2 * kt
                    if kr0 < 0 or kr0 >= 128:
                        continue
                    if m == 0:
                        mk, mb = edg[:, kt - 2, :], edg
                    elif m == 15:
                        mk, mb = edg[:, 6 + kt, :], edg
                    else:
                        e0 = 14 - 2 * kt
                        mk, mb = msk[:, e0 * 64:(e0 + 8) * 64], msk
                    keys.append((KT2[h], KT2[h][:, kr0 * 64:kr0 * 64 + 128], VA, kr0 // 2, mk, mb))
                for ct in range(2):
                    keys.append((KTc2[h], KTc2[h][:, ct * 128:(ct + 1) * 128], VAc, ct, None, None))
                for i, kk in enumerate(keys):
                    items.append((h, m, i, len(keys)) + kk + (QT, QT[:, qcols], 512, m * 512))
            if with_ctx:
                for ct in range(2):
                    items.append((h, 16, ct, 2, KTc2[h], KTc2[h][:, ct * 128:(ct + 1) * 128], VAc, ct, None, None,
                                  QTc, QTc[:, :], CTX, SEQ))
        cur_h = [-1]
        pend = {}
        for idx in range(len(items) + LAG):
            if idx < len(items):
                (h, m, i, nk, kbuf, kap, vbuf, vt, mk, mb, qbuf, qap, n, t0) = items[idx]
                if h != cur_h[0]:
                    cur_h[0] = h
                    for i2 in range(2):
                        b_ = bst[i2]
                        S.dma(b_[:], na_strip[l, h][:, i2 * 704:(i2 + 1) * 704], writes=[b_])
                        S.I("act", "activation", [b_], [msk], msk[:, i2 * 704:(i2 + 1) * 704], b_[:], AF.Exp)
                    for i2 in range(12):
                        b_ = bst[i2 % 2]
                        S.dma(b_[:, 0:512], na_edge[l, h, i2], writes=[b_])
                        S.I("act", "activation", [b_], [edg], edg[:, i2, :], b_[:, 0:512], AF.Exp)
                ps_ = sbanks[idx % 6]
                pt_ = pts[idx % 5]
                S.I("pe", "matmul", [kbuf, qbuf], [ps_], ps_[:, 0:n], kap, qap, start=True, stop=True)
                S.I("act", "activation", [ps_], [pt_], pt_[:, 0:n], ps_[:, 0:n], AF.Exp, scale=0.125)
                if mk is not None:
                    S.I("dve" if idx % 2 == 0 else "pool", "tensor_tensor", [pt_, mb], [pt_], pt_[:], pt_[:], mk, ALU.mult)
                pend[idx] = pt_
            j2 = idx - LAG
            if j2 >= 0:
                (h, m, i, nk, kbuf, kap, vbuf, vt, mk, mb, qbuf, qap, n, t0) = items[j2]
                hp = h * 64
                pt_ = pend.pop(j2)
                po = pobanks[(h * 17 + m) % 2]
                S.I("pe", "matmul", [vbuf, pt_], [po], po[:, 0:n], va_lhsT(vbuf, vt, h), pt_[:, 0:n],
                    start=(i == 0), stop=(i == nk - 1))
                if i == nk - 1:
                    ob = ona[(h * 17 + m) % 2]
                    dp = 64 - hp
                    S.I("act", "activation", [po], [lnd], lnd[dp:dp + 64, 0:n], po[dp:dp + 64, 0:n], AF.Ln)
                    S.I("act", "activation", [lnd], [lnd], lnd[dp:dp + 64, 0:n], lnd[dp:dp + 64, 0:n], AF.Exp, scale=-1.0)
                    S.I("dve", "tensor_copy", [lnd], [rden], rden[hp:hp + 64, 0:n], lnd[dp:dp + 64, 0:n])
                    S.I("dve", "tensor_tensor", [po, rden], [ob], ob[hp:hp + 64, 0:n], po[hp:hp + 64, 0:n],
                        rden[hp:hp + 64, 0:n], ALU.mult)
                    oap, obf = o_loc_ap(hp, hp + 64, t0, n)
                    S.dma(oap, ob[hp:hp + 64, 0:n], reads=[ob], writes=[obf])
        P.close()


    def phase_GR(l, with_ctx):
        P = Phase()
        wgr = P.tile("wgr", [128, 8, 800], BF16)
        S.dma(wgr[:], w_gr[l].rearrange("(k p) c -> p k c", p=128), writes=[wgr], eng="pool")
        wgt = P.tile("wgt", [16, 2, 64], BF16)
        S.dma(wgt[:], gla_wg[l].rearrange("a k c -> k a c"), writes=[wgt], eng="pool")
        nb = P.tile("nb", [64, 2])
        S.dma(nb[:], gla_bg[l], writes=[nb])
        S.I("act", "mul", [nb], [nb], nb[:], nb[:], -1.0)
        ong = P.tile("ong", [128, 2])
        S.dma(ong[:], on_g[l], writes=[ong])
        onesb = P.tile("onesb", [128, 128], BF16)
        S.I("dve", "memset", [], [onesb], onesb[:], 1.0)
        OF = P.tile("OF", [128, 2, SEQ + CTX], BF16)
        St = P.tile("St", [128, 128])
        Sbf = P.tile("Sbf", [128, 128], BF16)
        tmpS = P.tile("tmpS", [128, 128])
        ET = [[P.tile("E%d%d" % (k_, p_), [128, 512]) for p_ in range(2)] for k_ in range(2)]
        hbs = [P.tile("hb%d" % i, [128, 8, 512], BF16) for i in range(2)]
        cosb = [P.tile("cos%d" % i, [128, 512]) for i in range(2)]
        sinb = [P.tile("sin%d" % i, [128, 512]) for i in range(2)]
        t1 = P.tile("t1", [128, 512])
        t2 = P.tile("t2", [128, 512])
        t3 = P.tile("t3", [128, 512])
        t4 = P.tile("t4", [128, 512])
        qa = P.tile("qa", [128, 512])
        ka = P.tile("ka", [128, 512])
        QP = [P.tile("QP%d" % i, [128, 512], BF16) for i in range(2)]
        KP = [P.tile("KP%d" % i, [128, 512], BF16) for i in range(2)]
        VB = [P.tile("VB%d" % i, [128, 4, 256], BF16) for i in range(2)]
        gT = P.tile("gT", [16, 512], BF16)
        gTl = [P.tile("gTl%d" % i, [16, 512], BF16) for i in range(2)]
        e1 = P.tile("e1", [64, 512])
        nl = P.tile("nl", [64, 512])
        cum = P.tile("cum", [64, 512])
        rn = P.tile("rn", [64, 512])
        onesc = P.tile("onesc", [64, 128])
        S.I("dve", "memset", [], [onesc], onesc[:], 1.0)
        Am = [P.tile("Am%d" % i, [128, 2, 128], BF16) for i in range(2)]
        ktok = [P.tile("ktok%d" % i, [128, 128], BF16) for i in range(2)]
        o32s = [P.tile("o32_%d" % i, [128, 512]) for i in range(2)]
        sqos = [P.tile("sqo%d" % i, [128, 512], BF16) for i in range(2)]
        rsos = [P.tile("rso%d" % i, [128, 512]) for i in range(2)]
        onb = [P.tile("onb%d" % i, [128, 512], BF16) for i in range(2)]
        psO = [PSA[0], PSA[1]]
        PSBf = Buf("psbf", None)
        gen = [PSA[2], PSA[3]]
        gi = [0]

        def pg():
            gi[0] = (gi[0] + 1) % 2
            return gen[gi[0]]

        cnt = [0]
        pending_ag = []
        deferred = []
        for direction in range(2):
            if direction == 0:
                blocks = [-1] + list(range(16))
            else:
                blocks = [-1] + list(range(15, -1, -1))
            S.I("dve", "memset", [], [St], St[:], 0.0)
            S.I("dve", "memset", [], [Sbf], Sbf[:], 0.0)
            for k_ in range(2):
                for p_ in range(2):
                    S.dma(ET[k_][p_][:], econst[direction * 2 + k_], writes=[ET[k_][p_]])
            def issue_loads(bi2):
                tb2 = blocks[bi2]
                par2 = bi2 % 2
                if direction == 1:
                    n2 = CTX if tb2 < 0 else 512
                    t02 = SEQ if tb2 < 0 else tb2 * 512
                    S.dma(cosb[par2][:, 0:n2], qk_scr[0][:, t02:t02 + n2], reads=[scr_buf("q", tb2)], writes=[cosb[par2]])
                    S.dma(sinb[par2][:, 0:n2], qk_scr[1][:, t02:t02 + n2], reads=[scr_buf("k", tb2)], writes=[sinb[par2]])
                    S.dma(gTl[par2][:, 0:n2], gb_scr[:, t02:t02 + n2], reads=[scr_buf("g", tb2)], writes=[gTl[par2]])
                    return
                if tb2 < 0:
                    return
                load_hb(hbs[par2], tb2)
                S.dma(cosb[par2][:], rope[0][:, tb2 * 512:(tb2 + 1) * 512], writes=[cosb[par2]])
                S.dma(sinb[par2][:], rope[1][:, tb2 * 512:(tb2 + 1) * 512], writes=[sinb[par2]])

            def issue_v(bi2):
                tb2 = blocks[bi2]
                par2 = bi2 % 2
                n2 = CTX if tb2 < 0 else 512
                t02 = SEQ if tb2 < 0 else tb2 * 512
                S.dma(VB[par2][:, 0:n2 // 128, :], v_scr[t02:t02 + n2, :].rearrange("(t p) c -> p t c", p=128),
                      reads=[scr_buf("v", tb2)], writes=[VB[par2]])

            def prologue(bi):
                tb = blocks[bi]
                isctx = tb < 0
                n = CTX if isctx else 512
                nch = n // 128
                par = bi % 2
                hsrc = hTc if isctx else hbs[par]
                EQ, EK = ET[0][par], ET[1][par]
                t0s = SEQ if isctx else tb * 512
                pqp_q = None
                if direction == 0:
                    pqp_q = pg()
                    proj_fm(pqp_q, wgr, 256, 128, hsrc, 0, n)
                    S.I("act", "activation", [pqp_q], [gT], gT[:, 0:n], pqp_q[0:16, 0:n], AF.Copy)
                    gsrc = gT
                    S.I("dve", "tensor_copy", [pqp_q], [gTl[par]], gTl[par][:, 0:n], pqp_q[32:48, 0:n])
                    S.dma(gb_scr[:, t0s:t0s + n], gTl[par][:, 0:n], reads=[gTl[par]], writes=[scr_buf("g", tb)])
                    if not isctx:
                        S.I("dve", "tensor_tensor", [pqp_q, sinb[par]], [t2], t2[:], pqp_q[:, :], sinb[par][:], ALU.mult)
                else:
                    gsrc = gTl[par]
                pL = pg()
                S.I("pe", "matmul", [wgt, gsrc], [pL], pL[0:64, 0:n], wgt[:, direction, :], gsrc[:, 0:n], start=True, stop=True)
                S.I("act", "activation", [pL, nb], [e1], e1[:, 0:n], pL[0:64, 0:n], AF.Exp, bias=nb[:, direction:direction + 1], scale=-1.0)
                S.I("act", "activation", [e1], [nl], nl[:, 0:n], e1[:, 0:n], AF.Ln, bias=1.0)
                for c in range(nch):
                    cs = slice(c * 128, (c + 1) * 128)
                    S.I("dve", "tensor_tensor_scan", [onesc, nl], [cum], cum[:, cs], onesc[:], nl[:, cs], 0.0, ALU.mult, ALU.add)
                if direction == 0:
                    src_, sb_ = cum, cum
                else:
                    S.I("dve", "tensor_tensor", [nl, cum], [rn], rn[:, 0:n], nl[:, 0:n], cum[:, 0:n], ALU.subtract)
                    for c in range(nch):
                        cs = slice(c * 128, (c + 1) * 128)
                        S.I("dve", "tensor_scalar", [rn, cum], [rn], rn[:, cs], rn[:, cs], cum[:, c * 128 + 127:c * 128 + 128], None, ALU.add)
                    src_, sb_ = rn, rn
                S.I("act", "activation", [sb_], [EQ], EQ[0:64, 0:n], src_[:, 0:n], AF.Exp, scale=-1.0 / 16)
                S.I("act", "activation", [sb_], [EK], EK[0:64, 0:n], src_[:, 0:n], AF.Exp, scale=1.0 / 16)
                qp, kp = QP[par], KP[par]
                for qi_, (c0, c0p, dst, E_, ta, tb_, acc) in enumerate(((0, 256, qp, EQ, t1, t2, qa), (128, 384, kp, EK, t3, t4, ka))):
                    if direction == 1:
                        stag = cosb[par] if qi_ == 0 else sinb[par]
                        S.I("dve", "tensor_tensor", [stag, E_], [dst], dst[:, 0:n], stag[:, 0:n], E_[:, 0:n], ALU.mult)
                        continue
                    pq = pg()
                    proj_fm(pq, wgr, c0, 128, hsrc, 0, n)
                    if isctx:
                        S.I("dve", "tensor_copy", [pq], [acc], acc[:, 0:n], pq[:, 0:n])
                        S.dma(qk_scr[qi_][:, t0s:t0s + n], acc[:, 0:n], reads=[acc], writes=[scr_buf("qk"[qi_], tb)])
                        S.I("dve", "tensor_tensor", [acc, E_], [dst], dst[:, 0:n], acc[:, 0:n], E_[:, 0:n], ALU.mult)
                    else:
                        S.I("dve", "tensor_tensor", [pq, cosb[par]], [ta], ta[:], pq[:, :], cosb[par][:], ALU.mult)
                        if qi_ == 1:
                            pqp = pg()
                            proj_fm(pqp, wgr, c0p, 128, hsrc, 0, n)
                            S.I("dve", "tensor_tensor", [pqp, sinb[par]], [tb_], tb_[:], pqp[:, :], sinb[par][:], ALU.mult)
                        S.I("dve", "tensor_tensor", [ta, tb_], [acc], acc[:], ta[:], tb_[:], ALU.add)
                        S.dma(qk_scr[qi_][:, t0s:t0s + n], acc[:, 0:n], reads=[acc], writes=[scr_buf("qk"[qi_], tb)])
                        S.I("dve", "tensor_tensor", [acc, E_], [dst], dst[:], acc[:], E_[:], ALU.mult)
                vb = VB[par]
                for half in range((nch + 1) // 2 if direction == 0 else 0):
                    pv = pg()
                    for s2 in range(2):
                        s_ = half * 2 + s2
                        if s_ >= nch:
                            continue
                        for k in range(8):
                            S.I("pe", "matmul", [hsrc, wgr], [pv], pv[:, s2 * 256:(s2 + 1) * 256], hsrc[:, k, s_ * 128:(s_ + 1) * 128],
                                wgr[:, k, 544:800], start=(k == 0), stop=(k == 7))
                    S.I("act", "activation", [pv], [vb], vb[:, half * 2:half * 2 + 2, :],
                        pv[:, :].rearrange("p (s c) -> p s c", c=256), AF.Copy)
                if direction == 0:
                    S.dma(v_scr[t0s:t0s + n, :].rearrange("(t p) c -> p t c", p=128), vb[:, 0:nch, :],
                          reads=[vb], writes=[scr_buf("v", tb)])

            for bi, tb in enumerate(blocks):
                isctx = tb < 0
                n = CTX if isctx else 512
                nch = n // 128
                par = bi % 2
                if bi == 0:
                    issue_loads(0)
                    issue_loads(1)
                    if direction == 1:
                        issue_v(0)
                    prologue(0)
                if direction == 1 and bi + 1 < len(blocks):
                    issue_v(bi + 1)
                if bi + 2 < len(blocks):
                    issue_loads(bi + 2)
                if bi + 1 < len(blocks):
                    prologue(bi + 1)
                for (p_, when) in list(pending_ag):
                    if when <= bi:
                        allgather(o_loc[p_], o_all[p_], b_o_loc[p_], b_o_all[p_])
                        pending_ag.remove((p_, when))
                EQ, EK = ET[0][par], ET[1][par]
                qp, kp, vb = QP[par], KP[par], VB[par]
                for fn_ in deferred:
                    fn_()
                deferred = []
                chunks = list(range(nch)) if direction == 0 else list(range(nch - 1, -1, -1))
                for ci, c in enumerate(chunks):
                    cs = slice(c * 128, (c + 1) * 128)
                    cnt[0] += 1
                    am, kt_ = Am[cnt[0] % 2], ktok[cnt[0] % 2]
                    S.I("pe", "matmul", [kp, qp], [PS2], PS2[:, 0:128], kp[0:64, cs], qp[0:64, cs], start=True, stop=True)
                    S.I("pe", "matmul", [kp, qp], [PS2], PS2[:, 512:640], kp[64:128, cs], qp[64:128, cs], start=True, stop=True)
                    S.I("dve", "tensor_tensor", [PS2, tri], [am], am[:],
                        PS2[:, :].rearrange("p (a b) -> p a b", b=512)[:, :, 0:128],
                        tri[:, direction:direction + 1, :].to_broadcast([128, 2, 128]), ALU.mult)
                    pt = PSB[0]
                    S.I("pe", "transpose", [kp, ident], [pt], pt[:, 0:128], kp[:, cs], ident[:])
                    S.I("act", "activation", [pt], [kt_], kt_[:], pt[:, 0:128], AF.Copy)
                    for mx in range(2):
                        r0 = mx * 64
                        po_ = psO[mx]
                        S.I("pe", "matmul", [vb, am], [po_], po_[:, cs], vb[:, c, mx * 128:(mx + 1) * 128], am[:, mx, :], start=True, stop=False)
                        S.I("pe", "matmul", [Sbf, qp], [po_], po_[:, cs], Sbf[r0:r0 + 64, :], qp[r0:r0 + 64, cs], start=False, stop=True)
                    pd = pg()
                    S.I("pe", "matmul", [kt_, vb], [pd], pd[0:64, 0:128], kt_[:, 0:64], vb[:, c, 0:128], start=True, stop=True)
                    S.I("pe", "matmul", [kt_, vb], [pd], pd[64:128, 0:128], kt_[:, 64:128], vb[:, c, 128:256], start=True, stop=True)
                    dcol = c * 128 + 127 if direction == 0 else c * 128
                    S.I("dve", "tensor_tensor", [pd, St], [tmpS], tmpS[:], pd[:, 0:128], St[:], ALU.add)
                    S.I("act", "activation", [tmpS, EQ], [Sbf], Sbf[:], tmpS[:], AF.Copy, scale=EQ[:, dcol:dcol + 1])
                    S.I("dve", "tensor_scalar", [tmpS, EQ], [St], St[:], tmpS[:], EQ[:, dcol:dcol + 1], None, ALU.mult)
                t0 = SEQ if isctx else tb * 512
                if isctx and not with_ctx:
                    continue
                for mx in range(2):
                    po_ = psO[mx]
                    o32 = o32s[mx]
                    if direction == 0:
                        S.I("act", "activation", [po_], [OF], OF[:, mx, t0:t0 + n], po_[:, 0:n], AF.Copy, scale=0.125)
                    else:
                        S.I("dve", "scalar_tensor_tensor", [po_, OF], [o32], o32[:, 0:n], po_[:, 0:n], 0.125, OF[:, mx, t0:t0 + n], ALU.mult, ALU.add)

                        def norm_out(mx=mx, n=n, t0=t0):
                            o32, sqo, rso = o32s[mx], sqos[mx], rsos[mx]
                            S.I("act", "activation", [o32], [sqo], sqo[:, 0:n], o32[:, 0:n], AF.Square)
                            p2 = pg()
                            S.I("pe", "matmul", [onesb, sqo], [p2], p2[:, 0:n], onesb[:], sqo[:, 0:n], start=True, stop=True)
                            S.I("act", "activation", [p2], [rso], rso[:, 0:n], p2[:, 0:n], AF.Ln, bias=EPS, scale=1.0 / 128)
                            S.I("act", "activation", [rso], [rso], rso[:, 0:n], rso[:, 0:n], AF.Exp, scale=-0.5)
                            ob = onb[mx]
                            S.I("dve", "scalar_tensor_tensor", [o32, ong, rso], [ob], ob[:, 0:n], o32[:, 0:n], ong[:, mx:mx + 1], rso[:, 0:n], ALU.mult, ALU.mult)
                            oap, obf = o_loc_ap(128 + mx * 128, 256 + mx * 128, t0, n)
                            S.dma(oap, ob[:, 0:n], reads=[ob], writes=[obf])
                        deferred.append(norm_out)
                if direction == 1 and not isctx and tb % 2 == 0:
                    pending_ag.append((tb // 2, bi + 2))
                if direction == 1 and isctx:
                    pending_ag.append((8, bi + 2))
            for fn_ in deferred:
                fn_()
            deferred = []
            for (p_, when) in pending_ag:
                allgather(o_loc[p_], o_all[p_], b_o_loc[p_], b_o_all[p_])
            pending_ag = []
        P.close()


    def phase_D(l, with_ctx, NH=2):
        P = Phase()
        modT = modTs[l]
        W = TOK // NH
        CW = CTX // NH
        NT = W + (CW if with_ctx else 0)
        arena = [P.tile("arena%d" % i, [128, 8, 512], BF16) for i in range(2)]
        acc = P.tile("acc", [128, 512])
        tmq = [P.tile("tmq%d" % i, [128, 512]) for i in range(2)]
        gates = []
        for t in range(2 if with_ctx else 1):
            gb = P.tile("gate%d" % t, [128, D])
            for cb in range(2):
                dg = tmq[cb]
                for k4 in range(4):
                    k = cb * 4 + k4
                    S.I("dve", "tensor_scalar", [identf, modT], [dg], dg[:, k4 * 128:(k4 + 1) * 128], identf[:],
                        modT[:, 16 + k, t:t + 1], None, ALU.mult)
                pg_ = psa()
                S.I("pe", "matmul", [onesf, dg], [pg_], pg_[:, :], onesf[:], dg[:], start=True, stop=True)
                S.I("act", "activation", [pg_], [gb], gb[:, cb * 512:(cb + 1) * 512], pg_[:, :], AF.Copy)
            gates.append(gb)
        hTd = P.tile("hTd", [128, 8, W], BF16)
        OU = P.tile("OU", [128, 12, NT], BF16)
        Y = P.tile("Y", [128, 8, NT], BF16)
        stgs = [P.tile("stg%d" % i, [128, 4, 512], BF16) for i in range(3)]
        wbrs = [P.tile("wbr%d" % i, [128, 3, 4, 128], BF16) for i in range(2)]
        wgs = [P.tile("wg%d" % i, [128, 3, 8, 128], BF16) for i in range(2)]
        szb = [P.tile("sz%d" % i, [128, 512], BF16) for i in range(2)]
        sgb = [P.tile("sg%d" % i, [128, 512]) for i in range(2)]
        cnt = [0]
        nblk = W // 512

        def load_z(z4):
            a = arena[z4 % 2]
            for k in range(8):
                S.dma(a[:, k, :], w_zg[l][k * 128:(k + 1) * 128, z4 * 512:(z4 + 1) * 512], writes=[a], eng="pool", cw=True)

        def load_oc(oc):
            wbr, wg = wbrs[oc % 2], wgs[oc % 2]
            for br in range(3):
                S.dma(wbr[:, br, :, :], w_br[l, br][:, oc * 128:(oc + 1) * 128].rearrange("(s p) c -> p s c", p=128),
                      writes=[wbr], eng="pool", cw=True)
                gc0 = 1536 + br * 1024 + oc * 128
                S.dma(wg[:, br, :, :], w_zg[l][:, gc0:gc0 + 128].rearrange("(k p) c -> p k c", p=128),
                      writes=[wg], eng="pool", cw=True)

        def load_wo():
            for cb in range(2):
                for k in range(8):
                    S.dma(arena[cb][:, k, :], w_o[l][k * 128:(k + 1) * 128, cb * 512:(cb + 1) * 512], writes=[arena[cb]], eng="pool", cw=True)

        for part in range(NH):
            blocks = []
            for b_ in range(nblk):
                blocks.append((hTd, b_ * 512, b_ * 512, 512))
            if with_ctx:
                blocks.append((hTc, part * CW, W, CW))
            load_z(0)
            load_z(1)
            for b_ in range(nblk):
                gblk = part * nblk + b_
                S.dma(hTd[:, :, b_ * 512:(b_ + 1) * 512], hT_loc[gblk].rearrange("(k p) n -> p k n", p=128),
                      reads=[b_hT_loc[gblk]], writes=[hTd], cw=True)
            for cidx in range(12):
                src_, br = cidx // 3, cidx % 3
                r0 = src_ * 384 + br * 128
                for b_ in range(nblk):
                    cnt[0] += 1
                    stg = stgs[cnt[0] % 3]
                    t0 = part * W + b_ * 512
                    for q in range(4):
                        tq = q * TOK + t0
                        pc = tq // 1024
                        S.dma(stg[:, q, :], o_all[pc][r0:r0 + 128, tq % 1024:tq % 1024 + 512], reads=[b_o_all[pc]], writes=[stg], cw=True)
                    dst = OU[:, cidx, b_ * 512:(b_ + 1) * 512]
                    psel = psa()
                    for q in range(4):
                        S.I("pe", "matmul", [dsel, stg], [psel], psel[:, :], dsel[:, q, :], stg[:, q, :], start=(q == 0), stop=(q == 3))
                    S.I("act", "activation", [psel], [OU], dst, psel[:, :], AF.Copy)
                if with_ctx:
                    S.dma(OU[:, cidx, W:W + CW], o_all[8][r0:r0 + 128, part * CW:(part + 1) * CW], reads=[b_o_all[8]], writes=[OU], cw=True)
            for z4 in range(3):
                wz4 = arena[z4 % 2]
                for sub in range(4):
                    zc = z4 * 4 + sub
                    for (hsrc, h0, o0, n) in blocks:
                        ps = psa()
                        proj_fm(ps, wz4, sub * 128, 128, hsrc, h0, n)
                        cnt[0] += 1
                        sz = szb[cnt[0] % 2]
                        S.I("act", "activation", [ps], [sz], sz[:, 0:n], ps[:, 0:n], AF.Silu)
                        S.I("dve", "tensor_tensor", [OU, sz], [OU], OU[:, zc, o0:o0 + n],
                            OU[:, zc, o0:o0 + n], sz[:, 0:n], ALU.mult)
                if z4 == 0:
                    load_z(2)
                    load_oc(0)
            load_oc(1)
            for oc in range(8):
                wbr, wg = wbrs[oc % 2], wgs[oc % 2]
                for (hsrc, h0, o0, n) in blocks:
                    for br in range(3):
                        pB = psa()
                        for s_ in range(4):
                            S.I("pe", "matmul", [wbr, OU], [pB], pB[:, 0:n], wbr[:, br, s_, :], OU[:, s_ * 3 + br, o0:o0 + n],
                                start=(s_ == 0), stop=(s_ == 3))
                        pG = psa()
                        for k in range(8):
                            S.I("pe", "matmul", [wg, hsrc], [pG], pG[:, 0:n], wg[:, br, k, :], hsrc[:, k, h0:h0 + n],
                                start=(k == 0), stop=(k == 7))
                        cnt[0] += 1
                        sg = sgb[cnt[0] % 2]
                        S.I("act", "activation", [pG], [sg], sg[:, 0:n], pG[:, 0:n], AF.Sigmoid)
                        if br == 0:
                            S.I("dve", "tensor_tensor", [pB, sg], [acc], acc[:, 0:n], pB[:, 0:n], sg[:, 0:n], ALU.mult)
                        else:
                            tq_ = tmq[br % 2]
                            S.I("dve", "tensor_tensor", [pB, sg], [tq_], tq_[:, 0:n], pB[:, 0:n], sg[:, 0:n], ALU.mult)
                            if br == 1:
                                S.I("dve", "tensor_tensor", [acc, tq_], [acc], acc[:, 0:n], acc[:, 0:n], tq_[:, 0:n], ALU.add)
                            else:
                                S.I("dve", "tensor_tensor", [acc, tq_], [Y], Y[:, oc, o0:o0 + n], acc[:, 0:n], tq_[:, 0:n], ALU.add)
                if oc == 0:
                    load_wo()
                if oc + 2 < 8:
                    load_oc(oc + 2)
            tiles = []
            for i in range(W // 128):
                tiles.append((xres, xres[:, part * (W // 128) + i, :], 0, 128, i * 128, gates[0]))
            if with_ctx:
                tok0 = part * CW
                tiles.append((xcres, xcres[tok0 % 128:tok0 % 128 + CW, tok0 // 128, :], tok0 % 128, CW, W, gates[1]))
            for (xb_, xap, p0, m, o0, gb) in tiles:
                for cb in range(2):
                    ps = psa()
                    for k in range(8):
                        S.I("pe", "matmul", [Y, arena[cb]], [ps], ps[p0:p0 + m, :], Y[:, k, o0:o0 + m], arena[cb][:, k, :],
                            start=(k == 0), stop=(k == 7))
                    cnt[0] += 1
                    tq_ = tmq[cnt[0] % 2]
                    S.I("dve", "tensor_tensor", [ps, gb], [tq_], tq_[p0:p0 + m, :], ps[p0:p0 + m, :], gb[p0:p0 + m, cb * 512:(cb + 1) * 512], ALU.mult)
                    xs = xap[:, cb * 512:(cb + 1) * 512]
                    S.I("dve", "tensor_tensor", [xb_, tq_], [xb_], xs, xs, tq_[p0:p0 + m, :], ALU.add)
        P.close()

    if stage >= 1:
        phase_mod([0])
    for l in range(depth):
        if stage == 0:
            break
        phase_A(l)
        if l == 0 and depth > 1:
            phase_mod([1])
        if stage == 1:
            break
        phase_N(l, l < DEPTH - 1)
        if stage == 2:
            break
        phase_GR(l, l < DEPTH - 1)
        if stage == 3:
            break
        phase_D(l, l < DEPTH - 1)

    if stage == 1:
        t = nc.dram_tensor("dbg_hT", [4 * D, TOK], BF16, kind="ExternalOutput").ap()
        dbg_out["hT"] = t
        tmp = T("dbgtmp", [128, 32, 512], BF16)
        for q in range(4):
            S.dma(tmp[:], hT_all[q].rearrange("(a p) n -> p a n", p=128), reads=[b_hT_all[q]], writes=[tmp])
            outs.append(S.dma(t[:, q * 512:(q + 1) * 512].rearrange("(a p) n -> p a n", p=128), tmp[:], reads=[tmp]))
        dbg("hTc", hTc, hTc[:], [128, 8, CTX], BF16)
        dbg("modT", modTs[0], modTs[0][:], [128, 24, 2])
    if stage == 3:
        for i_ in range(3):
            tmp = T("dbgtmp%d" % i_, [128, SEQ + CTX], BF16)
            for p in range(9):
                S.dma(tmp[:, p * 1024:p * 1024 + OW[p]], o_loc[p][i_ * 128:(i_ + 1) * 128, :], reads=[b_o_loc[p]], writes=[tmp])
            dbg("o%d" % i_, tmp, tmp[:], [128, SEQ + CTX], BF16)
    if stage == 2:
        tmp = T("dbgtmp", [128, SEQ + CTX], BF16)
        for p in range(9):
            S.dma(tmp[:, p * 1024:p * 1024 + OW[p]], o_loc[p][0:128, :], reads=[b_o_loc[p]], writes=[tmp])
        dbg("ona", tmp, tmp[:], [128, SEQ + CTX], BF16)
    if stage == 4:
        dbg("xc", xcres, xcres[:], [128, 2, D])
    for t in range(16):
        outs.append(S.dma(y_out[t * 128:(t + 1) * 128, :], xres[:, t, :], reads=[xres]))
    S.emit(final_waits=outs)
    return nc, declared


def _rope_tables():
    pos = np.arange(SEQ)
    row = (pos // 64).astype(np.float64)
    col = (pos % 64).astype(np.float64)
    inv = 10000.0 ** (-np.arange(16, dtype=np.float64) / 16)
    cos = np.ones((128, SEQ), np.float64)
    sin = np.zeros((128, SEQ), np.float64)
    for d in range(64):
        half = d // 32
        dd = d % 32
        a = dd % 16
        p = row if half == 0 else col
        ang = p * inv[a]
        cos[64 + d] = np.cos(ang)
        sin[64 + d] = (-1.0 if dd < 16 else 1.0) * np.sin(ang)
    return np.stack([cos, sin]).astype(np.float32)


def _rope_perm():
    perm = np.zeros(64, np.int64)
    for d in range(64):
        base = (d // 32) * 32
        dd = d % 32
        perm[d] = base + (dd + 16 if dd < 16 else dd - 16)
    return perm


def _econst(j):
    lgf = np.log1p(-2.0 ** (-(5.0 + j)))
    lgb = np.log1p(-2.0 ** (-(5.5 + j)))
    i = np.arange(128, dtype=np.float64)
    rows = [np.exp((i + 1) * lgf), np.exp(-(i + 1) * lgf), np.exp((128 - i) * lgb), np.exp(-(128 - i) * lgb)]
    e = np.zeros((4, 128, 512), np.float32)
    for k in range(4):
        e[k, 64:128, :] = np.tile(rows[k], 4)[None, :].astype(np.float32)
    return e


def _na_bias(rpb_h):
    kc = np.arange(64)[:, None]
    qc = np.arange(64)[None, :]
    cs = np.clip(qc - 8, 0, 48)
    colok = (kc >= cs) & (kc <= cs + 15)
    cidx = np.clip(kc - qc + 15, 0, 30)
    strip = np.full((128, 22 * 64), NEG, np.float32)
    for a in range(2):
        for e in range(22):
            dr = a - e + 10
            if -4 <= dr <= 3:
                blk = np.where(colok, rpb_h[dr + 7][cidx], NEG)
                strip[a * 64:(a + 1) * 64, e * 64:(e + 1) * 64] = blk
    edge = np.full((12, 128, 512), NEG, np.float32)
    idx = 0
    for m, kts in ((0, range(2, 8)), (15, range(0, 6))):
        for kt in kts:
            for a in range(2):
                kr = 8 * m - 4 + 2 * kt + a
                for qi in range(8):
                    qr = 8 * m + qi
                    rs = min(max(qr - 4, 0), 120)
                    if rs <= kr <= rs + 7:
                        blk = np.where(colok, rpb_h[kr - qr + 7][cidx], NEG)
                        edge[idx, a * 64:(a + 1) * 64, qi * 64:(qi + 1) * 64] = blk
            idx += 1
    return strip, edge


def make_in_maps(inp):
    f32 = np.float32
    x, c, ctx, c_ctx = inp["x"], inp["c"], inp["ctx"], inp["c_ctx"]
    w_in = inp["w_in"]
    L = DEPTH
    rope = _rope_tables()
    perm = _rope_perm()
    ident = np.eye(128, dtype=f32)
    tri = np.stack([np.triu(np.ones((128, 128), f32)), np.tril(np.ones((128, 128), f32))])
    bones = np.zeros((128, 128), f32)
    bones[:64, :64] = 1
    bones[64:, 64:] = 1
    b_modT = np.ascontiguousarray(inp["b_mod"].reshape(L, 24, 128).transpose(0, 2, 1))
    norm_wT = np.ascontiguousarray(inp["norm_w"].reshape(L, 8, 128).transpose(0, 2, 1))
    w_mod = np.ascontiguousarray(inp["w_mod"])
    w_br = np.ascontiguousarray(inp["w_branch"])
    w_o = np.ascontiguousarray(inp["w_out"])
    zcols = np.concatenate([np.arange(3616 + br * 512 + s * 128, 3616 + br * 512 + s * 128 + 128)
                            for s in range(4) for br in range(3)])
    w_zg = np.ascontiguousarray(np.concatenate([w_in[:, :, zcols], w_in[:, :, 5152:8224]], axis=2))
    maps = []
    for core in range(8):
        b, j = core // 4, core % 4
        cv = np.stack([c[b], c_ctx])
        cTm = np.ascontiguousarray(cv.T.reshape(8, 128, 2).transpose(1, 0, 2))
        sl = lambda o, n: slice(o, o + n)
        w_na = np.concatenate([w_in[:, :, sl(128 * j, 128)], w_in[:, :, sl(512 + 128 * j, 128)],
                               w_in[:, :, sl(1024 + 128 * j, 128)]], axis=2)
        glaq = w_in[:, :, sl(1536 + 64 * j, 64)]
        glak = w_in[:, :, sl(1792 + 64 * j, 64)]
        retq = w_in[:, :, sl(2592 + 64 * j, 64)]
        retk = w_in[:, :, sl(2848 + 64 * j, 64)]
        z16 = np.zeros_like(w_in[:, :, 0:16])
        gpad = np.concatenate([w_in[:, :, 2560:2576], z16, w_in[:, :, 2576:2592], z16], axis=2)
        w_gr = np.concatenate([glaq, retq, glak, retk, gpad, retq[:, :, perm], glak, retk[:, :, perm],
                               w_in[:, :, 2560:2592], w_in[:, :, sl(2048 + 128 * j, 128)],
                               w_in[:, :, sl(3104 + 128 * j, 128)]], axis=2)
        na_g = np.stack([np.tile(inp["na_q_norm"], (1, 2)), np.tile(inp["na_k_norm"], (1, 2))], axis=2)
        strips = np.zeros((L, 2, 128, 22 * 64), f32)
        edges = np.zeros((L, 2, 12, 128, 512), f32)
        for l in range(L):
            for h in range(2):
                strips[l, h], edges[l, h] = _na_bias(inp["na_rpb"][l, 2 * j + h])
        gla_wg = np.ascontiguousarray(inp["gla_w_gate"][:, :, :, 64 * j:64 * j + 64])
        gla_bg = np.ascontiguousarray(inp["gla_b_gate"][:, :, 64 * j:64 * j + 64].transpose(0, 2, 1))
        on_g = np.stack([inp["gla_out_norm"][:, 128 * j:128 * j + 128], inp["ret_out_norm"][:, 128 * j:128 * j + 128]], axis=2)
        selv = np.zeros((128, 4), f32)
        selv[:, j] = 1.0
        maps.append({
            "x_sh": np.ascontiguousarray(x[b, TOK * j:TOK * (j + 1)]),
            "ctx_b": np.ascontiguousarray(ctx[b]),
            "cT": cTm, "w_mod": w_mod, "b_modT": b_modT, "norm_wT": norm_wT,
            "w_na": np.ascontiguousarray(w_na), "w_gr": np.ascontiguousarray(w_gr), "w_zg": w_zg,
            "w_br": w_br, "w_o": w_o, "na_g": np.ascontiguousarray(na_g.astype(f32)),
            "na_strip": strips, "na_edge": edges, "gla_wg": gla_wg, "gla_bg": gla_bg,
            "on_g": np.ascontiguousarray(on_g.astype(f32)), "econst": _econst(j), "rope": rope, "sel": selv,
            "c_ident": ident.astype(ml_dtypes.bfloat16), "c_identf": ident,
            "c_tri": tri.astype(ml_dtypes.bfloat16), "c_bones": bones.astype(ml_dtypes.bfloat16),
        })
    return maps


_CACHE = {}


def kernel(**inputs):
    inp = {k: np.asarray(v) for k, v in inputs.items()}
    maps = make_in_maps(inp)
    if "nc" not in _CACHE:
        _CACHE["nc"] = build_program()[0]
    res = run_bass_kernel_spmd(_CACHE["nc"], maps, core_ids=list(range(8)))
    out = np.zeros((2, SEQ, D), np.float32)
    for core in range(8):
        b, j = core // 4, core % 4
        out[b, TOK * j:TOK * (j + 1)] = res.results[core]["y"]
    return out
```
